# Optimizing a Trainium2 kernel written in Bass

```python
import math
import jax, jax.numpy as jnp
from jax import lax
import numpy as np

D_MODEL = 1024
BATCH = 8
SEQ = 4096
DEPTH = 2

MIX_WIDTH = 2 * D_MODEL
GROUP_WIDTH = MIX_WIDTH // 4
MEM_LEN = 256
EPS = 1e-6

MOBA_HEADS = 8
MOBA_HEAD_DIM = GROUP_WIDTH // MOBA_HEADS
MOBA_BLOCK = 256
MOBA_TOPK = 3
MOBA_Q_CHUNK = 16

NSA_HEADS = 8
NSA_KV_HEADS = 2
NSA_HEAD_DIM = GROUP_WIDTH // NSA_HEADS
NSA_KV_WIDTH = NSA_KV_HEADS * NSA_HEAD_DIM
NSA_CMP_LEN = 32
NSA_CMP_STRIDE = 16
NSA_CMP_HIDDEN = 128
NSA_SLC_BLOCK = 64
NSA_SLC_TOPK = 16
NSA_WINDOW = 512
NSA_Q_CHUNK = 64

RET_HEADS = 4
RET_KEY_DIM = 64
RET_VAL_DIM = GROUP_WIDTH // RET_HEADS
RET_QK_WIDTH = RET_HEADS * RET_KEY_DIM
RET_CHUNK = 128

MEM_HEADS = 4
MEM_HEAD_DIM = GROUP_WIDTH // MEM_HEADS

IN_SPLITS = (
    GROUP_WIDTH, GROUP_WIDTH, GROUP_WIDTH,
    GROUP_WIDTH,
    NSA_KV_WIDTH, NSA_KV_WIDTH,
    NSA_KV_WIDTH, NSA_KV_WIDTH,
    NSA_KV_WIDTH, NSA_KV_WIDTH,
    3 * NSA_HEADS,
    RET_QK_WIDTH, RET_QK_WIDTH, GROUP_WIDTH,
    GROUP_WIDTH,
    MIX_WIDTH,
)
IN_COLS = sum(IN_SPLITS)

kernel_name = "hybrid_moba_nsa_retention_block"


def rms_norm(x, g):
    xf = x.astype(jnp.float32)
    y = xf * lax.rsqrt(jnp.mean(xf * xf, axis=-1, keepdims=True) + EPS)
    return (y * g.astype(jnp.float32)).astype(x.dtype)


def split_cols(t, sizes):
    offs = np.cumsum(np.array(sizes))[:-1].tolist()
    return jnp.split(t, offs, axis=-1)


def masked_softmax(s, mask):
    s = jnp.where(mask, s.astype(jnp.float32), -jnp.inf)
    m = jnp.max(s, axis=-1, keepdims=True)
    m = jnp.where(jnp.isfinite(m), m, 0.0)
    p = jnp.exp(s - m)
    den = jnp.sum(p, axis=-1, keepdims=True)
    return p / jnp.where(den > 0, den, 1.0)


def moba_attention(q, k, v):
    B, S, H, Dh = q.shape
    nb = -(-S // MOBA_BLOCK)
    pad = nb * MOBA_BLOCK - S
    top = min(MOBA_TOPK, nb)
    qc_len = MOBA_Q_CHUNK
    nc = S // qc_len
    scale = Dh ** -0.5
    kp = jnp.pad(k, ((0, 0), (0, pad), (0, 0), (0, 0)))
    vp = jnp.pad(v, ((0, 0), (0, pad), (0, 0), (0, 0)))
    kb = kp.reshape(B, nb, MOBA_BLOCK, H, Dh).transpose(0, 3, 1, 2, 4)
    vb = vp.reshape(B, nb, MOBA_BLOCK, H, Dh).transpose(0, 3, 1, 2, 4)
    k_mean = jnp.mean(kb.astype(jnp.float32), axis=3)
    own = jnp.arange(S) // MOBA_BLOCK
    past = jnp.arange(nb)[None, :] < own[:, None]
    gate = jnp.einsum('bshd,bhnd->bhsn', q.astype(jnp.float32), k_mean)
    _, sel = lax.top_k(jnp.where(past, gate, -jnp.inf), top)
    valid = sel < own[None, None, :, None]

    q_ch = q.reshape(B, nc, qc_len, H, Dh).transpose(1, 0, 2, 3, 4)
    sel_ch = sel.reshape(B, H, nc, qc_len, top).transpose(2, 0, 1, 3, 4)
    val_ch = valid.reshape(B, H, nc, qc_len, top).transpose(2, 0, 1, 3, 4)
    starts = jnp.arange(nc, dtype=jnp.int32) * qc_len
    bi = jnp.arange(B)[:, None, None, None]
    hi = jnp.arange(H)[None, :, None, None]

    def chunk(args):
        qc, sc, vc_ok, t0 = args
        tq = t0 + jnp.arange(qc_len)
        ks = kb[bi, hi, sc]
        vs = vb[bi, hi, sc]
        s_sel = jnp.einsum('bqhd,bhqjkd->bhqjk', qc, ks) * scale
        m_sel = jnp.broadcast_to(vc_ok[..., None], s_sel.shape)
        n0 = (t0 // MOBA_BLOCK) * MOBA_BLOCK
        ko = lax.dynamic_slice_in_dim(kp, n0, MOBA_BLOCK, axis=1)
        vo = lax.dynamic_slice_in_dim(vp, n0, MOBA_BLOCK, axis=1)
        s_own = jnp.einsum('bqhd,bkhd->bhqk', qc, ko) * scale
        m_own = (n0 + jnp.arange(MOBA_BLOCK))[None, :] <= tq[:, None]
        n_sel = top * MOBA_BLOCK
        s = jnp.concatenate([s_sel.reshape(B, H, qc_len, n_sel), s_own], axis=-1)
        m = jnp.concatenate([m_sel.reshape(B, H, qc_len, n_sel),
                             jnp.broadcast_to(m_own, (B, H, qc_len, MOBA_BLOCK))], axis=-1)
        p = masked_softmax(s, m).astype(v.dtype)
        p_sel = p[..., :n_sel].reshape(B, H, qc_len, top, MOBA_BLOCK)
        p_own = p[..., n_sel:]
        return (jnp.einsum('bhqjk,bhqjkd->bqhd', p_sel, vs)
                + jnp.einsum('bhqk,bkhd->bqhd', p_own, vo))

    o = lax.map(chunk, (q_ch, sel_ch, val_ch, starts))
    return o.transpose(1, 0, 2, 3, 4).reshape(B, S, H * Dh).astype(q.dtype)


def _cmp_to_slc_matrix(n_cmp, n_slc):
    rs = NSA_SLC_BLOCK // NSA_CMP_STRIDE
    rc = NSA_CMP_LEN // NSA_CMP_STRIDE
    j = np.arange(n_slc)[:, None, None]
    i = np.broadcast_to(rs * j + np.arange(rs)[None, :, None] - np.arange(rc)[None, None, :], (n_slc, rs, rc))
    jj = np.broadcast_to(j, i.shape)
    ok = (i >= 0) & (i < n_cmp)
    mat = np.zeros((n_cmp, n_slc), np.float32)
    np.add.at(mat, (i[ok], jj[ok]), 1.0)
    return jnp.asarray(mat)


def nsa_attention(q, k_cmp, v_cmp, k_slc, v_slc, k_win, v_win, gate_logits,
                  pe_k, w1_k, w2_k, pe_v, w1_v, w2_v):
    B, S, H, Dh = q.shape
    G = k_cmp.shape[2]
    P = H // G
    scale = Dh ** -0.5
    n_cmp = (S - NSA_CMP_LEN) // NSA_CMP_STRIDE + 1
    n_slc = S // NSA_SLC_BLOCK
    top = min(NSA_SLC_TOPK, n_slc)
    qc_len = NSA_Q_CHUNK
    nc = S // qc_len
    W = NSA_WINDOW

    cmp_idx = NSA_CMP_STRIDE * np.arange(n_cmp)[:, None] + np.arange(NSA_CMP_LEN)[None, :]

    def compress(t, pe, w1, w2):
        blk = t[:, cmp_idx] + pe[None, None, :, None, :]
        flat = blk.transpose(0, 1, 3, 2, 4).reshape(B, n_cmp, G, NSA_CMP_LEN * Dh)
        return jax.nn.silu(flat @ w1) @ w2

    kc = compress(k_cmp, pe_k, w1_k, w2_k)
    vc = compress(v_cmp, pe_v, w1_v, w2_v)
    cmp_end = NSA_CMP_STRIDE * jnp.arange(n_cmp) + NSA_CMP_LEN - 1
    cmp_to_slc = _cmp_to_slc_matrix(n_cmp, n_slc)

    kb = k_slc.reshape(B, n_slc, NSA_SLC_BLOCK, G, Dh).transpose(0, 3, 1, 2, 4)
    vb = v_slc.reshape(B, n_slc, NSA_SLC_BLOCK, G, Dh).transpose(0, 3, 1, 2, 4)
    kw = jnp.pad(k_win, ((0, 0), (W, 0), (0, 0), (0, 0)))
    vw = jnp.pad(v_win, ((0, 0), (W, 0), (0, 0), (0, 0)))
    gates = jax.nn.sigmoid(gate_logits.astype(jnp.float32)).reshape(B, S, G, P, 3)

    q_ch = q.reshape(B, nc, qc_len, G, P, Dh).transpose(1, 0, 2, 3, 4, 5)
    g_ch = gates.reshape(B, nc, qc_len, G, P, 3).transpose(1, 0, 2, 3, 4, 5)
    starts = jnp.arange(nc, dtype=jnp.int32) * qc_len
    bi = jnp.arange(B)[:, None, None, None]
    gi = jnp.arange(G)[None, :, None, None]
    blk_ids = jnp.arange(n_slc)

    def chunk(args):
        qc, gc, t0 = args
        tq = t0 + jnp.arange(qc_len)
        s = jnp.einsum('bqgpd,bngd->bgpqn', qc, kc) * scale
        p_cmp = masked_softmax(s, cmp_end[None, :] <= tq[:, None])
        o_cmp = jnp.einsum('bgpqn,bngd->bqgpd', p_cmp.astype(vc.dtype), vc)
        imp = jnp.einsum('bgpqn,ns->bgqs', p_cmp, cmp_to_slc)
        own = tq // NSA_SLC_BLOCK
        forced = ((blk_ids[None, :] == 0) | (blk_ids[None, :] == own[:, None])
                  | (blk_ids[None, :] == own[:, None] - 1))
        imp = jnp.where(forced, jnp.inf, imp)
        imp = jnp.where(blk_ids[None, :] <= own[:, None], imp, -jnp.inf)
        _, sel = lax.top_k(imp, top)
        ks = kb[bi, gi, sel]
        vs = vb[bi, gi, sel]
        kpos = (sel[..., None] * NSA_SLC_BLOCK + jnp.arange(NSA_SLC_BLOCK)).reshape(
            B, G, 1, qc_len, top * NSA_SLC_BLOCK)
        s = jnp.einsum('bqgpd,bgqjkd->bgpqjk', qc, ks).reshape(
            B, G, P, qc_len, top * NSA_SLC_BLOCK) * scale
        p = masked_softmax(s, kpos <= tq[:, None]).astype(vs.dtype)
        o_slc = jnp.einsum('bgpqjk,bgqjkd->bqgpd',
                           p.reshape(B, G, P, qc_len, top, NSA_SLC_BLOCK), vs)
        kwc = lax.dynamic_slice_in_dim(kw, t0, W + qc_len, axis=1)
        vwc = lax.dynamic_slice_in_dim(vw, t0, W + qc_len, axis=1)
        kp = t0 - W + jnp.arange(W + qc_len)
        wmask = ((kp[None, :] <= tq[:, None]) & (kp[None, :] > tq[:, None] - W)
                 & (kp[None, :] >= 0))
        s = jnp.einsum('bqgpd,bkgd->bgpqk', qc, kwc) * scale
        p = masked_softmax(s, wmask).astype(vwc.dtype)
        o_win = jnp.einsum('bgpqk,bkgd->bqgpd', p, vwc)
        return gc[..., 0:1] * o_cmp + gc[..., 1:2] * o_slc + gc[..., 2:3] * o_win

    o = lax.map(chunk, (q_ch, g_ch, starts))
    return o.transpose(1, 0, 2, 3, 4, 5).reshape(B, S, H * Dh).astype(q.dtype)


def _rotate(t, cos, sin):
    t1, t2 = jnp.split(t, 2, axis=-1)
    c = cos[None, :, None, :]
    s = sin[None, :, None, :]
    return jnp.concatenate([t1 * c - t2 * s, t1 * s + t2 * c], axis=-1)


def retention(q, k, v, gn_g):
    B, S, H, Dk = q.shape
    Dv = v.shape[-1]
    C = RET_CHUNK
    nc = S // C
    gamma = 1.0 - 2.0 ** (-5.0 - np.arange(H))
    log_g = jnp.asarray(np.log(gamma).astype(np.float32))
    inv_freq = jnp.asarray((1.0 / (10000.0 ** np.linspace(0.0, 1.0, Dk // 2))).astype(np.float32))
    ang = jnp.arange(S, dtype=jnp.float32)[:, None] * inv_freq[None, :]
    cos, sin = jnp.cos(ang), jnp.sin(ang)
    qf = _rotate(q.astype(jnp.float32), cos, sin)
    kf = _rotate(k.astype(jnp.float32), cos, sin) * (Dk ** -0.5)
    vf = v.astype(jnp.float32)
    q_ch = qf.reshape(B, nc, C, H, Dk).transpose(1, 0, 3, 2, 4)
    k_ch = kf.reshape(B, nc, C, H, Dk).transpose(1, 0, 3, 2, 4)
    v_ch = vf.reshape(B, nc, C, H, Dv).transpose(1, 0, 3, 2, 4)
    idx = jnp.arange(C, dtype=jnp.float32)
    diff = idx[:, None] - idx[None, :]
    intra = jnp.where(diff >= 0, jnp.exp(log_g[:, None, None] * jnp.maximum(diff, 0.0)), 0.0)
    cross = jnp.exp(log_g[:, None] * (idx[None, :] + 1.0))[None, :, :, None]
    kdec = jnp.exp(log_g[:, None] * (C - 1.0 - idx[None, :]))[None, :, :, None]
    chunk_dec = jnp.exp(log_g * C)[None, :, None, None]

    def step(R, xs):
        qc, kc, vc = xs
        s = jnp.einsum('bhid,bhjd->bhij', qc, kc) * intra
        o = (jnp.einsum('bhij,bhjv->bhiv', s, vc)
             + jnp.einsum('bhid,bhdv->bhiv', qc, R) * cross)
        R = R * chunk_dec + jnp.einsum('bhjd,bhjv->bhdv', kc * kdec, vc)
        return R, o

    R0 = jnp.zeros((B, H, Dk, Dv), jnp.float32)
    _, o = lax.scan(step, R0, (q_ch, k_ch, v_ch))
    o = o.transpose(1, 0, 3, 2, 4).reshape(B, S, H, Dv)
    mu = jnp.mean(o, axis=-1, keepdims=True)
    var = jnp.mean(jnp.square(o - mu), axis=-1, keepdims=True)
    o = (o - mu) * lax.rsqrt(var + EPS)
    return (o.reshape(B, S, H * Dv) * gn_g.astype(jnp.float32)).astype(v.dtype)


def memory_attention(q, mem_k, mem_v):
    B, S, H, Dh = q.shape
    s = jnp.einsum('bshd,bmhd->bhsm', q, mem_k).astype(jnp.float32) * (Dh ** -0.5)
    p = jax.nn.softmax(s, axis=-1).astype(mem_v.dtype)
    return jnp.einsum('bhsm,bmhd->bshd', p, mem_v).reshape(B, S, H * Dh).astype(q.dtype)


def hybrid_layer(x, mem, pre_g, post_g, mem_g, w_in, w_mem_kv,
                 pe_k, w1_k, w2_k, pe_v, w1_v, w2_v, ret_gn_g, w_out):
    B, S, _ = x.shape
    h = rms_norm(x, pre_g)
    proj = h @ w_in
    (mq, mk, mv, nq, nkc, nvc, nks, nvs, nkw, nvw, ngate,
     rq, rk, rv, cq, z) = split_cols(proj, IN_SPLITS)

    def heads(t, n, d):
        return t.reshape(B, S, n, d)

    o_moba = moba_attention(heads(mq, MOBA_HEADS, MOBA_HEAD_DIM),
                            heads(mk, MOBA_HEADS, MOBA_HEAD_DIM),
                            heads(mv, MOBA_HEADS, MOBA_HEAD_DIM))
    kvh = lambda t: heads(t, NSA_KV_HEADS, NSA_HEAD_DIM)
    o_nsa = nsa_attention(heads(nq, NSA_HEADS, NSA_HEAD_DIM),
                          kvh(nkc), kvh(nvc), kvh(nks), kvh(nvs), kvh(nkw), kvh(nvw),
                          ngate.reshape(B, S, NSA_HEADS, 3),
                          pe_k, w1_k, w2_k, pe_v, w1_v, w2_v)
    o_ret = retention(heads(rq, RET_HEADS, RET_KEY_DIM), heads(rk, RET_HEADS, RET_KEY_DIM),
                      heads(rv, RET_HEADS, RET_VAL_DIM), ret_gn_g)
    mem_n = rms_norm(mem, mem_g)
    mem_k, mem_v = jnp.split(mem_n @ w_mem_kv, 2, axis=-1)
    M = mem.shape[1]
    o_mem = memory_attention(heads(cq, MEM_HEADS, MEM_HEAD_DIM),
                             mem_k.reshape(B, M, MEM_HEADS, MEM_HEAD_DIM),
                             mem_v.reshape(B, M, MEM_HEADS, MEM_HEAD_DIM))
    o = jnp.concatenate([o_moba, o_nsa, o_ret, o_mem], axis=-1) * jax.nn.silu(z)
    y = o @ w_out
    return x + rms_norm(y, post_g)


def setup_inputs(seed: int = 0) -> dict:
    key = jax.random.key(seed)
    ks = jax.random.split(key, 16)
    f32 = jnp.float32

    def normal(k, shape, scale):
        return jax.random.normal(k, shape, f32) * scale

    L = DEPTH
    cmp_in = NSA_CMP_LEN * NSA_HEAD_DIM
    return {
        "x": normal(ks[0], (BATCH, SEQ, D_MODEL), 1.0),
        "mem": normal(ks[1], (BATCH, MEM_LEN, D_MODEL), 1.0),
        "pre_norm_g": 1.0 + normal(ks[2], (L, D_MODEL), 0.02),
        "post_norm_g": 1.0 + normal(ks[3], (L, D_MODEL), 0.02),
        "mem_norm_g": 1.0 + normal(ks[4], (L, D_MODEL), 0.02),
        "w_in": normal(ks[5], (L, D_MODEL, IN_COLS), D_MODEL ** -0.5),
        "w_mem_kv": normal(ks[6], (L, D_MODEL, 2 * GROUP_WIDTH), D_MODEL ** -0.5),
        "nsa_pe_k": normal(ks[7], (L, NSA_CMP_LEN, NSA_HEAD_DIM), 0.1),
        "nsa_w1_k": normal(ks[8], (L, cmp_in, NSA_CMP_HIDDEN), cmp_in ** -0.5),
        "nsa_w2_k": normal(ks[9], (L, NSA_CMP_HIDDEN, NSA_HEAD_DIM), NSA_CMP_HIDDEN ** -0.5),
        "nsa_pe_v": normal(ks[10], (L, NSA_CMP_LEN, NSA_HEAD_DIM), 0.1),
        "nsa_w1_v": normal(ks[11], (L, cmp_in, NSA_CMP_HIDDEN), cmp_in ** -0.5),
        "nsa_w2_v": normal(ks[12], (L, NSA_CMP_HIDDEN, NSA_HEAD_DIM), NSA_CMP_HIDDEN ** -0.5),
        "ret_gn_g": 1.0 + normal(ks[13], (L, GROUP_WIDTH), 0.02),
        "w_out": normal(ks[14], (L, MIX_WIDTH, D_MODEL), MIX_WIDTH ** -0.5),
    }


def reference(x, mem, pre_norm_g, post_norm_g, mem_norm_g, w_in, w_mem_kv,
              nsa_pe_k, nsa_w1_k, nsa_w2_k, nsa_pe_v, nsa_w1_v, nsa_w2_v,
              ret_gn_g, w_out):
    for l in range(DEPTH):
        x = hybrid_layer(x, mem, pre_norm_g[l], post_norm_g[l], mem_norm_g[l],
                         w_in[l], w_mem_kv[l],
                         nsa_pe_k[l], nsa_w1_k[l], nsa_w2_k[l],
                         nsa_pe_v[l], nsa_w1_v[l], nsa_w2_v[l],
                         ret_gn_g[l], w_out[l])
    return x
```

```python
import numpy as np
import ml_dtypes
import concourse.bass as bass
import concourse.mybir as mybir
from concourse.bass_utils import run_bass_kernel_spmd

F32 = mybir.dt.float32
BF16 = mybir.dt.bfloat16
AF = mybir.ActivationFunctionType
ALU = mybir.AluOpType
AX = mybir.AxisListType

S = 4096
D = 1024
NL = 2
INC = 6424
NEG = -30000.0
EPS = 1e-6

C_MQ, C_MK, C_MV = 0, 512, 1024
C_NQ = 1536
C_NKC, C_NVC, C_NKS, C_NVS, C_NKW, C_NVW = 2048, 2176, 2304, 2432, 2560, 2688
C_NG = 2816
C_RQ, C_RK, C_RV = 2840, 3096, 3352
C_CQ = 3864
C_Z = 4376


class Buf:
    __slots__ = ("name", "w", "r")

    def __init__(self, name=""):
        self.name = name
        self.w = None
        self.r = []


class V:
    __slots__ = ("ap", "buf")

    def __init__(self, ap, buf):
        self.ap = ap
        self.buf = buf

    def __getitem__(self, k):
        return V(self.ap[k], self.buf)

    def re(self, s, **kw):
        return V(self.ap.rearrange(s, **kw), self.buf)

    def bc(self, dt):
        return V(self.ap.bitcast(dt), self.buf)


class Op:
    __slots__ = ("eng", "fn", "deps", "signal", "sem", "val", "is_dma", "chan", "gidx")

    def __init__(self, eng, fn, is_dma=False, chan=None):
        self.eng = eng
        self.fn = fn
        self.deps = []
        self.signal = False
        self.sem = None
        self.val = 0
        self.is_dma = is_dma
        self.chan = chan


ENGS = ("pe", "act", "dve", "pool", "sp")


class Prog:
    def __init__(self, nc):
        self.nc = nc
        self.ops = {e: [] for e in ENGS}
        self.all = []
        self.chans = {}
        self.chan_last = {}
        self.bar = None
        self.bar_pending = set()

    def add(self, eng, fn, reads=(), writes=(), is_dma=False, chan=None):
        op = Op(eng, fn, is_dma, chan)
        deps = []
        for b in reads:
            if b.w is not None:
                deps.append((b.w, "raw"))
        for b in writes:
            if b.w is not None:
                deps.append((b.w, "waw"))
            for r in b.r:
                deps.append((r, "war"))
        if eng in self.bar_pending:
            for d in self.bar:
                deps.append((d, "bar"))
            self.bar_pending.discard(eng)
        seen = set()
        for d, kind in deps:
            if d is op or id(d) in seen:
                continue
            if not self._needs(d, op, kind):
                continue
            seen.add(id(d))
            d.signal = True
            op.deps.append(d)
        for b in reads:
            b.r.append(op)
        for b in writes:
            b.w = op
            b.r = []
        op.gidx = len(self.all)
        self.all.append(op)
        self.ops[eng].append(op)
        return op

    @staticmethod
    def _needs(d, op, kind):
        if d.is_dma:
            return True
        if op.is_dma:
            return True
        if d.eng != op.eng:
            return True
        if d.eng == "pe":
            return False
        return True

    def barrier(self):
        last = []
        for e in ENGS:
            seen_compute = False
            for op in reversed(self.ops[e]):
                if op.is_dma:
                    last.append(op)
                elif not seen_compute:
                    last.append(op)
                    seen_compute = True
                if len(last) > 4000:
                    break
        per = {}
        out = []
        for op in last:
            if op.is_dma:
                if op.chan not in per:
                    per[op.chan] = op
                    out.append(op)
            else:
                out.append(op)
        self.bar = out
        self.bar_pending = set(ENGS)

    def mm(self, out, lhsT, rhs, start=True, stop=True, extra_reads=()):
        o, l, r = out.ap, lhsT.ap, rhs.ap
        return self.add("pe", lambda e: e.matmul(o, l, r, start=start, stop=stop),
                        reads=[lhsT.buf, rhs.buf] + list(extra_reads), writes=[out.buf])

    def tr(self, out, in_, ident):
        o, i, d = out.ap, in_.ap, ident.ap
        return self.add("pe", lambda e: e.transpose(o, i, d), reads=[in_.buf, ident.buf], writes=[out.buf])

    def act(self, out, in_, func, scale=1.0, bias=0.0, accum=None, eng="act"):
        o, i = out.ap, in_.ap
        reads = [in_.buf]
        writes = [out.buf]
        b = bias
        if isinstance(bias, V):
            reads.append(bias.buf)
            b = bias.ap
        sc = scale
        if isinstance(scale, V):
            reads.append(scale.buf)
            sc = scale.ap
        acc = None
        if accum is not None:
            writes.append(accum.buf)
            acc = accum.ap
        if acc is None:
            fn = lambda e: e.activation(o, i, func, bias=b, scale=sc)
        else:
            fn = lambda e: e.activation(o, i, func, bias=b, scale=sc, accum_out=acc)
        return self.add("act", fn, reads=reads, writes=writes)

    def ts(self, out, in0, s1, s2, op0, op1=None, eng="dve", accum=None):
        o, i = out.ap, in0.ap
        reads = [in0.buf]
        a1, a2 = s1, s2
        if isinstance(s1, V):
            reads.append(s1.buf)
            a1 = s1.ap
        if isinstance(s2, V):
            reads.append(s2.buf)
            a2 = s2.ap
        writes = [out.buf]
        kw = {}
        if op1 is not None:
            kw["op1"] = op1
        if accum is not None:
            kw["accum_out"] = accum.ap
            writes.append(accum.buf)
        return self.add(eng, lambda e: e.tensor_scalar(o, i, a1, a2, op0, **kw), reads=reads, writes=writes)

    def tt(self, out, in0, in1, op, eng="dve"):
        o, a, b = out.ap, in0.ap, in1.ap
        return self.add(eng, lambda e: e.tensor_tensor(o, a, b, op), reads=[in0.buf, in1.buf], writes=[out.buf])

    def stt(self, out, in0, scalar, in1, op0, op1, eng="dve"):
        o, a, b = out.ap, in0.ap, in1.ap
        reads = [in0.buf, in1.buf]
        s = scalar
        if isinstance(scalar, V):
            reads.append(scalar.buf)
            s = scalar.ap
        return self.add(eng, lambda e: e.scalar_tensor_tensor(o, a, s, b, op0, op1), reads=reads, writes=[out.buf])

    def copy(self, out, in_, eng="dve"):
        o, i = out.ap, in_.ap
        if eng == "act":
            return self.add("act", lambda e: e.copy(o, i), reads=[in_.buf], writes=[out.buf])
        return self.add(eng, lambda e: e.tensor_copy(o, i), reads=[in_.buf], writes=[out.buf])

    def recip(self, out, in_):
        o, i = out.ap, in_.ap
        return self.add("dve", lambda e: e.reciprocal(o, i), reads=[in_.buf], writes=[out.buf])

    def memset(self, out, val, eng="pool"):
        o = out.ap
        return self.add(eng, lambda e: e.memset(o, val), writes=[out.buf])

    def dma(self, out, in_, eng="sp", chan=None, noncontig=False, extra_reads=()):
        o, i = out.ap, in_.ap
        if chan is None:
            chan = "c_" + str(id(out.buf))
        if chan not in self.chans:
            self.chans[chan] = [None, 0]
            self.chan_last[chan] = None
        if noncontig:
            fn = lambda e: e.dma_start(out=o, in_=i, allow_slow_non_contiguous=True)
        else:
            fn = lambda e: e.dma_start(out=o, in_=i)
        op = self.add(eng, fn, reads=[in_.buf] + list(extra_reads), writes=[out.buf], is_dma=True, chan=chan)
        self.chans[chan][1] += 16
        op.val = self.chans[chan][1]
        prev = self.chan_last.get(chan)
        if prev is not None and prev not in op.deps:
            prev.signal = True
            op.deps.append(prev)
        self.chan_last[chan] = op
        return op

    def emit(self, stack):
        nc = self.nc
        esem = {}
        for e in ENGS:
            esem[e] = stack.enter_context(nc.semaphore("s_" + e))
        for c in self.chans:
            self.chans[c][0] = stack.enter_context(nc.semaphore("d%d" % len(esem)))
            esem["chan_" + c] = self.chans[c][0]
        for e in ENGS:
            cnt = 0
            for op in self.ops[e]:
                if op.is_dma:
                    op.sem = self.chans[op.chan][0]
                    op.signal = True
                else:
                    op.sem = esem[e]
                    if op.signal:
                        cnt += 1
                        op.val = cnt
        print("sem max", {e: max([op.val for op in self.ops[e] if not op.is_dma] + [0]) for e in ENGS},
              "chan max", max(v[1] for v in self.chans.values()), flush=True)
        final_waits = []
        for c, (sem, val) in self.chans.items():
            final_waits.append((sem, val))
        self.final_waits = final_waits
        prog = self

        def run(engine, ename):
            seen = {}
            nwait = 0
            for op in prog.ops[ename]:
                for d in op.deps:
                    k = id(d.sem)
                    if seen.get(k, 0) >= d.val:
                        continue
                    seen[k] = d.val
                    engine.wait_ge(d.sem, d.val)
                    nwait += 1
                ins = op.fn(engine)
                if op.signal:
                    ins.then_inc(op.sem, 16 if op.is_dma else 1)
            if ename == "sp":
                for sem, val in prog.final_waits:
                    if seen.get(id(sem), 0) < val:
                        engine.wait_ge(sem, val)

        with nc.Block() as block:
            @block.tensor
            def _(e):
                run(e, "pe")

            @block.scalar
            def _(e):
                run(e, "act")

            @block.vector
            def _(e):
                run(e, "dve")

            @block.gpsimd
            def _(e):
                run(e, "pool")

            @block.sync
            def _(e):
                run(e, "sp")


BIGM = 30000.0
NSA_STOP = 99
NSA_G = (0, 1)
NSA_BR = (0, 1, 2)
NSA_NOGATE = False
RET_G = [1.0 - 2.0 ** (-5.0 - h) for h in range(4)]


def _layout(items):
    off = 0
    d = {}
    for name, w in items:
        d[name] = (off, w)
        off += w
    return d, off


CB_ITEMS = [("ident", 128), ("ones", 128), ("tcaus", 896), ("twin", 1408), ("tcmp", 2560),
            ("eb16", 2048), ("eslc", 4096), ("sel", 3072), ("caug", 130)]
CB, NCB = _layout(CB_ITEMS)
CF_ITEMS = [("ident", 128), ("ones", 128), ("pastm", 512), ("ownm2", 512), ("fbt", 128),
            ("intra", 512), ("cross", 256), ("kdec", 4), ("neghalf", 1), ("pad", 3)]
CF, NCF = _layout(CF_ITEMS)


def make_consts():
    cb = np.zeros((128, NCB), np.float32)
    cf = np.zeros((128, NCF), np.float32)

    def put(arr, lay, name, val):
        o, w = lay[name]
        assert val.shape[1] == w, (name, val.shape, w)
        arr[: val.shape[0], o:o + w] = val

    p = np.arange(128)[:, None]
    put(cb, CB, "ident", np.eye(128, dtype=np.float32))
    put(cb, CB, "ones", np.ones((128, 128), np.float32))
    j = np.arange(896)[None, :]
    put(cb, CB, "tcaus", np.where((j - 384) >= p, 0.0, NEG).astype(np.float32))
    j = np.arange(1408)[None, :]
    dd = (j - 384) - p
    put(cb, CB, "twin", np.where((dd >= 0) & (dd < 512), 0.0, NEG).astype(np.float32))
    j = np.arange(2560)[None, :]
    put(cb, CB, "tcmp", np.where(16 * p + 31 <= j, 0.0, NEG).astype(np.float32))
    e = np.zeros((16, 2048), np.float32)
    for b in range(16):
        e[b, b * 128:(b + 1) * 128] = 1.0
    put(cb, CB, "eb16", e)
    e = np.zeros((128, 4096), np.float32)
    for b in range(64):
        e[b, b * 64:(b + 1) * 64] = 1.0
        e[64 + b, b * 64:(b + 1) * 64] = 1.0
    put(cb, CB, "eslc", e)
    e = np.zeros((56, 3072), np.float32)
    for r in range(24):
        e[r, r * 128:(r + 1) * 128] = 1.0
        e[32 + r, r * 128:(r + 1) * 128] = 1.0
    put(cb, CB, "sel", e)
    n_cmp, n_slc = 255, 64
    mat = np.zeros((256, 64), np.float32)
    for jj in range(n_slc):
        for a in range(4):
            for b in range(2):
                i = 4 * jj + a - b
                if 0 <= i < n_cmp:
                    mat[i, jj] += 1.0
    ca = np.zeros((128, 130), np.float32)
    for t in range(2):
        ca[:, t * 65:t * 65 + 64] = mat[t * 128:(t + 1) * 128]
        ca[:, t * 65 + 64] = 1.0
    put(cb, CB, "caug", ca)

    put(cf, CF, "ident", np.eye(128, dtype=np.float32))
    put(cf, CF, "ones", np.ones((128, 128), np.float32))
    pm = np.zeros((32, 16), np.float32)
    om = np.zeros((32, 16), np.float32)
    for qt in range(32):
        own = qt // 2
        for b in range(16):
            pm[qt, b] = 0.0 if b < own else -1e30
            om[qt, b] = (-BIGM if b < own else (0.0 if b == own else -2 * BIGM))
    put(cf, CF, "pastm", np.broadcast_to(pm.reshape(1, 512), (128, 512)))
    put(cf, CF, "ownm2", np.broadcast_to(om.reshape(1, 512), (128, 512)))
    fb = np.zeros((128, 128), np.float32)
    for ql in range(128):
        orel = ql // 64
        for jx in range(127):
            m = jx - 63
            if m > orel:
                fb[ql, jx] = -1000.0
            elif m == orel or m == orel - 1:
                fb[ql, jx] = 1000.0
    put(cf, CF, "fbt", fb)
    it = np.zeros((128, 512), np.float32)
    jj = np.arange(128)[:, None]
    ii = np.arange(128)[None, :]
    for h in range(4):
        g = RET_G[h]
        it[:, h * 128:(h + 1) * 128] = np.where(ii >= jj, g ** np.maximum(ii - jj, 0), 0.0) * 0.125
    put(cf, CF, "intra", it)
    cr = np.zeros((128, 256), np.float32)
    for pair in range(2):
        for half in range(2):
            g = RET_G[2 * pair + half]
            cr[half * 64:(half + 1) * 64, pair * 128:(pair + 1) * 128] = (g ** (np.arange(128) + 1.0))[None, :]
    put(cf, CF, "cross", cr)
    kd = np.zeros((128, 4), np.float32)
    for h in range(4):
        kd[:, h] = RET_G[h] ** (127.0 - np.arange(128)) * 0.125
    put(cf, CF, "kdec", kd)
    put(cf, CF, "neghalf", np.full((128, 1), -0.5, np.float32))
    inv_freq = (1.0 / (10000.0 ** np.linspace(0.0, 1.0, 32))).astype(np.float32)
    ang = np.arange(S, dtype=np.float32)[:, None] * inv_freq[None, :]
    cos = np.cos(ang).astype(np.float32).T
    sin = np.sin(ang).astype(np.float32).T
    c2 = np.concatenate([cos, cos, cos, cos], 0)
    s2 = np.concatenate([sin, sin, sin, sin], 0)
    return cb, cf, np.ascontiguousarray(c2), np.ascontiguousarray(s2)


class Arena:
    def __init__(self, ap32, nwords):
        self.ap = ap32
        self.n = nwords
        self.top = 0
        self.peak = 0

    def alloc(self, shape, dt, name=""):
        free = 1
        for s in shape[1:]:
            free *= s
        words = free if dt == F32 else (free + 1) // 2
        a = self.top
        self.top += words
        self.peak = max(self.peak, self.top)
        assert self.top <= self.n, ("arena overflow", name, self.top, self.n)
        ap = self.ap[:, a:a + words]
        if dt != F32:
            ap = ap.bitcast(dt)[:, 0:free]
        if len(shape) == 3:
            ap = ap.rearrange("p (a b) -> p a b", a=shape[1])
        ap = ap[0:shape[0]]
        return V(ap, Buf(name))


def build(nl=NL, dbg=False, phases=("mem", "moba", "nsa", "ret")):
    from contextlib import ExitStack
    nc = bass.Bass("TRN2", target_bir_lowering=False)

    def din(name, shape, dt=F32):
        return V(nc.dram_tensor(name, shape, dt, kind="ExternalInput").ap(), Buf(name))

    x_d = din("x", [S, D])
    mem_d = din("mem", [256, D])
    pre_g_d = din("pre_norm_g", [NL, D])
    post_g_d = din("post_norm_g", [NL, D])
    mem_g_d = din("mem_norm_g", [NL, D])
    w_in_d = din("w_in", [NL, D, INC])
    w_mem_d = din("w_mem_kv", [NL, D, 1024])
    pe_k_d = din("nsa_pe_k", [NL, 32, 64])
    w1_k_d = din("nsa_w1_k", [NL, 2048, 128])
    w2_k_d = din("nsa_w2_k", [NL, 128, 64])
    pe_v_d = din("nsa_pe_v", [NL, 32, 64])
    w1_v_d = din("nsa_w1_v", [NL, 2048, 128])
    w2_v_d = din("nsa_w2_v", [NL, 128, 64])
    retg_d = din("ret_gn_g", [NL, 512])
    w_out_d = din("w_out", [NL, 2048, D])
    cb_d = din("cb", [128, NCB])
    cf_d = din("cf", [128, NCF])
    rc_d = din("rc", [128, S])
    rs_d = din("rs", [128, S])
    out_ap = nc.dram_tensor("out", [S, D], F32, kind="ExternalOutput").ap()
    x1_ap = nc.dram_tensor("x1s", [S, D], F32, kind="Internal").ap()
    oT_kind = "ExternalOutput" if dbg else "Internal"
    oT_ap = nc.dram_tensor("oTs", [2048, S], BF16, kind=oT_kind).ap()
    oT_bufs = [Buf("oT%d" % i) for i in range(16)]
    x1_bufs = [Buf("x1_%d" % i) for i in range(32)]
    out_bufs = [Buf("out_%d" % i) for i in range(32)]

    P = Prog(nc)
    st = ExitStack()
    with st:
        NW = 52000
        arena_t = st.enter_context(nc.sbuf_tensor("arena", [128, NW], F32))
        A = Arena(arena_t[:, :], NW)
        banks = []
        for i in range(8):
            pt = st.enter_context(nc.psum_tensor("psb%d" % i, [128, 512], F32))
            banks.append(V(pt[:, :], Buf("ps%d" % i)))
        grp = {"S": [0, 1, 2], "O": [3, 4], "X": [5, 6, 7]}
        gctr = {"S": 0, "O": 0, "X": 0}

        def ps(g):
            b = banks[grp[g][gctr[g] % len(grp[g])]]
            gctr[g] += 1
            return b

        cbf = A.alloc([128, NCB], BF16, "cbf")
        cff = A.alloc([128, NCF], F32, "cff")
        hT = A.alloc([128, 8, S], BF16, "hT")
        gpre = A.alloc([128, 8 * NL], F32, "gpre")
        gmem = A.alloc([128, 8 * NL], F32, "gmem")
        retg = A.alloc([128, 4 * NL], F32, "retg")

        def cb(name, rows=128):
            o, w = CB[name]
            return cbf[0:rows, o:o + w]

        def cf(name, rows=128):
            o, w = CF[name]
            return cff[0:rows, o:o + w]

        for k in range(0, NCB, 2048):
            e = min(NCB, k + 2048)
            P.dma(cbf[:, k:e], cb_d[:, k:e], eng="pool", chan="const")
        P.dma(cff, cf_d, eng="sp", chan="const2")
        for l in range(NL):
            P.dma(gpre[:, l * 8:(l + 1) * 8], V(pre_g_d.ap[l].rearrange("(c p) -> p c", p=128), pre_g_d.buf), eng="sp", chan="const2", noncontig=True)
            P.dma(gmem[:, l * 8:(l + 1) * 8], V(mem_g_d.ap[l].rearrange("(c p) -> p c", p=128), mem_g_d.buf), eng="sp", chan="const2", noncontig=True)
            P.dma(retg[:, l * 4:(l + 1) * 4], V(retg_d.ap[l].rearrange("(h v) -> v h", v=128), retg_d.buf), eng="sp", chan="const2", noncontig=True)
        ident_bf = cb("ident")
        ones_bf = cb("ones")
        ident_f = cf("ident")
        ones_f = cf("ones")
        mark0 = A.top

        ev_ctr = [0]

        def evac(dst, src, scale=None):
            ev_ctr[0] += 1
            if ev_ctr[0] % 2 == 0:
                P.act(dst, src, AF.Copy, scale=(1.0 if scale is None else scale))
            else:
                if scale is None:
                    P.copy(dst, src, eng="dve")
                else:
                    P.ts(dst, src, scale, None, ALU.mult)

        sh = {}

        def new_phase(n_oc=2, n_w=4):
            P.barrier()
            A.top = mark0
            sh["w"] = [A.alloc([128, 8, 128], BF16, "w%d" % i) for i in range(n_w)]
            sh["wc"] = 0
            sh["p"] = [A.alloc([128, 512], BF16, "pb%d" % i) for i in range(4)]
            sh["pc"] = 0
            sh["rr"] = [A.alloc([128, 512], F32, "rr%d" % i) for i in range(2)]
            sh["bs"] = [A.alloc([128, 512], F32, "bs%d" % i) for i in range(2)]
            sh["fc"] = 0
            sh["sz"] = [A.alloc([128, 512], BF16, "sz%d" % i) for i in range(2)]
            sh["szc"] = 0
            sh["oc"] = [A.alloc([128, S], BF16, "oc%d" % i) for i in range(n_oc)]

        def load_w(l, col0, n, src=None, dstoff=0, slot=None, neg=False):
            if slot is None:
                k = sh["wc"] % len(sh["w"])
                slot = (sh["w"][k], k)
                sh["wc"] += 1
            sl, k = slot
            srcv = (w_in_d if src is None else src)
            s3 = V(srcv.ap[l].rearrange("(c p) n -> p c n", p=128)[:, :, col0:col0 + n], srcv.buf)
            P.dma(sl[:, :, dstoff:dstoff + n], s3, eng="pool", chan="w%d" % k)
            if neg:
                P.ts(sl[:, :, dstoff:dstoff + n], sl[:, :, dstoff:dstoff + n], -1.0, None, ALU.mult, eng="pool")
            return sl[:, :, 0:dstoff + n], slot

        def lw(l, col0, n, src=None):
            return load_w(l, col0, n, src=src)[0]

        def proj_fm(dst, w, prow, M, sink=None):
            for tc in range(8):
                pb = ps("X")
                for c in range(8):
                    P.mm(pb[prow:prow + M, :], w[:, c, :], hT[:, c, tc * 512:(tc + 1) * 512], start=(c == 0), stop=(c == 7))
                if sink is None:
                    evac(dst[prow:prow + M, tc * 512:(tc + 1) * 512], pb[prow:prow + M, :])
                else:
                    sink(tc, pb)

        def proj_tm(w, n, sink, src=None, ntile=32):
            srcT = hT if src is None else src
            for tt in range(ntile):
                pb = ps("X")
                for c in range(8):
                    P.mm(pb[:, 0:n], srcT[:, c, tt * 128:(tt + 1) * 128], w[:, c, :], start=(c == 0), stop=(c == 7))
                sink(tt, pb)

        def attn_chunk(qc, qT, kT, pl, vfn, M, scale, extra=None):
            n = len(pl)
            O = ps("O")
            O2 = ps("O") if extra is not None else None
            pbs = {}

            def do_s(i):
                kt, masks = pl[i]
                Sp = ps("S")
                P.mm(Sp, kT[:, kt * 128:(kt + 1) * 128], qT[:, qc * 512:(qc + 1) * 512], start=True, stop=(len(masks) == 0))
                for mi, (ml, mr) in enumerate(masks):
                    P.mm(Sp, ml, mr, start=False, stop=(mi == len(masks) - 1))
                Pb = sh["p"][sh["pc"] % 4]
                sh["pc"] += 1
                P.act(Pb, Sp, AF.Exp, scale=scale)
                return Pb

            LOOK = 2
            for i in range(min(LOOK, n)):
                pbs[i] = do_s(i)
            for i in range(n):
                if i + LOOK < n:
                    pbs[i + LOOK] = do_s(i + LOOK)
                pb = pbs.pop(i)
                P.mm(O[0:M, :], vfn(pl[i][0]), pb, start=(i == 0), stop=(i == n - 1))
                if extra is not None:
                    em, efn = extra
                    P.mm(O2[0:em, :], efn(pl[i][0]), pb, start=(i == 0), stop=(i == n - 1))
            return O, O2

        def bcast_rden(O, dr, gate_ps=None, guard=False):
            k = sh["fc"]
            sh["fc"] += 1
            r1 = sh["rr"][k % 2]
            if guard:
                P.ts(r1[dr:dr + 1, :], O[dr:dr + 1, :], 1e-30, None, ALU.max)
                P.recip(r1[dr:dr + 1, :], r1[dr:dr + 1, :])
            else:
                P.recip(r1[dr:dr + 1, :], O[dr:dr + 1, :])
            if gate_ps is not None:
                P.tt(r1[dr:dr + 1, :], r1[dr:dr + 1, :], gate_ps[dr:dr + 1, :], ALU.mult)
            B = ps("X")
            P.mm(B, ones_f[dr:dr + 1, :], r1[dr:dr + 1, :])
            Bs = sh["bs"][k % 2]
            P.copy(Bs, B, eng="act")
            return Bs

        def gate_store(l, ci, ocT, slot_id):
            w = lw(l, C_Z + ci * 128, 128)

            def sink(tc, pb):
                sz = sh["sz"][sh["szc"] % 2]
                sh["szc"] += 1
                P.act(sz, pb, AF.Silu)
                sl = ocT[:, tc * 512:(tc + 1) * 512]
                P.tt(sl, sl, sz, ALU.mult, eng="pool")
            proj_fm(None, w, 0, 128, sink=sink)
            P.dma(V(oT_ap[ci * 128:(ci + 1) * 128, :], oT_bufs[ci]), ocT, eng="sp", chan="oc%d" % slot_id)

        def phase_norm_T(src_ap, src_bufs, ntok, g_sb, dstT):
            xsl = [A.alloc([128, D], F32, "xs%d" % i) for i in range(3)]
            xnl = [A.alloc([128, D], BF16, "xn%d" % i) for i in range(2)]
            junk = A.alloc([128, D], BF16, "junk")
            stt_ = [A.alloc([128, 4], F32, "st%d" % i) for i in range(4)]
            g3 = V(g_sb.ap.rearrange("p (c o) -> p c o", o=1).to_broadcast([128, 8, 128]), g_sb.buf)
            for tt in range(ntok // 128):
                xs = xsl[tt % 3]
                P.dma(xs, V(src_ap[tt * 128:(tt + 1) * 128, :], src_bufs[tt]), eng="sp", chan="xs%d" % (tt % 3))
                stt = stt_[tt % 4]
                P.act(junk, xs, AF.Square, accum=stt[:, 0:1])
                P.ts(stt[:, 1:2], stt[:, 0:1], 1.0 / D, EPS, ALU.mult, ALU.add)
                P.tt(stt[:, 2:3], stt[:, 1:2], cf("neghalf"), ALU.pow, eng="pool")
                xn = xnl[tt % 2]
                P.ts(xn, xs, stt[:, 2:3], None, ALU.mult)
                pt = ps("X").bc(BF16)
                for c in range(8):
                    P.tr(pt[:, c * 128:(c + 1) * 128], xn[:, c * 128:(c + 1) * 128], ident_bf)
                P.tt(dstT[:, :, tt * 128:(tt + 1) * 128], pt.re("p (c t) -> p c t", c=8), g3, ALU.mult)

        def phase_out(l, src_ap, src_bufs, dst_ap, dst_bufs):
            wout = A.alloc([128, 16, D], BF16, "wout")
            w3 = V(w_out_d.ap[l].rearrange("(c p) n -> p c n", p=128), w_out_d.buf)
            for c0 in range(0, 16, 2):
                P.dma(wout[:, c0:c0 + 2, :], w3[:, c0:c0 + 2, :], eng="pool", chan="wout")
            gp = A.alloc([128, D], F32, "gpost")
            P.dma(gp, V(post_g_d.ap[l:l + 1, :].to_broadcast([128, D]), post_g_d.buf), eng="sp", chan="gpost")
            xsl = [A.alloc([128, D], F32, "xo%d" % i) for i in range(2)]
            otl = [A.alloc([128, 16, 128], BF16, "ot%d" % i) for i in range(2)]
            ynl = [A.alloc([128, D], F32, "yn%d" % i) for i in range(2)]
            junk = A.alloc([128, 512], BF16, "junk2")
            stt_ = [A.alloc([128, 4], F32, "sto%d" % i) for i in range(4)]
            oT3 = oT_ap.rearrange("(c p) t -> p c t", p=128)
            for tt in range(32):
                ot = otl[tt % 2]
                P.dma(ot, V(oT3[:, :, tt * 128:(tt + 1) * 128], oT_bufs[0]), eng="sp", chan="ot%d" % (tt % 2), extra_reads=oT_bufs[1:])
                xs = xsl[tt % 2]
                P.dma(xs, V(src_ap[tt * 128:(tt + 1) * 128, :], src_bufs[tt]), eng="sp", chan="xo%d" % (tt % 2))
                stt = stt_[tt % 4]
                pbs_ = []
                for half in range(2):
                    pb = ps("X")
                    pbs_.append(pb)
                    for c in range(16):
                        P.mm(pb, ot[:, c, :], wout[:, c, half * 512:(half + 1) * 512], start=(c == 0), stop=(c == 15))
                    P.act(junk, pb, AF.Square, accum=stt[:, half:half + 1])
                P.tt(stt[:, 2:3], stt[:, 0:1], stt[:, 1:2], ALU.add)
                P.ts(stt[:, 2:3], stt[:, 2:3], 1.0 / D, EPS, ALU.mult, ALU.add)
                P.tt(stt[:, 3:4], stt[:, 2:3], cf("neghalf"), ALU.pow, eng="pool")
                yn = ynl[tt % 2]
                for half in range(2):
                    hs = slice(half * 512, (half + 1) * 512)
                    P.stt(yn[:, hs], pbs_[half], stt[:, 3:4], gp[:, hs], ALU.mult, ALU.mult)
                P.tt(yn, yn, xs, ALU.add, eng="pool")
                P.dma(V(dst_ap[tt * 128:(tt + 1) * 128, :], dst_bufs[tt]), yn, eng="sp", chan="sto%d" % (tt % 2))

        def phase_mem(l):
            memT = A.alloc([128, 8, 256], BF16, "memT")
            m0 = A.top
            phase_norm_T(mem_d.ap, [mem_d.buf] * 2, 256, gmem[:, l * 8:(l + 1) * 8], memT)
            A.top = m0
            P.barrier()
            vtm = A.alloc([128, 2, 512], BF16, "memv")
            kTm = A.alloc([128, 256], BF16, "memk")
            qTm = A.alloc([128, S], BF16, "memq")
            rdn = [A.alloc([128, 512], F32, "rdn%d" % i) for i in range(2)]
            for j in range(4):
                w = lw(l, 512 + j * 128, 128, src=w_mem_d)

                def sinkv(tt, pb, j=j):
                    evac(vtm[:, tt, j * 128:(j + 1) * 128], pb[:, 0:128])
                proj_tm(w, 128, sinkv, src=memT, ntile=2)
            sc = 128.0 ** -0.5
            for h in range(4):
                wk = lw(l, h * 128, 128, src=w_mem_d)
                pb = ps("X")
                for c in range(8):
                    P.mm(pb[:, 0:256], wk[:, c, :], memT[:, c, :], start=(c == 0), stop=(c == 7))
                evac(kTm, pb[:, 0:256])
                wq = lw(l, C_CQ + h * 128, 128)
                proj_fm(qTm, wq, 0, 128)
                oc = sh["oc"][h % 2]
                for qc in range(8):
                    O, O2 = attn_chunk(qc, qTm, kTm, [(0, []), (1, [])], lambda kt, h=h: vtm[:, kt, h * 128:(h + 1) * 128], 128, sc,
                                       extra=(128, lambda kt: ones_bf))
                    r = rdn[qc % 2]
                    P.recip(r, O2)
                    P.tt(oc[:, qc * 512:(qc + 1) * 512], O, r, ALU.mult)
                gate_store(l, 12 + h, oc, h % 2)

        def phase_moba(l):
            qT = A.alloc([128, S], BF16, "mqT")
            kT = A.alloc([128, S], BF16, "mkT")
            vaug = A.alloc([128, 32, 193], BF16, "mvaug")
            mblk = A.alloc([16, S], BF16, "mblk")
            kmf = A.alloc([128, 16], F32, "kmf")
            kmb = A.alloc([128, 16], BF16, "kmb")
            gsb = A.alloc([128, 512], F32, "gsb")
            m8 = A.alloc([128, 256], F32, "m8")
            a1 = A.alloc([128, 512], F32, "a1")
            mv = A.alloc([128, 512], BF16, "mv")
            P.memset(vaug[:, :, 64:66], 1.0)
            P.memset(vaug[:, :, 66:129], 0.0)
            o16 = CB["eb16"][0]
            otc = CB["tcaus"][0]
            for pair in range(4):
                wq = lw(l, C_MQ + pair * 128, 128)
                wk = lw(l, C_MK + pair * 128, 128)
                wv = lw(l, C_MV + pair * 128, 128)
                proj_fm(qT, wq, 0, 128)
                proj_fm(kT, wk, 0, 128)

                def sinkv(tt, pb):
                    evac(vaug[:, tt, 0:64], pb[:, 0:64])
                    evac(vaug[:, tt, 129:193], pb[:, 64:128])
                proj_tm(wv, 128, sinkv)
                oc = sh["oc"][pair % 2]
                for hh in range(2):
                    r0 = 64 * hh
                    rows = slice(r0, r0 + 64)
                    o_, i_ = kmf.ap[rows], kT.ap[rows].rearrange("p (b k) -> p b k", k=256)
                    P.add("dve", lambda e, o_=o_, i_=i_: e.tensor_reduce(o_, i_, AX.X, ALU.add), reads=[kT.buf], writes=[kmf.buf])
                    P.ts(kmb[rows], kmf[rows], 1.0 / 256, None, ALU.mult)
                    G = ps("X")
                    for qt in range(32):
                        P.mm(G[:, qt * 16:(qt + 1) * 16], qT[rows, qt * 128:(qt + 1) * 128], kmb[rows, :], start=True, stop=True)
                    P.tt(gsb, G, cf("pastm"), ALU.add)
                    for qt in range(32):
                        o_, i_ = m8.ap[:, qt * 8:(qt + 1) * 8], gsb.ap[:, qt * 16:(qt + 1) * 16]
                        P.add("dve", lambda e, o_=o_, i_=i_: e.max(o_, i_), reads=[gsb.buf], writes=[m8.buf])
                    thr = V(m8.ap.rearrange("p (q e) -> p q e", e=8)[:, :, 2:3].to_broadcast([128, 32, 16]), m8.buf)
                    P.tt(a1.re("p (q b) -> p q b", b=16), gsb.re("p (q b) -> p q b", b=16), thr, ALU.is_ge)
                    P.stt(a1, a1, BIGM, cf("ownm2"), ALU.mult, ALU.add)
                    P.ts(mv, a1, 0.0, None, ALU.min)
                    for g4 in range(4):
                        pt = ps("X").bc(BF16)
                        for j in range(8):
                            qt = g4 * 8 + j
                            P.tr(pt[0:16, j * 128:(j + 1) * 128], mv[:, qt * 16:(qt + 1) * 16], ident_bf)
                        evac(mblk[0:16, g4 * 1024:(g4 + 1) * 1024], pt[0:16, :])
                    if hh == 0:
                        vfn = lambda kt: vaug[:, kt, 0:65]
                        M, dr = 65, 64
                    else:
                        vfn = lambda kt: vaug[:, kt, 65:193]
                        M, dr = 128, 0
                    for qc in range(8):
                        pl = []
                        for kt in range(4 * qc + 4):
                            b0 = o16 + (kt // 2) * 128
                            masks = [(cbf[0:16, b0:b0 + 128], mblk[0:16, qc * 512:(qc + 1) * 512])]
                            if kt >= 4 * qc:
                                delta = 128 * kt - 512 * qc
                                masks.append((ident_bf, cbf[:, otc + 384 - delta:otc + 384 - delta + 512]))
                            pl.append((kt, masks))
                        O, _ = attn_chunk(qc, qT[rows], kT[rows], pl, vfn, M, 0.125)
                        Bs = bcast_rden(O, dr)
                        P.tt(oc[rows, qc * 512:(qc + 1) * 512], O[rows, :], Bs[rows, :], ALU.mult)
                gate_store(l, pair, oc, pair % 2)

        def phase_nsa(l):
            oc = sh["oc"][0]
            kcT = A.alloc([128, 256], BF16, "kcT")
            vcaug = A.alloc([128, 4, 129], BF16, "vcaug")
            Gt = A.alloc([64, S], BF16, "Gt")
            mslc = A.alloc([128, S], BF16, "mslc")
            qT = A.alloc([128, S], BF16, "nqT")
            acc = [A.alloc([128, 512], F32, "acc%d" % i) for i in range(2)]
            tmpb = [A.alloc([128, 512], F32, "tmpb%d" % i) for i in range(2)]
            m1 = A.top
            kin = A.alloc([128, S], BF16, "kin")
            vin = A.alloc([128, S], BF16, "vin")
            proj_fm(kin, lw(l, C_NKC, 128), 0, 128)
            proj_fm(vin, lw(l, C_NVC, 128), 0, 128)
            P.memset(vcaug[:, :, 0:1], 1.0)
            P.memset(vcaug[:, :, 1:64], 0.0)
            P.memset(vcaug[:, :, 128:129], 1.0)
            w1 = A.alloc([128, 32, 128], BF16, "w1")
            peT = A.alloc([128, 32], BF16, "peT")
            w2 = A.alloc([128, 64], BF16, "w2")
            bias = A.alloc([128, 2], F32, "cbias")
            hid = A.alloc([128, 256], BF16, "hid")
            for which, pe_d, w1_d, w2_d, xin in (("k", pe_k_d, w1_k_d, w2_k_d, kin), ("v", pe_v_d, w1_v_d, w2_v_d, vin)):
                s3 = V(w1_d.ap[l].rearrange("(l d) j -> d l j", d=64), w1_d.buf)
                for hlf in range(2):
                    for l0 in range(0, 32, 8):
                        P.dma(w1[hlf * 64:(hlf + 1) * 64, l0:l0 + 8, :], s3[:, l0:l0 + 8, :], eng="pool", chan="w1")
                    P.dma(peT[hlf * 64:(hlf + 1) * 64, :], V(pe_d.ap[l].rearrange("l d -> d l"), pe_d.buf), eng="pool", chan="w1", noncontig=True)
                P.dma(w2, V(w2_d.ap[l], w2_d.buf), eng="pool", chan="w1")
                for g in range(2):
                    rows = slice(64 * g, 64 * g + 64)
                    pbias = ps("X")
                    for l_ in range(32):
                        P.mm(pbias[:, 0:1], w1[rows, l_, :], peT[rows, l_:l_ + 1], start=(l_ == 0), stop=(l_ == 31))
                    P.copy(bias[:, g:g + 1], pbias[:, 0:1], eng="dve")
                    ph = ps("X")
                    x3 = V(xin.ap[rows].rearrange("p (i s) -> p i s", s=16), xin.buf)
                    for l_ in range(32):
                        P.mm(ph[:, 0:255], w1[rows, l_, :], x3[:, l_ // 16:l_ // 16 + 255, l_ % 16], start=(l_ == 0), stop=(l_ == 31))
                    P.memset(hid[:, 255:256], 0.0, eng="dve")
                    P.act(hid[:, 0:255], ph[:, 0:255], AF.Silu, bias=bias[:, g:g + 1])
                    if which == "k":
                        pk = ps("X")
                        P.mm(pk[rows, 0:256], w2, hid)
                        evac(kcT[rows, :], pk[rows, 0:256])
                    else:
                        for t in range(2):
                            pv = ps("X")
                            P.mm(pv[:, 0:64], hid[:, t * 128:(t + 1) * 128], w2)
                            evac(vcaug[:, g * 2 + t, 64:128], pv[:, 0:64])
            if NSA_STOP <= 1:
                return
            wg = lw(l, C_NG, 24)
            P.memset(Gt, 0.0)
            sg = A.alloc([56, 512], F32, "sg")
            hi2 = A.alloc([56, 512], BF16, "hi2")
            for tc in range(8):
                pb = ps("X")
                for r0 in (0, 32):
                    for c in range(8):
                        P.mm(pb[r0:r0 + 24, :], wg[:, c, :], hT[:, c, tc * 512:(tc + 1) * 512], start=(c == 0), stop=(c == 7))
                    P.act(sg[r0:r0 + 24, :], pb[r0:r0 + 24, :], AF.Sigmoid)
                P.copy(Gt[0:24, tc * 512:(tc + 1) * 512], sg[0:24, :], eng="dve")
                P.copy(hi2[32:56, :], sg[32:56, :], eng="dve")
                P.tt(Gt[32:56, tc * 512:(tc + 1) * 512], sg[32:56, :], hi2[32:56, :], ALU.subtract)
            if NSA_STOP <= 2:
                return
            A.top = m1
            P.barrier()
            ksT = A.alloc([128, S], BF16, "ksT")
            kwT = A.alloc([128, S], BF16, "kwT")
            proj_fm(ksT, lw(l, C_NKS, 128), 0, 128)
            proj_fm(kwT, lw(l, C_NKW, 128), 0, 128)
            m2 = A.top
            ocaug = CB["caug"][0]
            otcmp = CB["tcmp"][0]
            otc = CB["tcaus"][0]
            otw = CB["twin"][0]
            oes = CB["eslc"][0]
            osel = CB["sel"][0]

            def pairs_cmp(qc):
                pl = []
                if qc >= 5:
                    pl.append((0, []))
                else:
                    d0 = 512 * qc
                    pl.append((0, [(ident_bf, cbf[:, otcmp + d0:otcmp + d0 + 512])]))
                if qc >= 4:
                    d1 = 512 * qc - 2048
                    pl.append((1, [(ident_bf, cbf[:, otcmp + d1:otcmp + d1 + 512])]))
                return pl

            for g in NSA_G:
                rows = slice(64 * g, 64 * g + 64)
                A.top = m2
                P.barrier()
                impT = A.alloc([64, S], F32, "impT")
                imp2 = A.alloc([128, 64], F32, "imp2")
                tmp2 = A.alloc([128, 64], F32, "tmp2")
                m8a = A.alloc([128, 8], F32, "m8a")
                m8b = A.alloc([128, 8], F32, "m8b")
                mvb = A.alloc([128, 8 * 128], BF16, "mvb")
                P.memset(mvb, 0.0)
                for p in range(4):
                    h = 4 * g + p
                    proj_fm(qT, lw(l, C_NQ + h * 64, 64), 64 * g, 64)
                    for qc in range(8):
                        O, _ = attn_chunk(qc, qT[rows], kcT[rows], pairs_cmp(qc),
                                          lambda kt: cbf[:, ocaug + kt * 65:ocaug + kt * 65 + 65], 65, 0.125)
                        Bs = bcast_rden(O, 64, guard=True)
                        sl = impT[0:64, qc * 512:(qc + 1) * 512]
                        if p == 0:
                            P.tt(sl, O[0:64, :], Bs[0:64, :], ALU.mult)
                        else:
                            tb = tmpb[qc % 2]
                            P.tt(tb[0:64, :], O[0:64, :], Bs[0:64, :], ALU.mult)
                            P.tt(sl, sl, tb[0:64, :], ALU.add, eng="pool")
                if NSA_STOP <= 3:
                    return
                ofb = CF["fbt"][0]
                for g8 in range(4):
                    for j in range(8):
                        qt = g8 * 8 + j
                        pt = ps("X")
                        P.tr(pt[:, 0:64], impT[0:64, qt * 128:(qt + 1) * 128], ident_f[0:64, 0:64])
                        P.tt(imp2, pt[:, 0:64], cff[:, ofb + 63 - 2 * qt:ofb + 127 - 2 * qt], ALU.add)
                        P.memset(imp2[:, 0:1], 1000.0, eng="dve")
                        a_, b_, c_, d_ = m8a.ap, imp2.ap, tmp2.ap, m8b.ap
                        P.add("dve", lambda e, a_=a_, b_=b_: e.max(a_, b_), reads=[imp2.buf], writes=[m8a.buf])
                        P.add("dve", lambda e, a_=a_, b_=b_, c_=c_: e.match_replace(c_, a_, b_, -1e30), reads=[imp2.buf, m8a.buf], writes=[tmp2.buf])
                        P.add("dve", lambda e, c_=c_, d_=d_: e.max(d_, c_), reads=[tmp2.buf], writes=[m8b.buf])
                        P.ts(mvb[:, j * 128 + 64 * g:j * 128 + 64 * g + 64], imp2, m8b[:, 7:8], -BIGM, ALU.is_lt, ALU.mult)
                    pt = ps("X").bc(BF16)
                    for j in range(8):
                        P.tr(pt[:, j * 128:(j + 1) * 128], mvb[:, j * 128:(j + 1) * 128], ident_bf)
                    evac(mslc[rows, g8 * 1024:(g8 + 1) * 1024], pt[rows, :])
                if NSA_STOP <= 4:
                    return
                P.barrier()
                A.top = m2
                vsa = A.alloc([128, 32, 129], BF16, "vsa")
                vwa = A.alloc([128, 32, 129], BF16, "vwa")
                for t_ in (vsa, vwa):
                    P.memset(t_[:, :, 0:1], 1.0)
                    P.memset(t_[:, :, 1:64], 0.0)
                    P.memset(t_[:, :, 128:129], 1.0)
                for cbase, dstv in ((C_NVS, vsa), (C_NVW, vwa)):
                    wv = lw(l, cbase + 64 * g, 64)

                    def sinkv(tt, pb, dstv=dstv):
                        evac(dstv[:, tt, 64:128], pb[:, 0:64])
                    proj_tm(wv, 64, sinkv)
                if NSA_STOP <= 5:
                    return
                for p in range(4):
                    if NSA_STOP <= 9 and p >= NSA_STOP - 5:
                        return
                    h = 4 * g + p
                    par = p % 2
                    orow = slice(64 * par, 64 * par + 64)
                    proj_fm(qT, lw(l, C_NQ + h * 64, 64), 64 * g, 64)
                    if par == 0:
                        c0, c1, M, dr = 64, 129, 65, 64
                    else:
                        c0, c1, M, dr = 0, 128, 128, 0
                    for qc in range(8):
                        ac = acc[qc % 2]
                        for j in range(3):
                            if j not in NSA_BR:
                                continue
                            if j == 0:
                                pl = pairs_cmp(qc)
                                kTj = kcT
                                vfn = lambda kt: vcaug[:, g * 2 + kt, c0:c1]
                            elif j == 1:
                                pl = []
                                for kt in range(4 * qc + 4):
                                    masks = [(cbf[rows, oes + kt * 128:oes + (kt + 1) * 128], mslc[rows, qc * 512:(qc + 1) * 512])]
                                    if kt >= 4 * qc:
                                        delta = 128 * kt - 512 * qc
                                        masks.append((ident_bf, cbf[:, otc + 384 - delta:otc + 384 - delta + 512]))
                                    pl.append((kt, masks))
                                kTj = ksT
                                vfn = lambda kt: vsa[:, kt, c0:c1]
                            else:
                                pl = []
                                for kt in range(max(0, 4 * qc - 4), 4 * qc + 4):
                                    delta = 128 * kt - 512 * qc
                                    pl.append((kt, [(ident_bf, cbf[:, otw + 384 - delta:otw + 384 - delta + 512])]))
                                kTj = kwT
                                vfn = lambda kt: vwa[:, kt, c0:c1]
                            O, _ = attn_chunk(qc, qT[rows], kTj[rows], pl, vfn, M, 0.125)
                            idx = h * 3 + j
                            Gb = None
                            if not NSA_NOGATE:
                                Gb = ps("X")
                                P.mm(Gb, cbf[0:64, osel + idx * 128:osel + (idx + 1) * 128], Gt[0:64, qc * 512:(qc + 1) * 512])
                            Bs = bcast_rden(O, dr, gate_ps=Gb, guard=(j == 0))
                            if j == 0:
                                P.tt(ac[orow, :], O[orow, :], Bs[orow, :], ALU.mult)
                            else:
                                tb = tmpb[j % 2]
                                P.tt(tb[orow, :], O[orow, :], Bs[orow, :], ALU.mult)
                                if j == 1:
                                    P.tt(ac[orow, :], ac[orow, :], tb[orow, :], ALU.add, eng="pool")
                                else:
                                    P.tt(oc[orow, qc * 512:(qc + 1) * 512], ac[orow, :], tb[orow, :], ALU.add, eng="pool")
                    if par == 1:
                        gate_store(l, 4 + h // 2, oc, 0)
                if NSA_STOP <= 10:
                    return

        def phase_ret(l):
            qfT = A.alloc([128, S], BF16, "qfT")
            qcT = A.alloc([128, S], BF16, "qcT")
            kfT = A.alloc([128, S], BF16, "kfT")
            kdtm = A.alloc([128, 32, 128], BF16, "kdtm")
            vtm = A.alloc([128, 32, 128], BF16, "rvtm")
            Rb = [A.alloc([128, 128], BF16, "Rb%d" % i) for i in range(8)]
            Rf = A.alloc([128, 128], F32, "Rf")
            cs = [A.alloc([128, 512], F32, "cs%d" % i) for i in range(2)]
            t12 = [A.alloc([128, 512], F32, "t12%d" % i) for i in range(2)]
            sm = [A.alloc([128, 512], BF16, "sm%d" % i) for i in range(2)]
            on = [A.alloc([128, 128], BF16, "on%d" % i) for i in range(2)]
            bst = A.alloc([128, 4 * 6], F32, "bst")
            bag = A.alloc([128, 4 * 2], F32, "bag")
            rsd = A.alloc([128, 8], F32, "rsd")
            cross3 = lambda pair: V(cff.ap[:, CF["cross"][0] + pair * 128:CF["cross"][0] + (pair + 1) * 128]
                                    .rearrange("p (o i) -> p o i", o=1).to_broadcast([128, 4, 128]), cff.buf)
            for pair in range(2):
                for which, cbase, dst in (("q", C_RQ, qfT), ("k", C_RK, kfT)):
                    wx, slx = load_w(l, cbase + pair * 128, 128)
                    wy, sly = load_w(l, cbase + pair * 128 + 32, 32, dstoff=0, neg=True)
                    load_w(l, cbase + pair * 128, 32, dstoff=32, slot=sly)
                    load_w(l, cbase + pair * 128 + 96, 32, dstoff=64, slot=sly, neg=True)
                    wy, _ = load_w(l, cbase + pair * 128 + 64, 32, dstoff=96, slot=sly)
                    for tc in range(8):
                        px = ps("X")
                        py = ps("X")
                        for c in range(8):
                            P.mm(px, wx[:, c, :], hT[:, c, tc * 512:(tc + 1) * 512], start=(c == 0), stop=(c == 7))
                        for c in range(8):
                            P.mm(py, wy[:, c, :], hT[:, c, tc * 512:(tc + 1) * 512], start=(c == 0), stop=(c == 7))
                        P.dma(cs[0], rc_d[:, tc * 512:(tc + 1) * 512], eng="sp", chan="cs0")
                        P.dma(cs[1], rs_d[:, tc * 512:(tc + 1) * 512], eng="sp", chan="cs1")
                        P.tt(t12[0], px, cs[0], ALU.mult)
                        P.tt(t12[1], py, cs[1], ALU.mult)
                        sl = dst[:, tc * 512:(tc + 1) * 512]
                        P.tt(sl, t12[0], t12[1], ALU.add, eng="pool")
                        if which == "q":
                            P.tt(qcT[:, tc * 512:(tc + 1) * 512].re("p (n i) -> p n i", i=128), sl.re("p (n i) -> p n i", i=128),
                                 cross3(pair), ALU.mult, eng="pool")
                for hh in range(2):
                    h = 2 * pair + hh
                    rows = slice(64 * hh, 64 * hh + 64)
                    dec = RET_G[h] ** 128.0
                    wv = lw(l, C_RV + h * 128, 128)

                    def sinkv(tt, pb):
                        evac(vtm[:, tt, :], pb[:, 0:128])
                    proj_tm(wv, 128, sinkv)
                    okd = CF["kdec"][0]
                    for n8 in range(4):
                        pt = ps("X").bc(BF16)
                        for j in range(8):
                            n = n8 * 8 + j
                            P.tr(pt[:, j * 64:(j + 1) * 64], kfT[rows, n * 128:(n + 1) * 128], cbf[rows, CB["ident"][0] + 64 * hh:CB["ident"][0] + 64 * hh + 64])
                        P.ts(kdtm[:, n8 * 8:(n8 + 1) * 8, 64 * hh:64 * hh + 64], pt[:, 0:512].re("p (n d) -> p n d", d=64),
                             cff[:, okd + h:okd + h + 1], None, ALU.mult)
                    oc = sh["oc"][h % 2]
                    P.memset(Rf[rows, :], 0.0, eng="dve")
                    P.memset(Rb[0][rows, :], 0.0, eng="dve")
                    oin = CF["intra"][0]
                    intra3 = V(cff.ap[:, oin + h * 128:oin + (h + 1) * 128].rearrange("p (o i) -> p o i", o=1).to_broadcast([128, 4, 128]), cff.buf)
                    for n4 in range(8):
                        pS = ps("S")
                        for k in range(4):
                            n = n4 * 4 + k
                            P.mm(pS[:, k * 128:(k + 1) * 128], kfT[rows, n * 128:(n + 1) * 128], qfT[rows, n * 128:(n + 1) * 128])
                        smt = sm[n4 % 2]
                        P.tt(smt.re("p (k i) -> p k i", i=128), pS.re("p (k i) -> p k i", i=128), intra3, ALU.mult)
                        pU = ps("X")
                        for k in range(4):
                            n = n4 * 4 + k
                            P.mm(pU[rows, k * 128:(k + 1) * 128], kdtm[:, n, 64 * hh:64 * hh + 64], vtm[:, n, :])
                        pO = ps("O")
                        for k in range(4):
                            n = n4 * 4 + k
                            Rcur = Rb[n % 8]
                            P.mm(pO[:, k * 128:(k + 1) * 128], smt[:, k * 128:(k + 1) * 128], vtm[:, n, :], start=True, stop=(n == 0))
                            if n > 0:
                                P.mm(pO[:, k * 128:(k + 1) * 128], qcT[rows, n * 128:(n + 1) * 128], Rcur[rows, :], start=False, stop=True)
                            if n < 31:
                                P.stt(Rf[rows, :], Rf[rows, :], dec, pU[rows, k * 128:(k + 1) * 128], ALU.mult, ALU.add)
                                P.copy(Rb[(n + 1) % 8][rows, :], Rf[rows, :], eng="act")
                        for k in range(4):
                            o_, i_ = bst.ap[:, k * 6:(k + 1) * 6], pO.ap[:, k * 128:(k + 1) * 128]
                            P.add("dve", lambda e, o_=o_, i_=i_: e.bn_stats(o_, i_), reads=[pO.buf], writes=[bst.buf])
                            o2_, i2_ = bag.ap[:, k * 2:(k + 1) * 2], bst.ap[:, k * 6:(k + 1) * 6]
                            P.add("dve", lambda e, o2_=o2_, i2_=i2_: e.bn_aggr(o2_, i2_), reads=[bst.buf], writes=[bag.buf])
                        bag3 = bag.re("p (k t) -> p k t", t=2)
                        P.ts(rsd[:, 0:4], bag3[:, :, 1], EPS, None, ALU.add)
                        P.tt(rsd[:, 4:8], rsd[:, 0:4], V(cff.ap[:, CF["neghalf"][0]:CF["neghalf"][0] + 1].to_broadcast([128, 4]), cff.buf), ALU.pow, eng="pool")
                        pt = ps("X").bc(BF16)
                        for k in range(4):
                            n = n4 * 4 + k
                            ont = on[k % 2]
                            P.ts(ont, pO[:, k * 128:(k + 1) * 128], bag[:, 2 * k:2 * k + 1], rsd[:, 4 + k:5 + k], ALU.subtract, ALU.mult)
                            P.tr(pt[:, k * 128:(k + 1) * 128], ont, ident_bf)
                        P.ts(oc[:, n4 * 512:(n4 + 1) * 512], pt[:, 0:512], retg[:, l * 4 + h:l * 4 + h + 1], None, ALU.mult)
                    gate_store(l, 8 + h, oc, h % 2)

        fns = {"mem": phase_mem, "moba": phase_moba, "nsa": phase_nsa, "ret": phase_ret}
        for l in range(nl):
            if l == 0:
                src_ap, src_bufs = x_d.ap, [x_d.buf] * 32
            else:
                src_ap, src_bufs = x1_ap, x1_bufs
            if l == nl - 1:
                dst_ap, dst_bufs = out_ap, out_bufs
            else:
                dst_ap, dst_bufs = x1_ap, x1_bufs
            P.barrier()
            A.top = mark0
            phase_norm_T(src_ap, src_bufs, S, gpre[:, l * 8:(l + 1) * 8], hT)
            for ph in phases:
                new_phase(n_oc=(1 if ph == "nsa" else 2))
                fns[ph](l)
            P.barrier()
            A.top = mark0
            phase_out(l, src_ap, src_bufs, dst_ap, dst_bufs)
        print("arena peak words", A.peak, "of", NW, "ops", {e: len(P.ops[e]) for e in ENGS}, "chans", len(P.chans), flush=True)
        P.emit(st)
    return nc


_CACHE = {}


def kernel(**inputs):
    n = 8
    if "nc" not in _CACHE:
        _CACHE["nc"] = build()
        _CACHE["consts"] = make_consts()
    nc = _CACHE["nc"]
    cbv, cfv, rc, rs = _CACHE["consts"]
    f = lambda a: np.ascontiguousarray(np.asarray(a, dtype=np.float32))
    shared = {k: f(inputs[k]) for k in ("pre_norm_g", "post_norm_g", "mem_norm_g", "w_in", "w_mem_kv", "nsa_pe_k", "nsa_w1_k",
                                        "nsa_w2_k", "nsa_pe_v", "nsa_w1_v", "nsa_w2_v", "ret_gn_g", "w_out")}
    shared.update(cb=cbv, cf=cfv, rc=rc, rs=rs)
    x = f(inputs["x"])
    mem = f(inputs["mem"])
    in_maps = []
    for i in range(n):
        m = dict(shared)
        m["x"] = np.ascontiguousarray(x[i])
        m["mem"] = np.ascontiguousarray(mem[i])
        in_maps.append(m)
    res = run_bass_kernel_spmd(nc, in_maps, core_ids=list(range(n)))
    return np.stack([np.asarray(r["out"], dtype=np.float32) for r in res.results], axis=0)
```

```python
import numpy as np
import ml_dtypes
import concourse.bass as bass
import concourse.mybir as mybir
from concourse.bass_utils import run_bass_kernel_spmd

F32 = mybir.dt.float32
BF16 = mybir.dt.bfloat16
AF = mybir.ActivationFunctionType
ALU = mybir.AluOpType
AX = mybir.AxisListType

S = 4096
D = 1024
NL = 2
INC = 6424
NEG = -30000.0
EPS = 1e-6

C_MQ, C_MK, C_MV = 0, 512, 1024
C_NQ = 1536
C_NKC, C_NVC, C_NKS, C_NVS, C_NKW, C_NVW = 2048, 2176, 2304, 2432, 2560, 2688
C_NG = 2816
C_RQ, C_RK, C_RV = 2840, 3096, 3352
C_CQ = 3864
C_Z = 4376


class Buf:
    __slots__ = ("name", "w", "r")

    def __init__(self, name=""):
        self.name = name
        self.w = None
        self.r = []


class V:
    __slots__ = ("ap", "buf")

    def __init__(self, ap, buf):
        self.ap = ap
        self.buf = buf

    def __getitem__(self, k):
        return V(self.ap[k], self.buf)

    def re(self, s, **kw):
        return V(self.ap.rearrange(s, **kw), self.buf)

    def bc(self, dt):
        return V(self.ap.bitcast(dt), self.buf)


class Op:
    __slots__ = ("eng", "fn", "deps", "signal", "sem", "val", "is_dma", "chan", "gidx")

    def __init__(self, eng, fn, is_dma=False, chan=None):
        self.eng = eng
        self.fn = fn
        self.deps = []
        self.signal = False
        self.sem = None
        self.val = 0
        self.is_dma = is_dma
        self.chan = chan


ENGS = ("pe", "act", "dve", "pool", "sp")


class Prog:
    def __init__(self, nc):
        self.nc = nc
        self.ops = {e: [] for e in ENGS}
        self.all = []
        self.chans = {}
        self.chan_last = {}
        self.bar = None
        self.bar_pending = set()

    def add(self, eng, fn, reads=(), writes=(), is_dma=False, chan=None):
        op = Op(eng, fn, is_dma, chan)
        deps = []
        for b in reads:
            if b.w is not None:
                deps.append((b.w, "raw"))
        for b in writes:
            if b.w is not None:
                deps.append((b.w, "waw"))
            for r in b.r:
                deps.append((r, "war"))
        if eng in self.bar_pending:
            for d in self.bar:
                deps.append((d, "bar"))
            self.bar_pending.discard(eng)
        seen = set()
        for d, kind in deps:
            if d is op or id(d) in seen:
                continue
            if not self._needs(d, op, kind):
                continue
            seen.add(id(d))
            d.signal = True
            op.deps.append(d)
        for b in reads:
            b.r.append(op)
        for b in writes:
            b.w = op
            b.r = []
        op.gidx = len(self.all)
        self.all.append(op)
        self.ops[eng].append(op)
        return op

    @staticmethod
    def _needs(d, op, kind):
        if d.is_dma:
            return True
        if op.is_dma:
            return True
        if d.eng != op.eng:
            return True
        if d.eng == "pe":
            return False
        return True

    def barrier(self):
        last = []
        for e in ENGS:
            seen_compute = False
            for op in reversed(self.ops[e]):
                if op.is_dma:
                    last.append(op)
                elif not seen_compute:
                    last.append(op)
                    seen_compute = True
                if len(last) > 4000:
                    break
        per = {}
        out = []
        for op in last:
            if op.is_dma:
                if op.chan not in per:
                    per[op.chan] = op
                    out.append(op)
            else:
                out.append(op)
        self.bar = out
        self.bar_pending = set(ENGS)

    def mm(self, out, lhsT, rhs, start=True, stop=True, extra_reads=()):
        o, l, r = out.ap, lhsT.ap, rhs.ap
        return self.add("pe", lambda e: e.matmul(o, l, r, start=start, stop=stop),
                        reads=[lhsT.buf, rhs.buf] + list(extra_reads), writes=[out.buf])

    def tr(self, out, in_, ident):
        o, i, d = out.ap, in_.ap, ident.ap
        return self.add("pe", lambda e: e.transpose(o, i, d), reads=[in_.buf, ident.buf], writes=[out.buf])

    def act(self, out, in_, func, scale=1.0, bias=0.0, accum=None, eng="act"):
        o, i = out.ap, in_.ap
        reads = [in_.buf]
        writes = [out.buf]
        b = bias
        if isinstance(bias, V):
            reads.append(bias.buf)
            b = bias.ap
        sc = scale
        if isinstance(scale, V):
            reads.append(scale.buf)
            sc = scale.ap
        acc = None
        if accum is not None:
            writes.append(accum.buf)
            acc = accum.ap
        if acc is None:
            fn = lambda e: e.activation(o, i, func, bias=b, scale=sc)
        else:
            fn = lambda e: e.activation(o, i, func, bias=b, scale=sc, accum_out=acc)
        return self.add("act", fn, reads=reads, writes=writes)

    def ts(self, out, in0, s1, s2, op0, op1=None, eng="dve", accum=None):
        o, i = out.ap, in0.ap
        reads = [in0.buf]
        a1, a2 = s1, s2
        if isinstance(s1, V):
            reads.append(s1.buf)
            a1 = s1.ap
        if isinstance(s2, V):
            reads.append(s2.buf)
            a2 = s2.ap
        writes = [out.buf]
        kw = {}
        if op1 is not None:
            kw["op1"] = op1
        if accum is not None:
            kw["accum_out"] = accum.ap
            writes.append(accum.buf)
        return self.add(eng, lambda e: e.tensor_scalar(o, i, a1, a2, op0, **kw), reads=reads, writes=writes)

    def tt(self, out, in0, in1, op, eng="dve"):
        o, a, b = out.ap, in0.ap, in1.ap
        return self.add(eng, lambda e: e.tensor_tensor(o, a, b, op), reads=[in0.buf, in1.buf], writes=[out.buf])

    def stt(self, out, in0, scalar, in1, op0, op1, eng="dve"):
        o, a, b = out.ap, in0.ap, in1.ap
        reads = [in0.buf, in1.buf]
        s = scalar
        if isinstance(scalar, V):
            reads.append(scalar.buf)
            s = scalar.ap
        return self.add(eng, lambda e: e.scalar_tensor_tensor(o, a, s, b, op0, op1), reads=reads, writes=[out.buf])

    def copy(self, out, in_, eng="dve"):
        o, i = out.ap, in_.ap
        if eng == "act":
            return self.add("act", lambda e: e.copy(o, i), reads=[in_.buf], writes=[out.buf])
        return self.add(eng, lambda e: e.tensor_copy(o, i), reads=[in_.buf], writes=[out.buf])

    def recip(self, out, in_):
        o, i = out.ap, in_.ap
        return self.add("dve", lambda e: e.reciprocal(o, i), reads=[in_.buf], writes=[out.buf])

    def memset(self, out, val, eng="pool"):
        o = out.ap
        return self.add(eng, lambda e: e.memset(o, val), writes=[out.buf])

    def dma(self, out, in_, eng="sp", chan=None, noncontig=False, extra_reads=()):
        o, i = out.ap, in_.ap
        if chan is None:
            chan = "c_" + str(id(out.buf))
        if chan not in self.chans:
            self.chans[chan] = [None, 0]
            self.chan_last[chan] = None
        if noncontig:
            fn = lambda e: e.dma_start(out=o, in_=i, allow_slow_non_contiguous=True)
        else:
            fn = lambda e: e.dma_start(out=o, in_=i)
        op = self.add(eng, fn, reads=[in_.buf] + list(extra_reads), writes=[out.buf], is_dma=True, chan=chan)
        self.chans[chan][1] += 16
        op.val = self.chans[chan][1]
        prev = self.chan_last.get(chan)
        if prev is not None and prev not in op.deps:
            prev.signal = True
            op.deps.append(prev)
        self.chan_last[chan] = op
        return op

    def emit(self, stack):
        nc = self.nc
        esem = {}
        for e in ENGS:
            esem[e] = stack.enter_context(nc.semaphore("s_" + e))
        for c in self.chans:
            self.chans[c][0] = stack.enter_context(nc.semaphore("d%d" % len(esem)))
            esem["chan_" + c] = self.chans[c][0]
        for e in ENGS:
            cnt = 0
            for op in self.ops[e]:
                if op.is_dma:
                    op.sem = self.chans[op.chan][0]
                    op.signal = True
                else:
                    op.sem = esem[e]
                    if op.signal:
                        cnt += 1
                        op.val = cnt
        print("sem max", {e: max([op.val for op in self.ops[e] if not op.is_dma] + [0]) for e in ENGS},
              "chan max", max(v[1] for v in self.chans.values()), flush=True)
        final_waits = []
        for c, (sem, val) in self.chans.items():
            final_waits.append((sem, val))
        self.final_waits = final_waits
        prog = self

        def run(engine, ename):
            seen = {}
            nwait = 0
            for op in prog.ops[ename]:
                for d in op.deps:
                    k = id(d.sem)
                    if seen.get(k, 0) >= d.val:
                        continue
                    seen[k] = d.val
                    engine.wait_ge(d.sem, d.val)
                    nwait += 1
                ins = op.fn(engine)
                if op.signal:
                    ins.then_inc(op.sem, 16 if op.is_dma else 1)
            if ename == "sp":
                for sem, val in prog.final_waits:
                    if seen.get(id(sem), 0) < val:
                        engine.wait_ge(sem, val)

        with nc.Block() as block:
            @block.tensor
            def _(e):
                run(e, "pe")

            @block.scalar
            def _(e):
                run(e, "act")

            @block.vector
            def _(e):
                run(e, "dve")

            @block.gpsimd
            def _(e):
                run(e, "pool")

            @block.sync
            def _(e):
                run(e, "sp")


BIGM = 30000.0
NSA_STOP = 99
NSA_G = (0, 1)
NSA_BR = (0, 1, 2)
NSA_NOGATE = False
RET_G = [1.0 - 2.0 ** (-5.0 - h) for h in range(4)]


def _layout(items):
    off = 0
    d = {}
    for name, w in items:
        d[name] = (off, w)
        off += w
    return d, off


CB_ITEMS = [("ident", 128), ("ones", 128), ("tcaus", 896), ("twin", 1408), ("tcmp", 2560),
            ("eb16", 2048), ("eslc", 4096), ("sel", 3072), ("caug", 130)]
CB, NCB = _layout(CB_ITEMS)
CF_ITEMS = [("ident", 128), ("ones", 128), ("pastm", 512), ("ownm2", 512), ("fbt", 128),
            ("intra", 512), ("cross", 256), ("kdec", 4), ("neghalf", 1), ("pad", 3)]
CF, NCF = _layout(CF_ITEMS)


def make_consts():
    cb = np.zeros((128, NCB), np.float32)
    cf = np.zeros((128, NCF), np.float32)

    def put(arr, lay, name, val):
        o, w = lay[name]
        assert val.shape[1] == w, (name, val.shape, w)
        arr[: val.shape[0], o:o + w] = val

    p = np.arange(128)[:, None]
    put(cb, CB, "ident", np.eye(128, dtype=np.float32))
    put(cb, CB, "ones", np.ones((128, 128), np.float32))
    j = np.arange(896)[None, :]
    put(cb, CB, "tcaus", np.where((j - 384) >= p, 0.0, NEG).astype(np.float32))
    j = np.arange(1408)[None, :]
    dd = (j - 384) - p
    put(cb, CB, "twin", np.where((dd >= 0) & (dd < 512), 0.0, NEG).astype(np.float32))
    j = np.arange(2560)[None, :]
    put(cb, CB, "tcmp", np.where(16 * p + 31 <= j, 0.0, NEG).astype(np.float32))
    e = np.zeros((16, 2048), np.float32)
    for b in range(16):
        e[b, b * 128:(b + 1) * 128] = 1.0
    put(cb, CB, "eb16", e)
    e = np.zeros((128, 4096), np.float32)
    for b in range(64):
        e[b, b * 64:(b + 1) * 64] = 1.0
        e[64 + b, b * 64:(b + 1) * 64] = 1.0
    put(cb, CB, "eslc", e)
    e = np.zeros((56, 3072), np.float32)
    for r in range(24):
        e[r, r * 128:(r + 1) * 128] = 1.0
        e[32 + r, r * 128:(r + 1) * 128] = 1.0
    put(cb, CB, "sel", e)
    n_cmp, n_slc = 255, 64
    mat = np.zeros((256, 64), np.float32)
    for jj in range(n_slc):
        for a in range(4):
            for b in range(2):
                i = 4 * jj + a - b
                if 0 <= i < n_cmp:
                    mat[i, jj] += 1.0
    ca = np.zeros((128, 130), np.float32)
    for t in range(2):
        ca[:, t * 65:t * 65 + 64] = mat[t * 128:(t + 1) * 128]
        ca[:, t * 65 + 64] = 1.0
    put(cb, CB, "caug", ca)

    put(cf, CF, "ident", np.eye(128, dtype=np.float32))
    put(cf, CF, "ones", np.ones((128, 128), np.float32))
    pm = np.zeros((32, 16), np.float32)
    om = np.zeros((32, 16), np.float32)
    for qt in range(32):
        own = qt // 2
        for b in range(16):
            pm[qt, b] = 0.0 if b < own else -1e30
            om[qt, b] = (-BIGM if b < own else (0.0 if b == own else -2 * BIGM))
    put(cf, CF, "pastm", np.broadcast_to(pm.reshape(1, 512), (128, 512)))
    put(cf, CF, "ownm2", np.broadcast_to(om.reshape(1, 512), (128, 512)))
    fb = np.zeros((128, 128), np.float32)
    for ql in range(128):
        orel = ql // 64
        for jx in range(127):
            m = jx - 63
            if m > orel:
                fb[ql, jx] = -1000.0
            elif m == orel or m == orel - 1:
                fb[ql, jx] = 1000.0
    put(cf, CF, "fbt", fb)
    it = np.zeros((128, 512), np.float32)
    jj = np.arange(128)[:, None]
    ii = np.arange(128)[None, :]
    for h in range(4):
        g = RET_G[h]
        it[:, h * 128:(h + 1) * 128] = np.where(ii >= jj, g ** np.maximum(ii - jj, 0), 0.0) * 0.125
    put(cf, CF, "intra", it)
    cr = np.zeros((128, 256), np.float32)
    for pair in range(2):
        for half in range(2):
            g = RET_G[2 * pair + half]
            cr[half * 64:(half + 1) * 64, pair * 128:(pair + 1) * 128] = (g ** (np.arange(128) + 1.0))[None, :]
    put(cf, CF, "cross", cr)
    kd = np.zeros((128, 4), np.float32)
    for h in range(4):
        kd[:, h] = RET_G[h] ** (127.0 - np.arange(128)) * 0.125
    put(cf, CF, "kdec", kd)
    put(cf, CF, "neghalf", np.full((128, 1), -0.5, np.float32))
    inv_freq = (1.0 / (10000.0 ** np.linspace(0.0, 1.0, 32))).astype(np.float32)
    ang = np.arange(S, dtype=np.float32)[:, None] * inv_freq[None, :]
    cos = np.cos(ang).astype(np.float32).T
    sin = np.sin(ang).astype(np.float32).T
    c2 = np.concatenate([cos, cos, cos, cos], 0)
    s2 = np.concatenate([sin, sin, sin, sin], 0)
    return cb, cf, np.ascontiguousarray(c2), np.ascontiguousarray(s2)


class Arena:
    def __init__(self, ap32, nwords):
        self.ap = ap32
        self.n = nwords
        self.top = 0
        self.peak = 0

    def alloc(self, shape, dt, name=""):
        free = 1
        for s in shape[1:]:
            free *= s
        words = free if dt == F32 else (free + 1) // 2
        a = self.top
        self.top += words
        self.peak = max(self.peak, self.top)
        assert self.top <= self.n, ("arena overflow", name, self.top, self.n)
        ap = self.ap[:, a:a + words]
        if dt != F32:
            ap = ap.bitcast(dt)[:, 0:free]
        if len(shape) == 3:
            ap = ap.rearrange("p (a b) -> p a b", a=shape[1])
        ap = ap[0:shape[0]]
        return V(ap, Buf(name))


def build(nl=NL, dbg=False, phases=("mem", "moba", "nsa", "ret")):
    from contextlib import ExitStack
    nc = bass.Bass("TRN2", target_bir_lowering=False)

    def din(name, shape, dt=F32):
        return V(nc.dram_tensor(name, shape, dt, kind="ExternalInput").ap(), Buf(name))

    x_d = din("x", [S, D])
    mem_d = din("mem", [256, D])
    pre_g_d = din("pre_norm_g", [NL, D])
    post_g_d = din("post_norm_g", [NL, D])
    mem_g_d = din("mem_norm_g", [NL, D])
    w_in_d = din("w_in", [NL, D, INC])
    w_mem_d = din("w_mem_kv", [NL, D, 1024])
    pe_k_d = din("nsa_pe_k", [NL, 32, 64])
    w1_k_d = din("nsa_w1_k", [NL, 2048, 128])
    w2_k_d = din("nsa_w2_k", [NL, 128, 64])
    pe_v_d = din("nsa_pe_v", [NL, 32, 64])
    w1_v_d = din("nsa_w1_v", [NL, 2048, 128])
    w2_v_d = din("nsa_w2_v", [NL, 128, 64])
    retg_d = din("ret_gn_g", [NL, 512])
    w_out_d = din("w_out", [NL, 2048, D])
    cb_d = din("cb", [128, NCB])
    cf_d = din("cf", [128, NCF])
    rc_d = din("rc", [128, S])
    rs_d = din("rs", [128, S])
    out_ap = nc.dram_tensor("out", [S, D], F32, kind="ExternalOutput").ap()
    x1_ap = nc.dram_tensor("x1s", [S, D], F32, kind="Internal").ap()
    oT_kind = "ExternalOutput" if dbg else "Internal"
    oT_ap = nc.dram_tensor("oTs", [2048, S], BF16, kind=oT_kind).ap()
    oT_bufs = [Buf("oT%d" % i) for i in range(16)]
    x1_bufs = [Buf("x1_%d" % i) for i in range(32)]
    out_bufs = [Buf("out_%d" % i) for i in range(32)]

    P = Prog(nc)
    st = ExitStack()
    with st:
        NW = 52000
        arena_t = st.enter_context(nc.sbuf_tensor("arena", [128, NW], F32))
        A = Arena(arena_t[:, :], NW)
        banks = []
        for i in range(8):
            pt = st.enter_context(nc.psum_tensor("psb%d" % i, [128, 512], F32))
            banks.append(V(pt[:, :], Buf("ps%d" % i)))
        grp = {"S": [0, 1, 2], "O": [3, 4], "X": [5, 6, 7]}
        gctr = {"S": 0, "O": 0, "X": 0}

        def ps(g):
            b = banks[grp[g][gctr[g] % len(grp[g])]]
            gctr[g] += 1
            return b

        cbf = A.alloc([128, NCB], BF16, "cbf")
        cff = A.alloc([128, NCF], F32, "cff")
        hT = A.alloc([128, 8, S], BF16, "hT")
        gpre = A.alloc([128, 8 * NL], F32, "gpre")
        gmem = A.alloc([128, 8 * NL], F32, "gmem")
        retg = A.alloc([128, 4 * NL], F32, "retg")

        def cb(name, rows=128):
            o, w = CB[name]
            return cbf[0:rows, o:o + w]

        def cf(name, rows=128):
            o, w = CF[name]
            return cff[0:rows, o:o + w]

        for k in range(0, NCB, 2048):
            e = min(NCB, k + 2048)
            P.dma(cbf[:, k:e], cb_d[:, k:e], eng="pool", chan="const")
        P.dma(cff, cf_d, eng="sp", chan="const2")
        for l in range(NL):
            P.dma(gpre[:, l * 8:(l + 1) * 8], V(pre_g_d.ap[l].rearrange("(c p) -> p c", p=128), pre_g_d.buf), eng="sp", chan="const2", noncontig=True)
            P.dma(gmem[:, l * 8:(l + 1) * 8], V(mem_g_d.ap[l].rearrange("(c p) -> p c", p=128), mem_g_d.buf), eng="sp", chan="const2", noncontig=True)
            P.dma(retg[:, l * 4:(l + 1) * 4], V(retg_d.ap[l].rearrange("(h v) -> v h", v=128), retg_d.buf), eng="sp", chan="const2", noncontig=True)
        ident_bf = cb("ident")
        ones_bf = cb("ones")
        ident_f = cf("ident")
        ones_f = cf("ones")
        mark0 = A.top

        ev_ctr = [0]

        def evac(dst, src, scale=None):
            ev_ctr[0] += 1
            if ev_ctr[0] % 2 == 0:
                P.act(dst, src, AF.Copy, scale=(1.0 if scale is None else scale))
            else:
                if scale is None:
                    P.copy(dst, src, eng="dve")
                else:
                    P.ts(dst, src, scale, None, ALU.mult)

        sh = {}

        def new_phase(n_oc=2, n_w=4):
            P.barrier()
            A.top = mark0
            sh["w"] = [A.alloc([128, 8, 128], BF16, "w%d" % i) for i in range(n_w)]
            sh["wc"] = 0
            sh["p"] = [A.alloc([128, 512], BF16, "pb%d" % i) for i in range(4)]
            sh["pc"] = 0
            sh["rr"] = [A.alloc([128, 512], F32, "rr%d" % i) for i in range(2)]
            sh["bs"] = [A.alloc([128, 512], F32, "bs%d" % i) for i in range(2)]
            sh["fc"] = 0
            sh["sz"] = [A.alloc([128, 512], BF16, "sz%d" % i) for i in range(2)]
            sh["szc"] = 0
            sh["oc"] = [A.alloc([128, S], BF16, "oc%d" % i) for i in range(n_oc)]

        def load_w(l, col0, n, src=None, dstoff=0, slot=None, neg=False):
            if slot is None:
                k = sh["wc"] % len(sh["w"])
                slot = (sh["w"][k], k)
                sh["wc"] += 1
            sl, k = slot
            srcv = (w_in_d if src is None else src)
            s3 = V(srcv.ap[l].rearrange("(c p) n -> p c n", p=128)[:, :, col0:col0 + n], srcv.buf)
            P.dma(sl[:, :, dstoff:dstoff + n], s3, eng="pool", chan="w%d" % k)
            if neg:
                P.ts(sl[:, :, dstoff:dstoff + n], sl[:, :, dstoff:dstoff + n], -1.0, None, ALU.mult, eng="pool")
            return sl[:, :, 0:dstoff + n], slot

        def lw(l, col0, n, src=None):
            return load_w(l, col0, n, src=src)[0]

        def proj_fm(dst, w, prow, M, sink=None):
            for tc in range(8):
                pb = ps("X")
                for c in range(8):
                    P.mm(pb[prow:prow + M, :], w[:, c, :], hT[:, c, tc * 512:(tc + 1) * 512], start=(c == 0), stop=(c == 7))
                if sink is None:
                    evac(dst[prow:prow + M, tc * 512:(tc + 1) * 512], pb[prow:prow + M, :])
                else:
                    sink(tc, pb)

        def proj_tm(w, n, sink, src=None, ntile=32):
            srcT = hT if src is None else src
            for tt in range(ntile):
                pb = ps("X")
                for c in range(8):
                    P.mm(pb[:, 0:n], srcT[:, c, tt * 128:(tt + 1) * 128], w[:, c, :], start=(c == 0), stop=(c == 7))
                sink(tt, pb)

        def attn_chunk(qc, qT, kT, pl, vfn, M, scale, extra=None):
            n = len(pl)
            O = ps("O")
            O2 = ps("O") if extra is not None else None
            pbs = {}

            def do_s(i):
                kt, masks = pl[i]
                Sp = ps("S")
                P.mm(Sp, kT[:, kt * 128:(kt + 1) * 128], qT[:, qc * 512:(qc + 1) * 512], start=True, stop=(len(masks) == 0))
                for mi, (ml, mr) in enumerate(masks):
                    P.mm(Sp, ml, mr, start=False, stop=(mi == len(masks) - 1))
                Pb = sh["p"][sh["pc"] % 4]
                sh["pc"] += 1
                P.act(Pb, Sp, AF.Exp, scale=scale)
                return Pb

            LOOK = 2
            for i in range(min(LOOK, n)):
                pbs[i] = do_s(i)
            for i in range(n):
                if i + LOOK < n:
                    pbs[i + LOOK] = do_s(i + LOOK)
                pb = pbs.pop(i)
                P.mm(O[0:M, :], vfn(pl[i][0]), pb, start=(i == 0), stop=(i == n - 1))
                if extra is not None:
                    em, efn = extra
                    P.mm(O2[0:em, :], efn(pl[i][0]), pb, start=(i == 0), stop=(i == n - 1))
            return O, O2

        def bcast_rden(O, dr, gate_ps=None, guard=False):
            k = sh["fc"]
            sh["fc"] += 1
            r1 = sh["rr"][k % 2]
            P.copy(r1[dr:dr + 1, :], O[dr:dr + 1, :], eng="act")
            B = ps("X")
            P.mm(B, ones_f[dr:dr + 1, :], r1[dr:dr + 1, :])
            Bs = sh["bs"][k % 2]
            if guard:
                P.ts(Bs, B, 1e-30, None, ALU.max)
                P.recip(Bs, Bs)
            else:
                P.recip(Bs, B)
            if gate_ps is not None:
                P.tt(Bs, Bs, gate_ps, ALU.mult)
            return Bs

        def gate_store(l, ci, ocT, slot_id):
            w = lw(l, C_Z + ci * 128, 128)

            def sink(tc, pb):
                sz = sh["sz"][sh["szc"] % 2]
                sh["szc"] += 1
                P.act(sz, pb, AF.Silu)
                sl = ocT[:, tc * 512:(tc + 1) * 512]
                P.tt(sl, sl, sz, ALU.mult, eng="pool")
            proj_fm(None, w, 0, 128, sink=sink)
            P.dma(V(oT_ap[ci * 128:(ci + 1) * 128, :], oT_bufs[ci]), ocT, eng="sp", chan="oc%d" % slot_id)

        def phase_norm_T(src_ap, src_bufs, ntok, g_sb, dstT):
            xsl = [A.alloc([128, D], F32, "xs%d" % i) for i in range(3)]
            xnl = [A.alloc([128, D], BF16, "xn%d" % i) for i in range(2)]
            junk = A.alloc([128, D], BF16, "junk")
            stt_ = [A.alloc([128, 4], F32, "st%d" % i) for i in range(4)]
            g3 = V(g_sb.ap.rearrange("p (c o) -> p c o", o=1).to_broadcast([128, 8, 128]), g_sb.buf)
            for tt in range(ntok // 128):
                xs = xsl[tt % 3]
                P.dma(xs, V(src_ap[tt * 128:(tt + 1) * 128, :], src_bufs[tt]), eng="sp", chan="xs%d" % (tt % 3))
                stt = stt_[tt % 4]
                P.act(junk, xs, AF.Square, accum=stt[:, 0:1])
                P.ts(stt[:, 1:2], stt[:, 0:1], 1.0 / D, EPS, ALU.mult, ALU.add)
                P.tt(stt[:, 2:3], stt[:, 1:2], cf("neghalf"), ALU.pow, eng="pool")
                xn = xnl[tt % 2]
                P.ts(xn, xs, stt[:, 2:3], None, ALU.mult)
                pt = ps("X").bc(BF16)
                for c in range(8):
                    P.tr(pt[:, c * 128:(c + 1) * 128], xn[:, c * 128:(c + 1) * 128], ident_bf)
                P.tt(dstT[:, :, tt * 128:(tt + 1) * 128], pt.re("p (c t) -> p c t", c=8), g3, ALU.mult)

        def phase_out(l, src_ap, src_bufs, dst_ap, dst_bufs):
            wout = A.alloc([128, 16, D], BF16, "wout")
            w3 = V(w_out_d.ap[l].rearrange("(c p) n -> p c n", p=128), w_out_d.buf)
            for c0 in range(0, 16, 2):
                P.dma(wout[:, c0:c0 + 2, :], w3[:, c0:c0 + 2, :], eng="pool", chan="wout")
            gp = A.alloc([128, D], F32, "gpost")
            P.dma(gp, V(post_g_d.ap[l:l + 1, :].to_broadcast([128, D]), post_g_d.buf), eng="sp", chan="gpost")
            xsl = [A.alloc([128, D], F32, "xo%d" % i) for i in range(2)]
            otl = [A.alloc([128, 16, 128], BF16, "ot%d" % i) for i in range(2)]
            ynl = [A.alloc([128, D], F32, "yn%d" % i) for i in range(2)]
            junk = A.alloc([128, 512], BF16, "junk2")
            stt_ = [A.alloc([128, 4], F32, "sto%d" % i) for i in range(4)]
            oT3 = oT_ap.rearrange("(c p) t -> p c t", p=128)
            for tt in range(32):
                ot = otl[tt % 2]
                P.dma(ot, V(oT3[:, :, tt * 128:(tt + 1) * 128], oT_bufs[0]), eng="sp", chan="ot%d" % (tt % 2), extra_reads=oT_bufs[1:])
                xs = xsl[tt % 2]
                P.dma(xs, V(src_ap[tt * 128:(tt + 1) * 128, :], src_bufs[tt]), eng="sp", chan="xo%d" % (tt % 2))
                stt = stt_[tt % 4]
                pbs_ = []
                for half in range(2):
                    pb = ps("X")
                    pbs_.append(pb)
                    for c in range(16):
                        P.mm(pb, ot[:, c, :], wout[:, c, half * 512:(half + 1) * 512], start=(c == 0), stop=(c == 15))
                    P.act(junk, pb, AF.Square, accum=stt[:, half:half + 1])
                P.tt(stt[:, 2:3], stt[:, 0:1], stt[:, 1:2], ALU.add)
                P.ts(stt[:, 2:3], stt[:, 2:3], 1.0 / D, EPS, ALU.mult, ALU.add)
                P.tt(stt[:, 3:4], stt[:, 2:3], cf("neghalf"), ALU.pow, eng="pool")
                yn = ynl[tt % 2]
                for half in range(2):
                    hs = slice(half * 512, (half + 1) * 512)
                    P.stt(yn[:, hs], pbs_[half], stt[:, 3:4], gp[:, hs], ALU.mult, ALU.mult)
                P.tt(yn, yn, xs, ALU.add, eng="pool")
                P.dma(V(dst_ap[tt * 128:(tt + 1) * 128, :], dst_bufs[tt]), yn, eng="sp", chan="sto%d" % (tt % 2))

        def phase_mem(l):
            memT = A.alloc([128, 8, 256], BF16, "memT")
            m0 = A.top
            phase_norm_T(mem_d.ap, [mem_d.buf] * 2, 256, gmem[:, l * 8:(l + 1) * 8], memT)
            A.top = m0
            P.barrier()
            vtm = A.alloc([128, 2, 512], BF16, "memv")
            kTm = A.alloc([128, 256], BF16, "memk")
            qTm = A.alloc([128, S], BF16, "memq")
            rdn = [A.alloc([128, 512], F32, "rdn%d" % i) for i in range(2)]
            for j in range(4):
                w = lw(l, 512 + j * 128, 128, src=w_mem_d)

                def sinkv(tt, pb, j=j):
                    evac(vtm[:, tt, j * 128:(j + 1) * 128], pb[:, 0:128])
                proj_tm(w, 128, sinkv, src=memT, ntile=2)
            sc = 128.0 ** -0.5
            for h in range(4):
                wk = lw(l, h * 128, 128, src=w_mem_d)
                pb = ps("X")
                for c in range(8):
                    P.mm(pb[:, 0:256], wk[:, c, :], memT[:, c, :], start=(c == 0), stop=(c == 7))
                evac(kTm, pb[:, 0:256])
                wq = lw(l, C_CQ + h * 128, 128)
                proj_fm(qTm, wq, 0, 128)
                oc = sh["oc"][h % 2]
                for qc in range(8):
                    O, O2 = attn_chunk(qc, qTm, kTm, [(0, []), (1, [])], lambda kt, h=h: vtm[:, kt, h * 128:(h + 1) * 128], 128, sc,
                                       extra=(128, lambda kt: ones_bf))
                    r = rdn[qc % 2]
                    P.recip(r, O2)
                    P.tt(oc[:, qc * 512:(qc + 1) * 512], O, r, ALU.mult)
                gate_store(l, 12 + h, oc, h % 2)

        def phase_moba(l):
            qh = A.alloc([128, S], BF16, "mqh")
            kh = A.alloc([128, S], BF16, "mkh")
            vaug = A.alloc([128, 32, 193], BF16, "mvaug")
            kmf = A.alloc([128, 16], F32, "kmf")
            kmb = A.alloc([128, 16], BF16, "kmb")
            gsb = A.alloc([128, 512], F32, "gsb")
            m8 = A.alloc([128, 256], F32, "m8")
            a1 = A.alloc([128, 512], F32, "a1")
            mv = A.alloc([128, 512], BF16, "mv")
            mv64 = A.alloc([128, 2048], BF16, "mv64")
            mvp = A.alloc([128, 8 * 128], BF16, "mvp")
            P.memset(vaug[:, :, 64:66], 1.0)
            P.memset(vaug[:, :, 66:129], 0.0)
            P.memset(mvp, 0.0)
            otc = CB["tcaus"][0]
            oes = CB["eslc"][0]
            for pair in range(4):
                wv = lw(l, C_MV + pair * 128, 128)

                def sinkv(tt, pb):
                    evac(vaug[:, tt, 0:64], pb[:, 0:64])
                    evac(vaug[:, tt, 129:193], pb[:, 64:128])
                proj_tm(wv, 128, sinkv)
                oc = sh["oc"][pair % 2]
                for hh in range(2):
                    h = 2 * pair + hh
                    r0 = 64 * hh
                    rows = slice(r0, r0 + 64)
                    orows = slice(64 - r0, 128 - r0)
                    proj_fm(qh, lw(l, C_MQ + h * 64, 64), r0, 64)
                    proj_fm(kh, lw(l, C_MK + h * 64, 64), r0, 64)
                    P.copy(kh[orows, :], cbf[orows, oes:oes + S], eng="pool")
                    o_, i_ = kmf.ap[rows], kh.ap[rows].rearrange("p (b k) -> p b k", k=256)
                    P.add("dve", lambda e, o_=o_, i_=i_: e.tensor_reduce(o_, i_, AX.X, ALU.add), reads=[kh.buf], writes=[kmf.buf])
                    P.ts(kmb[rows], kmf[rows], 1.0 / 256, None, ALU.mult)
                    G = ps("X")
                    for qt in range(32):
                        P.mm(G[:, qt * 16:(qt + 1) * 16], qh[rows, qt * 128:(qt + 1) * 128], kmb[rows, :], start=True, stop=True)
                    P.tt(gsb, G, cf("pastm"), ALU.add)
                    for qt in range(32):
                        o_, i_ = m8.ap[:, qt * 8:(qt + 1) * 8], gsb.ap[:, qt * 16:(qt + 1) * 16]
                        P.add("dve", lambda e, o_=o_, i_=i_: e.max(o_, i_), reads=[gsb.buf], writes=[m8.buf])
                    thr = V(m8.ap.rearrange("p (q e) -> p q e", e=8)[:, :, 2:3].to_broadcast([128, 32, 16]), m8.buf)
                    P.tt(a1.re("p (q b) -> p q b", b=16), gsb.re("p (q b) -> p q b", b=16), thr, ALU.is_ge)
                    P.stt(a1, a1, BIGM, cf("ownm2"), ALU.mult, ALU.add)
                    P.ts(mv, a1, 0.0, None, ALU.min)
                    P.copy(mv64.re("p (c r) -> p c r", r=4),
                           V(mv.ap.rearrange("p (c o) -> p c o", o=1).to_broadcast([128, 512, 4]), mv.buf), eng="dve")
                    c0 = 64 - r0
                    for g4 in range(4):
                        P.copy(mvp.re("p (j c) -> p j c", c=128)[:, :, c0:c0 + 64],
                               mv64[:, g4 * 512:(g4 + 1) * 512].re("p (j c) -> p j c", c=64), eng="pool")
                        pt = ps("X").bc(BF16)
                        for j in range(8):
                            P.tr(pt[:, j * 128:(j + 1) * 128], mvp[:, j * 128:(j + 1) * 128], ident_bf)
                        evac(qh[orows, g4 * 1024:(g4 + 1) * 1024], pt[orows, :])
                    if hh == 0:
                        vfn = lambda kt: vaug[:, kt, 0:65]
                        M, dr = 65, 64
                    else:
                        vfn = lambda kt: vaug[:, kt, 65:193]
                        M, dr = 128, 0
                    for qc in range(8):
                        pl = []
                        for kt in range(4 * qc + 4):
                            masks = []
                            if kt >= 4 * qc:
                                delta = 128 * kt - 512 * qc
                                masks.append((ident_bf, cbf[:, otc + 384 - delta:otc + 384 - delta + 512]))
                            pl.append((kt, masks))
                        O, _ = attn_chunk(qc, qh, kh, pl, vfn, M, 0.125)
                        Bs = bcast_rden(O, dr)
                        P.tt(oc[rows, qc * 512:(qc + 1) * 512], O[rows, :], Bs[rows, :], ALU.mult)
                gate_store(l, pair, oc, pair % 2)

        def phase_nsa(l):
            oc = sh["oc"][0]
            kcT = A.alloc([128, 256], BF16, "kcT")
            vcaug = A.alloc([128, 4, 129], BF16, "vcaug")
            Gt = A.alloc([64, S], BF16, "Gt")
            qT = A.alloc([128, S], BF16, "nqT")
            acc = [A.alloc([128, 512], F32, "acc%d" % i) for i in range(2)]
            tmpb = [A.alloc([128, 512], F32, "tmpb%d" % i) for i in range(2)]
            m1 = A.top
            kin = A.alloc([128, S], BF16, "kin")
            vin = A.alloc([128, S], BF16, "vin")
            proj_fm(kin, lw(l, C_NKC, 128), 0, 128)
            proj_fm(vin, lw(l, C_NVC, 128), 0, 128)
            P.memset(vcaug[:, :, 0:1], 1.0)
            P.memset(vcaug[:, :, 1:64], 0.0)
            P.memset(vcaug[:, :, 128:129], 1.0)
            w1 = A.alloc([128, 32, 128], BF16, "w1")
            peT = A.alloc([128, 32], BF16, "peT")
            w2 = A.alloc([128, 64], BF16, "w2")
            bias = A.alloc([128, 2], F32, "cbias")
            hid = A.alloc([128, 256], BF16, "hid")
            for which, pe_d, w1_d, w2_d, xin in (("k", pe_k_d, w1_k_d, w2_k_d, kin), ("v", pe_v_d, w1_v_d, w2_v_d, vin)):
                s3 = V(w1_d.ap[l].rearrange("(l d) j -> d l j", d=64), w1_d.buf)
                for hlf in range(2):
                    for l0 in range(0, 32, 8):
                        P.dma(w1[hlf * 64:(hlf + 1) * 64, l0:l0 + 8, :], s3[:, l0:l0 + 8, :], eng="pool", chan="w1")
                    P.dma(peT[hlf * 64:(hlf + 1) * 64, :], V(pe_d.ap[l].rearrange("l d -> d l"), pe_d.buf), eng="pool", chan="w1", noncontig=True)
                P.dma(w2, V(w2_d.ap[l], w2_d.buf), eng="pool", chan="w1")
                for g in range(2):
                    rows = slice(64 * g, 64 * g + 64)
                    pbias = ps("X")
                    for l_ in range(32):
                        P.mm(pbias[:, 0:1], w1[rows, l_, :], peT[rows, l_:l_ + 1], start=(l_ == 0), stop=(l_ == 31))
                    P.copy(bias[:, g:g + 1], pbias[:, 0:1], eng="dve")
                    ph = ps("X")
                    x3 = V(xin.ap[rows].rearrange("p (i s) -> p i s", s=16), xin.buf)
                    for l_ in range(32):
                        P.mm(ph[:, 0:255], w1[rows, l_, :], x3[:, l_ // 16:l_ // 16 + 255, l_ % 16], start=(l_ == 0), stop=(l_ == 31))
                    P.memset(hid[:, 255:256], 0.0, eng="dve")
                    P.act(hid[:, 0:255], ph[:, 0:255], AF.Silu, bias=bias[:, g:g + 1])
                    if which == "k":
                        pk = ps("X")
                        P.mm(pk[rows, 0:256], w2, hid)
                        evac(kcT[rows, :], pk[rows, 0:256])
                    else:
                        for t in range(2):
                            pv = ps("X")
                            P.mm(pv[:, 0:64], hid[:, t * 128:(t + 1) * 128], w2)
                            evac(vcaug[:, g * 2 + t, 64:128], pv[:, 0:64])
            if NSA_STOP <= 1:
                return
            wg = lw(l, C_NG, 24)
            P.memset(Gt, 0.0)
            sg = A.alloc([56, 512], F32, "sg")
            hi2 = A.alloc([56, 512], BF16, "hi2")
            for tc in range(8):
                pb = ps("X")
                for r0 in (0, 32):
                    for c in range(8):
                        P.mm(pb[r0:r0 + 24, :], wg[:, c, :], hT[:, c, tc * 512:(tc + 1) * 512], start=(c == 0), stop=(c == 7))
                    P.act(sg[r0:r0 + 24, :], pb[r0:r0 + 24, :], AF.Sigmoid)
                P.copy(Gt[0:24, tc * 512:(tc + 1) * 512], sg[0:24, :], eng="dve")
                P.copy(hi2[32:56, :], sg[32:56, :], eng="dve")
                P.tt(Gt[32:56, tc * 512:(tc + 1) * 512], sg[32:56, :], hi2[32:56, :], ALU.subtract)
            if NSA_STOP <= 2:
                return
            A.top = m1
            P.barrier()
            ksT = A.alloc([128, S], BF16, "ksT")
            kwT = A.alloc([128, S], BF16, "kwT")
            proj_fm(kwT, lw(l, C_NKW, 128), 0, 128)
            m2 = A.top
            ocaug = CB["caug"][0]
            otcmp = CB["tcmp"][0]
            otc = CB["tcaus"][0]
            otw = CB["twin"][0]
            oes = CB["eslc"][0]
            osel = CB["sel"][0]

            def pairs_cmp(qc):
                pl = []
                if qc >= 5:
                    pl.append((0, []))
                else:
                    d0 = 512 * qc
                    pl.append((0, [(ident_bf, cbf[:, otcmp + d0:otcmp + d0 + 512])]))
                if qc >= 4:
                    d1 = 512 * qc - 2048
                    pl.append((1, [(ident_bf, cbf[:, otcmp + d1:otcmp + d1 + 512])]))
                return pl

            for g in NSA_G:
                rows = slice(64 * g, 64 * g + 64)
                orows_g = slice(64 - 64 * g, 128 - 64 * g)
                A.top = m2
                P.barrier()
                impT = A.alloc([64, S], F32, "impT")
                imp2 = A.alloc([128, 64], F32, "imp2")
                tmp2 = A.alloc([128, 64], F32, "tmp2")
                m8a = A.alloc([128, 8], F32, "m8a")
                m8b = A.alloc([128, 8], F32, "m8b")
                mvb = A.alloc([128, 8 * 128], BF16, "mvb")
                P.memset(mvb, 0.0)
                for p in range(4):
                    h = 4 * g + p
                    proj_fm(qT, lw(l, C_NQ + h * 64, 64), 64 * g, 64)
                    for qc in range(8):
                        O, _ = attn_chunk(qc, qT[rows], kcT[rows], pairs_cmp(qc),
                                          lambda kt: cbf[:, ocaug + kt * 65:ocaug + kt * 65 + 65], 65, 0.125)
                        Bs = bcast_rden(O, 64, guard=True)
                        sl = impT[0:64, qc * 512:(qc + 1) * 512]
                        if p == 0:
                            P.tt(sl, O[0:64, :], Bs[0:64, :], ALU.mult)
                        else:
                            tb = tmpb[qc % 2]
                            P.tt(tb[0:64, :], O[0:64, :], Bs[0:64, :], ALU.mult)
                            P.tt(sl, sl, tb[0:64, :], ALU.add, eng="pool")
                if NSA_STOP <= 3:
                    return
                ofb = CF["fbt"][0]
                for g8 in range(4):
                    for j in range(8):
                        qt = g8 * 8 + j
                        pt = ps("X")
                        P.tr(pt[:, 0:64], impT[0:64, qt * 128:(qt + 1) * 128], ident_f[0:64, 0:64])
                        P.tt(imp2, pt[:, 0:64], cff[:, ofb + 63 - 2 * qt:ofb + 127 - 2 * qt], ALU.add)
                        P.memset(imp2[:, 0:1], 1000.0, eng="dve")
                        a_, b_, c_, d_ = m8a.ap, imp2.ap, tmp2.ap, m8b.ap
                        P.add("dve", lambda e, a_=a_, b_=b_: e.max(a_, b_), reads=[imp2.buf], writes=[m8a.buf])
                        P.add("dve", lambda e, a_=a_, b_=b_, c_=c_: e.match_replace(c_, a_, b_, -1e30), reads=[imp2.buf, m8a.buf], writes=[tmp2.buf])
                        P.add("dve", lambda e, c_=c_, d_=d_: e.max(d_, c_), reads=[tmp2.buf], writes=[m8b.buf])
                        P.ts(mvb[:, j * 128 + 64 - 64 * g:j * 128 + 128 - 64 * g], imp2, m8b[:, 7:8], -BIGM, ALU.is_lt, ALU.mult)
                    pt = ps("X").bc(BF16)
                    for j in range(8):
                        P.tr(pt[:, j * 128:(j + 1) * 128], mvb[:, j * 128:(j + 1) * 128], ident_bf)
                    evac(qT[orows_g, g8 * 1024:(g8 + 1) * 1024], pt[orows_g, :])
                if NSA_STOP <= 4:
                    return
                P.barrier()
                A.top = m2
                vsa = A.alloc([128, 32, 129], BF16, "vsa")
                vwa = A.alloc([128, 32, 129], BF16, "vwa")
                for t_ in (vsa, vwa):
                    P.memset(t_[:, :, 0:1], 1.0)
                    P.memset(t_[:, :, 1:64], 0.0)
                    P.memset(t_[:, :, 128:129], 1.0)
                proj_fm(ksT, lw(l, C_NKS + 64 * g, 64), 64 * g, 64)
                P.copy(ksT[orows_g, :], cbf[orows_g, oes:oes + S], eng="pool")
                for cbase, dstv in ((C_NVS, vsa), (C_NVW, vwa)):
                    wv = lw(l, cbase + 64 * g, 64)

                    def sinkv(tt, pb, dstv=dstv):
                        evac(dstv[:, tt, 64:128], pb[:, 0:64])
                    proj_tm(wv, 64, sinkv)
                if NSA_STOP <= 5:
                    return
                for p in range(4):
                    if NSA_STOP <= 9 and p >= NSA_STOP - 5:
                        return
                    h = 4 * g + p
                    par = p % 2
                    orow = slice(64 * par, 64 * par + 64)
                    proj_fm(qT, lw(l, C_NQ + h * 64, 64), 64 * g, 64)
                    if par == 0:
                        c0, c1, M, dr = 64, 129, 65, 64
                    else:
                        c0, c1, M, dr = 0, 128, 128, 0
                    for qc in range(8):
                        ac = acc[qc % 2]
                        for j in range(3):
                            if j not in NSA_BR:
                                continue
                            if j == 0:
                                pl = pairs_cmp(qc)
                                kTj = kcT
                                vfn = lambda kt: vcaug[:, g * 2 + kt, c0:c1]
                            elif j == 1:
                                pl = []
                                for kt in range(4 * qc + 4):
                                    masks = []
                                    if kt >= 4 * qc:
                                        delta = 128 * kt - 512 * qc
                                        masks.append((ident_bf, cbf[:, otc + 384 - delta:otc + 384 - delta + 512]))
                                    pl.append((kt, masks))
                                kTj = ksT
                                vfn = lambda kt: vsa[:, kt, c0:c1]
                            else:
                                pl = []
                                for kt in range(max(0, 4 * qc - 4), 4 * qc + 4):
                                    delta = 128 * kt - 512 * qc
                                    pl.append((kt, [(ident_bf, cbf[:, otw + 384 - delta:otw + 384 - delta + 512])]))
                                kTj = kwT
                                vfn = lambda kt: vwa[:, kt, c0:c1]
                            if j == 1:
                                O, _ = attn_chunk(qc, qT, ksT, pl, vfn, M, 0.125)
                            else:
                                O, _ = attn_chunk(qc, qT[rows], kTj[rows], pl, vfn, M, 0.125)
                            idx = h * 3 + j
                            Gb = None
                            if not NSA_NOGATE:
                                Gb = ps("X")
                                P.mm(Gb, cbf[0:64, osel + idx * 128:osel + (idx + 1) * 128], Gt[0:64, qc * 512:(qc + 1) * 512])
                            Bs = bcast_rden(O, dr, gate_ps=Gb, guard=(j == 0))
                            if j == 0:
                                P.tt(ac[orow, :], O[orow, :], Bs[orow, :], ALU.mult)
                            else:
                                tb = tmpb[j % 2]
                                P.tt(tb[orow, :], O[orow, :], Bs[orow, :], ALU.mult)
                                if j == 1:
                                    P.tt(ac[orow, :], ac[orow, :], tb[orow, :], ALU.add, eng="pool")
                                else:
                                    P.tt(oc[orow, qc * 512:(qc + 1) * 512], ac[orow, :], tb[orow, :], ALU.add, eng="pool")
                    if par == 1:
                        gate_store(l, 4 + h // 2, oc, 0)
                if NSA_STOP <= 10:
                    return

        def phase_ret(l):
            qfT = A.alloc([128, S], BF16, "qfT")
            qcT = A.alloc([128, S], BF16, "qcT")
            kfT = A.alloc([128, S], BF16, "kfT")
            kdtm = A.alloc([128, 32, 128], BF16, "kdtm")
            vtm = A.alloc([128, 32, 128], BF16, "rvtm")
            Rb = [A.alloc([128, 128], BF16, "Rb%d" % i) for i in range(8)]
            Rf = A.alloc([128, 128], F32, "Rf")
            cs = [A.alloc([128, 512], F32, "cs%d" % i) for i in range(2)]
            t12 = [A.alloc([128, 512], F32, "t12%d" % i) for i in range(2)]
            sm = [A.alloc([128, 512], BF16, "sm%d" % i) for i in range(2)]
            on = [A.alloc([128, 128], BF16, "on%d" % i) for i in range(2)]
            bst = A.alloc([128, 4 * 6], F32, "bst")
            bag = A.alloc([128, 4 * 2], F32, "bag")
            rsd = A.alloc([128, 8], F32, "rsd")
            cross3 = lambda pair: V(cff.ap[:, CF["cross"][0] + pair * 128:CF["cross"][0] + (pair + 1) * 128]
                                    .rearrange("p (o i) -> p o i", o=1).to_broadcast([128, 4, 128]), cff.buf)
            for pair in range(2):
                for which, cbase, dst in (("q", C_RQ, qfT), ("k", C_RK, kfT)):
                    wx, slx = load_w(l, cbase + pair * 128, 128)
                    wy, sly = load_w(l, cbase + pair * 128 + 32, 32, dstoff=0, neg=True)
                    load_w(l, cbase + pair * 128, 32, dstoff=32, slot=sly)
                    load_w(l, cbase + pair * 128 + 96, 32, dstoff=64, slot=sly, neg=True)
                    wy, _ = load_w(l, cbase + pair * 128 + 64, 32, dstoff=96, slot=sly)
                    for tc in range(8):
                        px = ps("X")
                        py = ps("X")
                        for c in range(8):
                            P.mm(px, wx[:, c, :], hT[:, c, tc * 512:(tc + 1) * 512], start=(c == 0), stop=(c == 7))
                        for c in range(8):
                            P.mm(py, wy[:, c, :], hT[:, c, tc * 512:(tc + 1) * 512], start=(c == 0), stop=(c == 7))
                        P.dma(cs[0], rc_d[:, tc * 512:(tc + 1) * 512], eng="sp", chan="cs0")
                        P.dma(cs[1], rs_d[:, tc * 512:(tc + 1) * 512], eng="sp", chan="cs1")
                        P.tt(t12[0], px, cs[0], ALU.mult)
                        P.tt(t12[1], py, cs[1], ALU.mult)
                        sl = dst[:, tc * 512:(tc + 1) * 512]
                        P.tt(sl, t12[0], t12[1], ALU.add, eng="pool")
                        if which == "q":
                            P.tt(qcT[:, tc * 512:(tc + 1) * 512].re("p (n i) -> p n i", i=128), sl.re("p (n i) -> p n i", i=128),
                                 cross3(pair), ALU.mult, eng="pool")
                for hh in range(2):
                    h = 2 * pair + hh
                    rows = slice(64 * hh, 64 * hh + 64)
                    dec = RET_G[h] ** 128.0
                    wv = lw(l, C_RV + h * 128, 128)

                    def sinkv(tt, pb):
                        evac(vtm[:, tt, :], pb[:, 0:128])
                    proj_tm(wv, 128, sinkv)
                    okd = CF["kdec"][0]
                    for n8 in range(4):
                        pt = ps("X").bc(BF16)
                        for j in range(8):
                            n = n8 * 8 + j
                            P.tr(pt[:, j * 64:(j + 1) * 64], kfT[rows, n * 128:(n + 1) * 128], cbf[rows, CB["ident"][0] + 64 * hh:CB["ident"][0] + 64 * hh + 64])
                        P.ts(kdtm[:, n8 * 8:(n8 + 1) * 8, 64 * hh:64 * hh + 64], pt[:, 0:512].re("p (n d) -> p n d", d=64),
                             cff[:, okd + h:okd + h + 1], None, ALU.mult)
                    oc = sh["oc"][h % 2]
                    P.memset(Rf[rows, :], 0.0, eng="dve")
                    P.memset(Rb[0][rows, :], 0.0, eng="dve")
                    oin = CF["intra"][0]
                    intra3 = V(cff.ap[:, oin + h * 128:oin + (h + 1) * 128].rearrange("p (o i) -> p o i", o=1).to_broadcast([128, 4, 128]), cff.buf)
                    for n4 in range(8):
                        pS = ps("S")
                        for k in range(4):
                            n = n4 * 4 + k
                            P.mm(pS[:, k * 128:(k + 1) * 128], kfT[rows, n * 128:(n + 1) * 128], qfT[rows, n * 128:(n + 1) * 128])
                        smt = sm[n4 % 2]
                        P.tt(smt.re("p (k i) -> p k i", i=128), pS.re("p (k i) -> p k i", i=128), intra3, ALU.mult)
                        pU = ps("X")
                        for k in range(4):
                            n = n4 * 4 + k
                            P.mm(pU[rows, k * 128:(k + 1) * 128], kdtm[:, n, 64 * hh:64 * hh + 64], vtm[:, n, :])
                        pO = ps("O")
                        for k in range(4):
                            n = n4 * 4 + k
                            Rcur = Rb[n % 8]
                            P.mm(pO[:, k * 128:(k + 1) * 128], smt[:, k * 128:(k + 1) * 128], vtm[:, n, :], start=True, stop=(n == 0))
                            if n > 0:
                                P.mm(pO[:, k * 128:(k + 1) * 128], qcT[rows, n * 128:(n + 1) * 128], Rcur[rows, :], start=False, stop=True)
                            if n < 31:
                                P.stt(Rf[rows, :], Rf[rows, :], dec, pU[rows, k * 128:(k + 1) * 128], ALU.mult, ALU.add)
                                P.copy(Rb[(n + 1) % 8][rows, :], Rf[rows, :], eng="act")
                        for k in range(4):
                            o_, i_ = bst.ap[:, k * 6:(k + 1) * 6], pO.ap[:, k * 128:(k + 1) * 128]
                            P.add("dve", lambda e, o_=o_, i_=i_: e.bn_stats(o_, i_), reads=[pO.buf], writes=[bst.buf])
                            o2_, i2_ = bag.ap[:, k * 2:(k + 1) * 2], bst.ap[:, k * 6:(k + 1) * 6]
                            P.add("dve", lambda e, o2_=o2_, i2_=i2_: e.bn_aggr(o2_, i2_), reads=[bst.buf], writes=[bag.buf])
                        bag3 = bag.re("p (k t) -> p k t", t=2)
                        P.ts(rsd[:, 0:4], bag3[:, :, 1], EPS, None, ALU.add)
                        P.tt(rsd[:, 4:8], rsd[:, 0:4], V(cff.ap[:, CF["neghalf"][0]:CF["neghalf"][0] + 1].to_broadcast([128, 4]), cff.buf), ALU.pow, eng="pool")
                        pt = ps("X").bc(BF16)
                        for k in range(4):
                            n = n4 * 4 + k
                            ont = on[k % 2]
                            P.ts(ont, pO[:, k * 128:(k + 1) * 128], bag[:, 2 * k:2 * k + 1], rsd[:, 4 + k:5 + k], ALU.subtract, ALU.mult)
                            P.tr(pt[:, k * 128:(k + 1) * 128], ont, ident_bf)
                        P.ts(oc[:, n4 * 512:(n4 + 1) * 512], pt[:, 0:512], retg[:, l * 4 + h:l * 4 + h + 1], None, ALU.mult)
                    gate_store(l, 8 + h, oc, h % 2)

        fns = {"mem": phase_mem, "moba": phase_moba, "nsa": phase_nsa, "ret": phase_ret}
        for l in range(nl):
            if l == 0:
                src_ap, src_bufs = x_d.ap, [x_d.buf] * 32
            else:
                src_ap, src_bufs = x1_ap, x1_bufs
            if l == nl - 1:
                dst_ap, dst_bufs = out_ap, out_bufs
            else:
                dst_ap, dst_bufs = x1_ap, x1_bufs
            P.barrier()
            A.top = mark0
            phase_norm_T(src_ap, src_bufs, S, gpre[:, l * 8:(l + 1) * 8], hT)
            for ph in phases:
                new_phase(n_oc=(1 if ph == "nsa" else 2))
                fns[ph](l)
            P.barrier()
            A.top = mark0
            phase_out(l, src_ap, src_bufs, dst_ap, dst_bufs)
        print("arena peak words", A.peak, "of", NW, "ops", {e: len(P.ops[e]) for e in ENGS}, "chans", len(P.chans), flush=True)
        P.emit(st)
    return nc


_CACHE = {}


def kernel(**inputs):
    n = 8
    if "nc" not in _CACHE:
        _CACHE["nc"] = build()
        _CACHE["consts"] = make_consts()
    nc = _CACHE["nc"]
    cbv, cfv, rc, rs = _CACHE["consts"]
    f = lambda a: np.ascontiguousarray(np.asarray(a, dtype=np.float32))
    shared = {k: f(inputs[k]) for k in ("pre_norm_g", "post_norm_g", "mem_norm_g", "w_in", "w_mem_kv", "nsa_pe_k", "nsa_w1_k",
                                        "nsa_w2_k", "nsa_pe_v", "nsa_w1_v", "nsa_w2_v", "ret_gn_g", "w_out")}
    shared.update(cb=cbv, cf=cfv, rc=rc, rs=rs)
    x = f(inputs["x"])
    mem = f(inputs["mem"])
    in_maps = []
    for i in range(n):
        m = dict(shared)
        m["x"] = np.ascontiguousarray(x[i])
        m["mem"] = np.ascontiguousarray(mem[i])
        in_maps.append(m)
    res = run_bass_kernel_spmd(nc, in_maps, core_ids=list(range(n)))
    return np.stack([np.asarray(r["out"], dtype=np.float32) for r in res.results], axis=0)
```

```python
import numpy as np
import ml_dtypes
import concourse.bass as bass
import concourse.mybir as mybir
from concourse.bass_utils import run_bass_kernel_spmd

F32 = mybir.dt.float32
BF16 = mybir.dt.bfloat16
AF = mybir.ActivationFunctionType
ALU = mybir.AluOpType
AX = mybir.AxisListType

S = 4096
D = 1024
NL = 2
INC = 6424
NEG = -30000.0
EPS = 1e-6

C_MQ, C_MK, C_MV = 0, 512, 1024
C_NQ = 1536
C_NKC, C_NVC, C_NKS, C_NVS, C_NKW, C_NVW = 2048, 2176, 2304, 2432, 2560, 2688
C_NG = 2816
C_RQ, C_RK, C_RV = 2840, 3096, 3352
C_CQ = 3864
C_Z = 4376


class Buf:
    __slots__ = ("name", "w", "r")

    def __init__(self, name=""):
        self.name = name
        self.w = None
        self.r = []


class V:
    __slots__ = ("ap", "buf")

    def __init__(self, ap, buf):
        self.ap = ap
        self.buf = buf

    def __getitem__(self, k):
        return V(self.ap[k], self.buf)

    def re(self, s, **kw):
        return V(self.ap.rearrange(s, **kw), self.buf)

    def bc(self, dt):
        return V(self.ap.bitcast(dt), self.buf)


class Op:
    __slots__ = ("eng", "fn", "deps", "signal", "sem", "val", "is_dma", "chan", "gidx")

    def __init__(self, eng, fn, is_dma=False, chan=None):
        self.eng = eng
        self.fn = fn
        self.deps = []
        self.signal = False
        self.sem = None
        self.val = 0
        self.is_dma = is_dma
        self.chan = chan


ENGS = ("pe", "act", "dve", "pool", "sp")


class Prog:
    def __init__(self, nc):
        self.nc = nc
        self.ops = {e: [] for e in ENGS}
        self.all = []
        self.chans = {}
        self.chan_last = {}
        self.bar = None
        self.bar_pending = set()

    def add(self, eng, fn, reads=(), writes=(), is_dma=False, chan=None):
        op = Op(eng, fn, is_dma, chan)
        deps = []
        for b in reads:
            if b.w is not None:
                deps.append((b.w, "raw"))
        for b in writes:
            if b.w is not None:
                deps.append((b.w, "waw"))
            for r in b.r:
                deps.append((r, "war"))
        if eng in self.bar_pending:
            for d in self.bar:
                deps.append((d, "bar"))
            self.bar_pending.discard(eng)
        seen = set()
        for d, kind in deps:
            if d is op or id(d) in seen:
                continue
            if not self._needs(d, op, kind):
                continue
            seen.add(id(d))
            d.signal = True
            op.deps.append(d)
        for b in reads:
            b.r.append(op)
        for b in writes:
            b.w = op
            b.r = []
        op.gidx = len(self.all)
        self.all.append(op)
        self.ops[eng].append(op)
        return op

    @staticmethod
    def _needs(d, op, kind):
        if d.is_dma:
            return True
        if op.is_dma:
            return True
        if d.eng != op.eng:
            return True
        if d.eng == "pe":
            return False
        return True

    def barrier(self):
        last = []
        for e in ENGS:
            seen_compute = False
            for op in reversed(self.ops[e]):
                if op.is_dma:
                    last.append(op)
                elif not seen_compute:
                    last.append(op)
                    seen_compute = True
                if len(last) > 4000:
                    break
        per = {}
        out = []
        for op in last:
            if op.is_dma:
                if op.chan not in per:
                    per[op.chan] = op
                    out.append(op)
            else:
                out.append(op)
        self.bar = out
        self.bar_pending = set(ENGS)

    def mm(self, out, lhsT, rhs, start=True, stop=True, extra_reads=()):
        o, l, r = out.ap, lhsT.ap, rhs.ap
        return self.add("pe", lambda e: e.matmul(o, l, r, start=start, stop=stop),
                        reads=[lhsT.buf, rhs.buf] + list(extra_reads), writes=[out.buf])

    def tr(self, out, in_, ident):
        o, i, d = out.ap, in_.ap, ident.ap
        return self.add("pe", lambda e: e.transpose(o, i, d), reads=[in_.buf, ident.buf], writes=[out.buf])

    def act(self, out, in_, func, scale=1.0, bias=0.0, accum=None, eng="act"):
        o, i = out.ap, in_.ap
        reads = [in_.buf]
        writes = [out.buf]
        b = bias
        if isinstance(bias, V):
            reads.append(bias.buf)
            b = bias.ap
        sc = scale
        if isinstance(scale, V):
            reads.append(scale.buf)
            sc = scale.ap
        acc = None
        if accum is not None:
            writes.append(accum.buf)
            acc = accum.ap
        if acc is None:
            fn = lambda e: e.activation(o, i, func, bias=b, scale=sc)
        else:
            fn = lambda e: e.activation(o, i, func, bias=b, scale=sc, accum_out=acc)
        return self.add("act", fn, reads=reads, writes=writes)

    def ts(self, out, in0, s1, s2, op0, op1=None, eng="dve", accum=None):
        o, i = out.ap, in0.ap
        reads = [in0.buf]
        a1, a2 = s1, s2
        if isinstance(s1, V):
            reads.append(s1.buf)
            a1 = s1.ap
        if isinstance(s2, V):
            reads.append(s2.buf)
            a2 = s2.ap
        writes = [out.buf]
        kw = {}
        if op1 is not None:
            kw["op1"] = op1
        if accum is not None:
            kw["accum_out"] = accum.ap
            writes.append(accum.buf)
        return self.add(eng, lambda e: e.tensor_scalar(o, i, a1, a2, op0, **kw), reads=reads, writes=writes)

    def tt(self, out, in0, in1, op, eng="dve"):
        o, a, b = out.ap, in0.ap, in1.ap
        return self.add(eng, lambda e: e.tensor_tensor(o, a, b, op), reads=[in0.buf, in1.buf], writes=[out.buf])

    def stt(self, out, in0, scalar, in1, op0, op1, eng="dve"):
        o, a, b = out.ap, in0.ap, in1.ap
        reads = [in0.buf, in1.buf]
        s = scalar
        if isinstance(scalar, V):
            reads.append(scalar.buf)
            s = scalar.ap
        return self.add(eng, lambda e: e.scalar_tensor_tensor(o, a, s, b, op0, op1), reads=reads, writes=[out.buf])

    def copy(self, out, in_, eng="dve"):
        o, i = out.ap, in_.ap
        if eng == "act":
            return self.add("act", lambda e: e.copy(o, i), reads=[in_.buf], writes=[out.buf])
        return self.add(eng, lambda e: e.tensor_copy(o, i), reads=[in_.buf], writes=[out.buf])

    def recip(self, out, in_):
        o, i = out.ap, in_.ap
        return self.add("dve", lambda e: e.reciprocal(o, i), reads=[in_.buf], writes=[out.buf])

    def memset(self, out, val, eng="pool"):
        o = out.ap
        return self.add(eng, lambda e: e.memset(o, val), writes=[out.buf])

    def dma(self, out, in_, eng="sp", chan=None, noncontig=False, extra_reads=()):
        o, i = out.ap, in_.ap
        if chan is None:
            chan = "c_" + str(id(out.buf))
        if chan not in self.chans:
            self.chans[chan] = [None, 0]
            self.chan_last[chan] = None
        if noncontig:
            fn = lambda e: e.dma_start(out=o, in_=i, allow_slow_non_contiguous=True)
        else:
            fn = lambda e: e.dma_start(out=o, in_=i)
        op = self.add(eng, fn, reads=[in_.buf] + list(extra_reads), writes=[out.buf], is_dma=True, chan=chan)
        self.chans[chan][1] += 16
        op.val = self.chans[chan][1]
        prev = self.chan_last.get(chan)
        if prev is not None and prev not in op.deps:
            prev.signal = True
            op.deps.append(prev)
        self.chan_last[chan] = op
        return op

    def emit(self, stack):
        nc = self.nc
        esem = {}
        for e in ENGS:
            esem[e] = stack.enter_context(nc.semaphore("s_" + e))
        for c in self.chans:
            self.chans[c][0] = stack.enter_context(nc.semaphore("d%d" % len(esem)))
            esem["chan_" + c] = self.chans[c][0]
        for e in ENGS:
            cnt = 0
            for op in self.ops[e]:
                if op.is_dma:
                    op.sem = self.chans[op.chan][0]
                    op.signal = True
                else:
                    op.sem = esem[e]
                    if op.signal:
                        cnt += 1
                        op.val = cnt
        print("sem max", {e: max([op.val for op in self.ops[e] if not op.is_dma] + [0]) for e in ENGS},
              "chan max", max(v[1] for v in self.chans.values()), flush=True)
        final_waits = []
        for c, (sem, val) in self.chans.items():
            final_waits.append((sem, val))
        self.final_waits = final_waits
        prog = self

        def run(engine, ename):
            seen = {}
            nwait = 0
            for op in prog.ops[ename]:
                for d in op.deps:
                    k = id(d.sem)
                    if seen.get(k, 0) >= d.val:
                        continue
                    seen[k] = d.val
                    engine.wait_ge(d.sem, d.val)
                    nwait += 1
                ins = op.fn(engine)
                if op.signal:
                    ins.then_inc(op.sem, 16 if op.is_dma else 1)
            if ename == "sp":
                for sem, val in prog.final_waits:
                    if seen.get(id(sem), 0) < val:
                        engine.wait_ge(sem, val)

        with nc.Block() as block:
            @block.tensor
            def _(e):
                run(e, "pe")

            @block.scalar
            def _(e):
                run(e, "act")

            @block.vector
            def _(e):
                run(e, "dve")

            @block.gpsimd
            def _(e):
                run(e, "pool")

            @block.sync
            def _(e):
                run(e, "sp")


BIGM = 30000.0
NSA_STOP = 99
NSA_G = (0, 1)
NSA_BR = (0, 1, 2)
NSA_NOGATE = False
RET_G = [1.0 - 2.0 ** (-5.0 - h) for h in range(4)]


def _layout(items):
    off = 0
    d = {}
    for name, w in items:
        d[name] = (off, w)
        off += w
    return d, off


CB_ITEMS = [("ident", 128), ("ones", 128), ("tcaus", 896), ("twin", 1408), ("tcmp", 2560),
            ("eb16", 2048), ("eslc", 4096), ("sel", 3072), ("caug", 130)]
CB, NCB = _layout(CB_ITEMS)
CF_ITEMS = [("ident", 128), ("ones", 128), ("pastm", 512), ("ownm2", 512), ("fbt", 128),
            ("intra", 512), ("cross", 256), ("kdec", 4), ("neghalf", 1), ("pad", 3)]
CF, NCF = _layout(CF_ITEMS)


def make_consts():
    cb = np.zeros((128, NCB), np.float32)
    cf = np.zeros((128, NCF), np.float32)

    def put(arr, lay, name, val):
        o, w = lay[name]
        assert val.shape[1] == w, (name, val.shape, w)
        arr[: val.shape[0], o:o + w] = val

    p = np.arange(128)[:, None]
    put(cb, CB, "ident", np.eye(128, dtype=np.float32))
    put(cb, CB, "ones", np.ones((128, 128), np.float32))
    j = np.arange(896)[None, :]
    put(cb, CB, "tcaus", np.where((j - 384) >= p, 0.0, NEG).astype(np.float32))
    j = np.arange(1408)[None, :]
    dd = (j - 384) - p
    put(cb, CB, "twin", np.where((dd >= 0) & (dd < 512), 0.0, NEG).astype(np.float32))
    j = np.arange(2560)[None, :]
    put(cb, CB, "tcmp", np.where(16 * p + 31 <= j, 0.0, NEG).astype(np.float32))
    e = np.zeros((16, 2048), np.float32)
    for b in range(16):
        e[b, b * 128:(b + 1) * 128] = 1.0
    put(cb, CB, "eb16", e)
    e = np.zeros((128, 4096), np.float32)
    for b in range(64):
        e[b, b * 64:(b + 1) * 64] = 1.0
        e[64 + b, b * 64:(b + 1) * 64] = 1.0
    put(cb, CB, "eslc", e)
    e = np.zeros((56, 3072), np.float32)
    for r in range(24):
        e[r, r * 128:(r + 1) * 128] = 1.0
        e[32 + r, r * 128:(r + 1) * 128] = 1.0
    put(cb, CB, "sel", e)
    n_cmp, n_slc = 255, 64
    mat = np.zeros((256, 64), np.float32)
    for jj in range(n_slc):
        for a in range(4):
            for b in range(2):
                i = 4 * jj + a - b
                if 0 <= i < n_cmp:
                    mat[i, jj] += 1.0
    ca = np.zeros((128, 130), np.float32)
    for t in range(2):
        ca[:, t * 65:t * 65 + 64] = mat[t * 128:(t + 1) * 128]
        ca[:, t * 65 + 64] = 1.0
    put(cb, CB, "caug", ca)

    put(cf, CF, "ident", np.eye(128, dtype=np.float32))
    put(cf, CF, "ones", np.ones((128, 128), np.float32))
    pm = np.zeros((32, 16), np.float32)
    om = np.zeros((32, 16), np.float32)
    for qt in range(32):
        own = qt // 2
        for b in range(16):
            pm[qt, b] = 0.0 if b < own else -1e30
            om[qt, b] = (-BIGM if b < own else (0.0 if b == own else -2 * BIGM))
    put(cf, CF, "pastm", np.broadcast_to(pm.reshape(1, 512), (128, 512)))
    put(cf, CF, "ownm2", np.broadcast_to(om.reshape(1, 512), (128, 512)))
    fb = np.zeros((128, 128), np.float32)
    for ql in range(128):
        orel = ql // 64
        for jx in range(127):
            m = jx - 63
            if m > orel:
                fb[ql, jx] = -1000.0
            elif m == orel or m == orel - 1:
                fb[ql, jx] = 1000.0
    put(cf, CF, "fbt", fb)
    it = np.zeros((128, 512), np.float32)
    jj = np.arange(128)[:, None]
    ii = np.arange(128)[None, :]
    for h in range(4):
        g = RET_G[h]
        it[:, h * 128:(h + 1) * 128] = np.where(ii >= jj, g ** np.maximum(ii - jj, 0), 0.0) * 0.125
    put(cf, CF, "intra", it)
    cr = np.zeros((128, 256), np.float32)
    for pair in range(2):
        for half in range(2):
            g = RET_G[2 * pair + half]
            cr[half * 64:(half + 1) * 64, pair * 128:(pair + 1) * 128] = (g ** (np.arange(128) + 1.0))[None, :]
    put(cf, CF, "cross", cr)
    kd = np.zeros((128, 4), np.float32)
    for h in range(4):
        kd[:, h] = RET_G[h] ** (127.0 - np.arange(128)) * 0.125
    put(cf, CF, "kdec", kd)
    put(cf, CF, "neghalf", np.full((128, 1), -0.5, np.float32))
    inv_freq = (1.0 / (10000.0 ** np.linspace(0.0, 1.0, 32))).astype(np.float32)
    ang = np.arange(S, dtype=np.float32)[:, None] * inv_freq[None, :]
    cos = np.cos(ang).astype(np.float32).T
    sin = np.sin(ang).astype(np.float32).T
    c2 = np.concatenate([cos, cos, cos, cos], 0)
    s2 = np.concatenate([sin, sin, sin, sin], 0)
    return cb, cf, np.ascontiguousarray(c2), np.ascontiguousarray(s2)


class Arena:
    def __init__(self, ap32, nwords):
        self.ap = ap32
        self.n = nwords
        self.top = 0
        self.peak = 0

    def alloc(self, shape, dt, name=""):
        free = 1
        for s in shape[1:]:
            free *= s
        words = free if dt == F32 else (free + 1) // 2
        a = self.top
        self.top += words
        self.peak = max(self.peak, self.top)
        assert self.top <= self.n, ("arena overflow", name, self.top, self.n)
        ap = self.ap[:, a:a + words]
        if dt != F32:
            ap = ap.bitcast(dt)[:, 0:free]
        if len(shape) == 3:
            ap = ap.rearrange("p (a b) -> p a b", a=shape[1])
        ap = ap[0:shape[0]]
        return V(ap, Buf(name))


def build(nl=NL, dbg=False, phases=("mem", "moba", "nsa", "ret")):
    from contextlib import ExitStack
    nc = bass.Bass("TRN2", target_bir_lowering=False)

    def din(name, shape, dt=F32):
        return V(nc.dram_tensor(name, shape, dt, kind="ExternalInput").ap(), Buf(name))

    x_d = din("x", [S, D])
    mem_d = din("mem", [256, D])
    pre_g_d = din("pre_norm_g", [NL, D])
    post_g_d = din("post_norm_g", [NL, D])
    mem_g_d = din("mem_norm_g", [NL, D])
    w_in_d = din("w_in", [NL, D, INC])
    w_mem_d = din("w_mem_kv", [NL, D, 1024])
    pe_k_d = din("nsa_pe_k", [NL, 32, 64])
    w1_k_d = din("nsa_w1_k", [NL, 2048, 128])
    w2_k_d = din("nsa_w2_k", [NL, 128, 64])
    pe_v_d = din("nsa_pe_v", [NL, 32, 64])
    w1_v_d = din("nsa_w1_v", [NL, 2048, 128])
    w2_v_d = din("nsa_w2_v", [NL, 128, 64])
    retg_d = din("ret_gn_g", [NL, 512])
    w_out_d = din("w_out", [NL, 2048, D])
    cb_d = din("cb", [128, NCB])
    cf_d = din("cf", [128, NCF])
    rc_d = din("rc", [128, S])
    rs_d = din("rs", [128, S])
    out_ap = nc.dram_tensor("out", [S, D], F32, kind="ExternalOutput").ap()
    x1_ap = nc.dram_tensor("x1s", [S, D], F32, kind="Internal").ap()
    oT_kind = "ExternalOutput" if dbg else "Internal"
    oT_ap = nc.dram_tensor("oTs", [2048, S], BF16, kind=oT_kind).ap()
    oT_bufs = [Buf("oT%d" % i) for i in range(16)]
    x1_bufs = [Buf("x1_%d" % i) for i in range(32)]
    out_bufs = [Buf("out_%d" % i) for i in range(32)]

    P = Prog(nc)
    st = ExitStack()
    with st:
        NW = 52000
        arena_t = st.enter_context(nc.sbuf_tensor("arena", [128, NW], F32))
        A = Arena(arena_t[:, :], NW)
        banks = []
        for i in range(8):
            pt = st.enter_context(nc.psum_tensor("psb%d" % i, [128, 512], F32))
            banks.append(V(pt[:, :], Buf("ps%d" % i)))
        grp = {"S": [0, 1, 2], "O": [3, 4], "X": [5, 6, 7]}
        gctr = {"S": 0, "O": 0, "X": 0}

        def ps(g):
            b = banks[grp[g][gctr[g] % len(grp[g])]]
            gctr[g] += 1
            return b

        cbf = A.alloc([128, NCB], BF16, "cbf")
        cff = A.alloc([128, NCF], F32, "cff")
        hT = A.alloc([128, 8, S], BF16, "hT")
        gpre = A.alloc([128, 8 * NL], F32, "gpre")
        gmem = A.alloc([128, 8 * NL], F32, "gmem")
        retg = A.alloc([128, 4 * NL], F32, "retg")

        def cb(name, rows=128):
            o, w = CB[name]
            return cbf[0:rows, o:o + w]

        def cf(name, rows=128):
            o, w = CF[name]
            return cff[0:rows, o:o + w]

        for k in range(0, NCB, 2048):
            e = min(NCB, k + 2048)
            P.dma(cbf[:, k:e], cb_d[:, k:e], eng="pool", chan="const")
        P.dma(cff, cf_d, eng="sp", chan="const2")
        for l in range(NL):
            P.dma(gpre[:, l * 8:(l + 1) * 8], V(pre_g_d.ap[l].rearrange("(c p) -> p c", p=128), pre_g_d.buf), eng="sp", chan="const2", noncontig=True)
            P.dma(gmem[:, l * 8:(l + 1) * 8], V(mem_g_d.ap[l].rearrange("(c p) -> p c", p=128), mem_g_d.buf), eng="sp", chan="const2", noncontig=True)
            P.dma(retg[:, l * 4:(l + 1) * 4], V(retg_d.ap[l].rearrange("(h v) -> v h", v=128), retg_d.buf), eng="sp", chan="const2", noncontig=True)
        ident_bf = cb("ident")
        ones_bf = cb("ones")
        ident_f = cf("ident")
        ones_f = cf("ones")
        mark0 = A.top

        ev_ctr = [0]

        def evac(dst, src, scale=None):
            ev_ctr[0] += 1
            if ev_ctr[0] % 2 == 0:
                P.act(dst, src, AF.Copy, scale=(1.0 if scale is None else scale))
            else:
                if scale is None:
                    P.copy(dst, src, eng="dve")
                else:
                    P.ts(dst, src, scale, None, ALU.mult)

        sh = {}

        def new_phase(n_oc=2, n_w=4):
            P.barrier()
            A.top = mark0
            sh["w"] = [A.alloc([128, 8, 128], BF16, "w%d" % i) for i in range(n_w)]
            sh["wc"] = 0
            sh["p"] = [A.alloc([128, 512], BF16, "pb%d" % i) for i in range(4)]
            sh["pc"] = 0
            sh["rr"] = [A.alloc([128, 512], F32, "rr%d" % i) for i in range(2)]
            sh["bs"] = [A.alloc([128, 512], F32, "bs%d" % i) for i in range(2)]
            sh["fc"] = 0
            sh["sz"] = [A.alloc([128, 512], BF16, "sz%d" % i) for i in range(2)]
            sh["szc"] = 0
            sh["oc"] = [A.alloc([128, S], BF16, "oc%d" % i) for i in range(n_oc)]

        def load_w(l, col0, n, src=None, dstoff=0, slot=None, neg=False):
            if slot is None:
                k = sh["wc"] % len(sh["w"])
                slot = (sh["w"][k], k)
                sh["wc"] += 1
            sl, k = slot
            srcv = (w_in_d if src is None else src)
            s3 = V(srcv.ap[l].rearrange("(c p) n -> p c n", p=128)[:, :, col0:col0 + n], srcv.buf)
            P.dma(sl[:, :, dstoff:dstoff + n], s3, eng="pool", chan="w%d" % k)
            if neg:
                P.ts(sl[:, :, dstoff:dstoff + n], sl[:, :, dstoff:dstoff + n], -1.0, None, ALU.mult, eng="pool")
            return sl[:, :, 0:dstoff + n], slot

        def lw(l, col0, n, src=None):
            return load_w(l, col0, n, src=src)[0]

        def proj_fm(dst, w, prow, M, sink=None):
            for tc in range(8):
                pb = ps("X")
                for c in range(8):
                    P.mm(pb[prow:prow + M, :], w[:, c, :], hT[:, c, tc * 512:(tc + 1) * 512], start=(c == 0), stop=(c == 7))
                if sink is None:
                    evac(dst[prow:prow + M, tc * 512:(tc + 1) * 512], pb[prow:prow + M, :])
                else:
                    sink(tc, pb)

        def proj_tm(w, n, sink, src=None, ntile=32):
            srcT = hT if src is None else src
            for tt in range(ntile):
                pb = ps("X")
                for c in range(8):
                    P.mm(pb[:, 0:n], srcT[:, c, tt * 128:(tt + 1) * 128], w[:, c, :], start=(c == 0), stop=(c == 7))
                sink(tt, pb)

        def attn_chunk(qc, qT, kT, pl, vfn, M, scale, extra=None):
            n = len(pl)
            O = ps("O")
            O2 = ps("O") if extra is not None else None
            pbs = {}

            def do_s(i):
                kt, masks = pl[i]
                Sp = ps("S")
                P.mm(Sp, kT[:, kt * 128:(kt + 1) * 128], qT[:, qc * 512:(qc + 1) * 512], start=True, stop=(len(masks) == 0))
                for mi, (ml, mr) in enumerate(masks):
                    P.mm(Sp, ml, mr, start=False, stop=(mi == len(masks) - 1))
                Pb = sh["p"][sh["pc"] % 4]
                sh["pc"] += 1
                P.act(Pb, Sp, AF.Exp, scale=scale)
                return Pb

            LOOK = 2
            for i in range(min(LOOK, n)):
                pbs[i] = do_s(i)
            for i in range(n):
                if i + LOOK < n:
                    pbs[i + LOOK] = do_s(i + LOOK)
                pb = pbs.pop(i)
                P.mm(O[0:M, :], vfn(pl[i][0]), pb, start=(i == 0), stop=(i == n - 1))
                if extra is not None:
                    em, efn = extra
                    P.mm(O2[0:em, :], efn(pl[i][0]), pb, start=(i == 0), stop=(i == n - 1))
            return O, O2

        def bcast_rden(O, dr, gate_ps=None, guard=False):
            k = sh["fc"]
            sh["fc"] += 1
            r1 = sh["rr"][k % 2]
            P.copy(r1[dr:dr + 1, :], O[dr:dr + 1, :], eng="act")
            B = ps("X")
            P.mm(B, ones_f[dr:dr + 1, :], r1[dr:dr + 1, :])
            Bs = sh["bs"][k % 2]
            if guard:
                P.ts(Bs, B, 1e-30, None, ALU.max)
                P.recip(Bs, Bs)
            else:
                P.recip(Bs, B)
            if gate_ps is not None:
                P.tt(Bs, Bs, gate_ps, ALU.mult)
            return Bs

        def gate_store(l, ci, ocT, slot_id):
            w = lw(l, C_Z + ci * 128, 128)

            def sink(tc, pb):
                sz = sh["sz"][sh["szc"] % 2]
                sh["szc"] += 1
                P.act(sz, pb, AF.Silu)
                sl = ocT[:, tc * 512:(tc + 1) * 512]
                P.tt(sl, sl, sz, ALU.mult, eng="pool")
            proj_fm(None, w, 0, 128, sink=sink)
            P.dma(V(oT_ap[ci * 128:(ci + 1) * 128, :], oT_bufs[ci]), ocT, eng="sp", chan="oc%d" % slot_id)

        def phase_norm_T(src_ap, src_bufs, ntok, g_sb, dstT):
            xsl = [A.alloc([128, D], F32, "xs%d" % i) for i in range(3)]
            xnl = [A.alloc([128, D], BF16, "xn%d" % i) for i in range(2)]
            junk = A.alloc([128, D], BF16, "junk")
            stt_ = [A.alloc([128, 4], F32, "st%d" % i) for i in range(4)]
            g3 = V(g_sb.ap.rearrange("p (c o) -> p c o", o=1).to_broadcast([128, 8, 128]), g_sb.buf)
            for tt in range(ntok // 128):
                xs = xsl[tt % 3]
                P.dma(xs, V(src_ap[tt * 128:(tt + 1) * 128, :], src_bufs[tt]), eng="sp", chan="xs%d" % (tt % 3))
                stt = stt_[tt % 4]
                P.act(junk, xs, AF.Square, accum=stt[:, 0:1])
                P.ts(stt[:, 1:2], stt[:, 0:1], 1.0 / D, EPS, ALU.mult, ALU.add)
                P.tt(stt[:, 2:3], stt[:, 1:2], cf("neghalf"), ALU.pow, eng="pool")
                xn = xnl[tt % 2]
                P.ts(xn, xs, stt[:, 2:3], None, ALU.mult)
                pt = ps("X").bc(BF16)
                for c in range(8):
                    P.tr(pt[:, c * 128:(c + 1) * 128], xn[:, c * 128:(c + 1) * 128], ident_bf)
                P.tt(dstT[:, :, tt * 128:(tt + 1) * 128], pt.re("p (c t) -> p c t", c=8), g3, ALU.mult)

        def phase_out(l, src_ap, src_bufs, dst_ap, dst_bufs):
            wout = A.alloc([128, 16, D], BF16, "wout")
            w3 = V(w_out_d.ap[l].rearrange("(c p) n -> p c n", p=128), w_out_d.buf)
            for c0 in range(0, 16, 2):
                P.dma(wout[:, c0:c0 + 2, :], w3[:, c0:c0 + 2, :], eng="pool", chan="wout")
            gp = A.alloc([128, D], F32, "gpost")
            P.dma(gp, V(post_g_d.ap[l:l + 1, :].to_broadcast([128, D]), post_g_d.buf), eng="sp", chan="gpost")
            xsl = [A.alloc([128, D], F32, "xo%d" % i) for i in range(2)]
            otl = [A.alloc([128, 16, 128], BF16, "ot%d" % i) for i in range(2)]
            ynl = [A.alloc([128, D], F32, "yn%d" % i) for i in range(2)]
            junk = A.alloc([128, 512], BF16, "junk2")
            stt_ = [A.alloc([128, 4], F32, "sto%d" % i) for i in range(4)]
            oT3 = oT_ap.rearrange("(c p) t -> p c t", p=128)
            for tt in range(32):
                ot = otl[tt % 2]
                P.dma(ot, V(oT3[:, :, tt * 128:(tt + 1) * 128], oT_bufs[0]), eng="sp", chan="ot%d" % (tt % 2), extra_reads=oT_bufs[1:])
                xs = xsl[tt % 2]
                P.dma(xs, V(src_ap[tt * 128:(tt + 1) * 128, :], src_bufs[tt]), eng="sp", chan="xo%d" % (tt % 2))
                stt = stt_[tt % 4]
                pbs_ = []
                for half in range(2):
                    pb = ps("X")
                    pbs_.append(pb)
                    for c in range(16):
                        P.mm(pb, ot[:, c, :], wout[:, c, half * 512:(half + 1) * 512], start=(c == 0), stop=(c == 15))
                    P.act(junk, pb, AF.Square, accum=stt[:, half:half + 1])
                P.tt(stt[:, 2:3], stt[:, 0:1], stt[:, 1:2], ALU.add)
                P.ts(stt[:, 2:3], stt[:, 2:3], 1.0 / D, EPS, ALU.mult, ALU.add)
                P.tt(stt[:, 3:4], stt[:, 2:3], cf("neghalf"), ALU.pow, eng="pool")
                yn = ynl[tt % 2]
                for half in range(2):
                    hs = slice(half * 512, (half + 1) * 512)
                    P.stt(yn[:, hs], pbs_[half], stt[:, 3:4], gp[:, hs], ALU.mult, ALU.mult)
                P.tt(yn, yn, xs, ALU.add, eng="pool")
                P.dma(V(dst_ap[tt * 128:(tt + 1) * 128, :], dst_bufs[tt]), yn, eng="sp", chan="sto%d" % (tt % 2))

        def phase_mem(l):
            memT = A.alloc([128, 8, 256], BF16, "memT")
            m0 = A.top
            phase_norm_T(mem_d.ap, [mem_d.buf] * 2, 256, gmem[:, l * 8:(l + 1) * 8], memT)
            A.top = m0
            P.barrier()
            vtm = A.alloc([128, 2, 512], BF16, "memv")
            kTm = A.alloc([128, 256], BF16, "memk")
            qTm = A.alloc([128, S], BF16, "memq")
            rdn = [A.alloc([128, 512], F32, "rdn%d" % i) for i in range(2)]
            for j in range(4):
                w = lw(l, 512 + j * 128, 128, src=w_mem_d)

                def sinkv(tt, pb, j=j):
                    evac(vtm[:, tt, j * 128:(j + 1) * 128], pb[:, 0:128])
                proj_tm(w, 128, sinkv, src=memT, ntile=2)
            sc = 128.0 ** -0.5
            for h in range(4):
                wk = lw(l, h * 128, 128, src=w_mem_d)
                pb = ps("X")
                for c in range(8):
                    P.mm(pb[:, 0:256], wk[:, c, :], memT[:, c, :], start=(c == 0), stop=(c == 7))
                evac(kTm, pb[:, 0:256])
                wq = lw(l, C_CQ + h * 128, 128)
                proj_fm(qTm, wq, 0, 128)
                oc = sh["oc"][h % 2]
                for qc in range(8):
                    O, O2 = attn_chunk(qc, qTm, kTm, [(0, []), (1, [])], lambda kt, h=h: vtm[:, kt, h * 128:(h + 1) * 128], 128, sc,
                                       extra=(128, lambda kt: ones_bf))
                    r = rdn[qc % 2]
                    P.recip(r, O2)
                    P.tt(oc[:, qc * 512:(qc + 1) * 512], O, r, ALU.mult)
                gate_store(l, 12 + h, oc, h % 2)

        def phase_moba(l):
            qh = A.alloc([128, S], BF16, "mqh")
            kh = A.alloc([128, S], BF16, "mkh")
            vaug = A.alloc([128, 32, 193], BF16, "mvaug")
            kmf = A.alloc([128, 16], F32, "kmf")
            kmb = A.alloc([128, 16], BF16, "kmb")
            gsb = A.alloc([128, 512], F32, "gsb")
            m8 = A.alloc([128, 256], F32, "m8")
            a1 = A.alloc([128, 512], F32, "a1")
            mv = A.alloc([128, 512], BF16, "mv")
            mv64 = A.alloc([128, 2048], BF16, "mv64")
            mvp = A.alloc([128, 8 * 128], BF16, "mvp")
            P.memset(vaug[:, :, 64:66], 1.0)
            P.memset(vaug[:, :, 66:129], 0.0)
            P.memset(mvp, 0.0)
            otc = CB["tcaus"][0]
            oes = CB["eslc"][0]
            for pair in range(4):
                wv = lw(l, C_MV + pair * 128, 128)

                def sinkv(tt, pb):
                    evac(vaug[:, tt, 0:64], pb[:, 0:64])
                    evac(vaug[:, tt, 129:193], pb[:, 64:128])
                proj_tm(wv, 128, sinkv)
                oc = sh["oc"][pair % 2]
                for hh in range(2):
                    h = 2 * pair + hh
                    r0 = 64 * hh
                    rows = slice(r0, r0 + 64)
                    orows = slice(64 - r0, 128 - r0)
                    proj_fm(qh, lw(l, C_MQ + h * 64, 64), r0, 64)
                    proj_fm(kh, lw(l, C_MK + h * 64, 64), r0, 64)
                    P.copy(kh[orows, :], cbf[orows, oes:oes + S], eng="pool")
                    o_, i_ = kmf.ap[rows], kh.ap[rows].rearrange("p (b k) -> p b k", k=256)
                    P.add("dve", lambda e, o_=o_, i_=i_: e.tensor_reduce(o_, i_, AX.X, ALU.add), reads=[kh.buf], writes=[kmf.buf])
                    P.ts(kmb[rows], kmf[rows], 1.0 / 256, None, ALU.mult)
                    G = ps("X")
                    for qt in range(32):
                        P.mm(G[:, qt * 16:(qt + 1) * 16], qh[rows, qt * 128:(qt + 1) * 128], kmb[rows, :], start=True, stop=True)
                    P.tt(gsb, G, cf("pastm"), ALU.add)
                    for qt in range(32):
                        o_, i_ = m8.ap[:, qt * 8:(qt + 1) * 8], gsb.ap[:, qt * 16:(qt + 1) * 16]
                        P.add("dve", lambda e, o_=o_, i_=i_: e.max(o_, i_), reads=[gsb.buf], writes=[m8.buf])
                    thr = V(m8.ap.rearrange("p (q e) -> p q e", e=8)[:, :, 2:3].to_broadcast([128, 32, 16]), m8.buf)
                    P.tt(a1.re("p (q b) -> p q b", b=16), gsb.re("p (q b) -> p q b", b=16), thr, ALU.is_ge)
                    P.stt(a1, a1, BIGM, cf("ownm2"), ALU.mult, ALU.add)
                    P.ts(mv, a1, 0.0, None, ALU.min)
                    P.copy(mv64.re("p (c r) -> p c r", r=4),
                           V(mv.ap.rearrange("p (c o) -> p c o", o=1).to_broadcast([128, 512, 4]), mv.buf), eng="dve")
                    c0 = 64 - r0
                    for g4 in range(4):
                        P.copy(mvp.re("p (j c) -> p j c", c=128)[:, :, c0:c0 + 64],
                               mv64[:, g4 * 512:(g4 + 1) * 512].re("p (j c) -> p j c", c=64), eng="pool")
                        pt = ps("X").bc(BF16)
                        for j in range(8):
                            P.tr(pt[:, j * 128:(j + 1) * 128], mvp[:, j * 128:(j + 1) * 128], ident_bf)
                        evac(qh[orows, g4 * 1024:(g4 + 1) * 1024], pt[orows, :])
                    if hh == 0:
                        vfn = lambda kt: vaug[:, kt, 0:65]
                        M, dr = 65, 64
                    else:
                        vfn = lambda kt: vaug[:, kt, 65:193]
                        M, dr = 128, 0
                    for qc in range(8):
                        pl = []
                        for kt in range(4 * qc + 4):
                            masks = []
                            if kt >= 4 * qc:
                                delta = 128 * kt - 512 * qc
                                masks.append((ident_bf, cbf[:, otc + 384 - delta:otc + 384 - delta + 512]))
                            pl.append((kt, masks))
                        O, _ = attn_chunk(qc, qh, kh, pl, vfn, M, 0.125)
                        Bs = bcast_rden(O, dr)
                        P.tt(oc[rows, qc * 512:(qc + 1) * 512], O[rows, :], Bs[rows, :], ALU.mult)
                gate_store(l, pair, oc, pair % 2)

        def phase_nsa(l):
            oc = sh["oc"][0]
            kcTs = [A.alloc([128, 256], BF16, "kcT%d" % i) for i in range(2)]
            for t_ in kcTs:
                P.memset(t_, 0.0)
            vcaug = A.alloc([128, 4, 129], BF16, "vcaug")
            Gt = A.alloc([128, S], BF16, "Gt")
            qT = A.alloc([128, S], BF16, "nqT")
            P.memset(qT, 0.0)
            acc = [A.alloc([128, 512], F32, "acc%d" % i) for i in range(2)]
            tmpb = [A.alloc([128, 512], F32, "tmpb%d" % i) for i in range(2)]
            m1 = A.top
            kin = A.alloc([128, S], BF16, "kin")
            vin = A.alloc([128, S], BF16, "vin")
            proj_fm(kin, lw(l, C_NKC, 128), 0, 128)
            proj_fm(vin, lw(l, C_NVC, 128), 0, 128)
            P.memset(vcaug[:, :, 0:1], 1.0)
            P.memset(vcaug[:, :, 1:64], 0.0)
            P.memset(vcaug[:, :, 128:129], 1.0)
            w1 = A.alloc([128, 32, 128], BF16, "w1")
            peT = A.alloc([128, 32], BF16, "peT")
            w2 = A.alloc([128, 64], BF16, "w2")
            bias = A.alloc([128, 2], F32, "cbias")
            hid = A.alloc([128, 256], BF16, "hid")
            for which, pe_d, w1_d, w2_d, xin in (("k", pe_k_d, w1_k_d, w2_k_d, kin), ("v", pe_v_d, w1_v_d, w2_v_d, vin)):
                s3 = V(w1_d.ap[l].rearrange("(l d) j -> d l j", d=64), w1_d.buf)
                for hlf in range(2):
                    for l0 in range(0, 32, 8):
                        P.dma(w1[hlf * 64:(hlf + 1) * 64, l0:l0 + 8, :], s3[:, l0:l0 + 8, :], eng="pool", chan="w1")
                    P.dma(peT[hlf * 64:(hlf + 1) * 64, :], V(pe_d.ap[l].rearrange("l d -> d l"), pe_d.buf), eng="pool", chan="w1", noncontig=True)
                P.dma(w2, V(w2_d.ap[l], w2_d.buf), eng="pool", chan="w1")
                for g in range(2):
                    rows = slice(64 * g, 64 * g + 64)
                    pbias = ps("X")
                    for l_ in range(32):
                        P.mm(pbias[:, 0:1], w1[rows, l_, :], peT[rows, l_:l_ + 1], start=(l_ == 0), stop=(l_ == 31))
                    P.copy(bias[:, g:g + 1], pbias[:, 0:1], eng="dve")
                    ph = ps("X")
                    x3 = V(xin.ap[rows].rearrange("p (i s) -> p i s", s=16), xin.buf)
                    for l_ in range(32):
                        P.mm(ph[:, 0:255], w1[rows, l_, :], x3[:, l_ // 16:l_ // 16 + 255, l_ % 16], start=(l_ == 0), stop=(l_ == 31))
                    P.memset(hid[:, 255:256], 0.0, eng="dve")
                    P.act(hid[:, 0:255], ph[:, 0:255], AF.Silu, bias=bias[:, g:g + 1])
                    if which == "k":
                        pk = ps("X")
                        P.mm(pk[rows, 0:256], w2, hid)
                        evac(kcTs[g][rows, :], pk[rows, 0:256])
                    else:
                        for t in range(2):
                            pv = ps("X")
                            P.mm(pv[:, 0:64], hid[:, t * 128:(t + 1) * 128], w2)
                            evac(vcaug[:, g * 2 + t, 64:128], pv[:, 0:64])
            if NSA_STOP <= 1:
                return
            wg = lw(l, C_NG, 24)
            P.memset(Gt, 0.0)
            sg = A.alloc([56, 512], F32, "sg")
            hi2 = A.alloc([56, 512], BF16, "hi2")
            for tc in range(8):
                pb = ps("X")
                for r0 in (0, 32):
                    for c in range(8):
                        P.mm(pb[r0:r0 + 24, :], wg[:, c, :], hT[:, c, tc * 512:(tc + 1) * 512], start=(c == 0), stop=(c == 7))
                    P.act(sg[r0:r0 + 24, :], pb[r0:r0 + 24, :], AF.Sigmoid)
                P.copy(Gt[0:24, tc * 512:(tc + 1) * 512], sg[0:24, :], eng="dve")
                P.copy(hi2[32:56, :], sg[32:56, :], eng="dve")
                P.tt(Gt[32:56, tc * 512:(tc + 1) * 512], sg[32:56, :], hi2[32:56, :], ALU.subtract)
            if NSA_STOP <= 2:
                return
            A.top = m1
            P.barrier()
            ksT = A.alloc([128, S], BF16, "ksT")
            kwT = A.alloc([128, S], BF16, "kwT")
            m2 = A.top
            ocaug = CB["caug"][0]
            otcmp = CB["tcmp"][0]
            otc = CB["tcaus"][0]
            otw = CB["twin"][0]
            oes = CB["eslc"][0]
            osel = CB["sel"][0]

            def pairs_cmp(qc):
                pl = []
                if qc >= 5:
                    pl.append((0, []))
                else:
                    d0 = 512 * qc
                    pl.append((0, [(ident_bf, cbf[:, otcmp + d0:otcmp + d0 + 512])]))
                if qc >= 4:
                    d1 = 512 * qc - 2048
                    pl.append((1, [(ident_bf, cbf[:, otcmp + d1:otcmp + d1 + 512])]))
                return pl

            for g in NSA_G:
                rows = slice(64 * g, 64 * g + 64)
                orows_g = slice(64 - 64 * g, 128 - 64 * g)
                A.top = m2
                P.barrier()
                impT = A.alloc([64, S], F32, "impT")
                imp2 = A.alloc([128, 64], F32, "imp2")
                tmp2 = A.alloc([128, 64], F32, "tmp2")
                m8a = A.alloc([128, 8], F32, "m8a")
                m8b = A.alloc([128, 8], F32, "m8b")
                mvb = A.alloc([128, 8 * 128], BF16, "mvb")
                P.memset(mvb, 0.0)
                for p in range(4):
                    h = 4 * g + p
                    proj_fm(qT, lw(l, C_NQ + h * 64, 64), 64 * g, 64)
                    for qc in range(8):
                        O, _ = attn_chunk(qc, qT, kcTs[g], pairs_cmp(qc),
                                          lambda kt: cbf[:, ocaug + kt * 65:ocaug + kt * 65 + 65], 65, 0.125)
                        Bs = bcast_rden(O, 64, guard=True)
                        sl = impT[0:64, qc * 512:(qc + 1) * 512]
                        if p == 0:
                            P.tt(sl, O[0:64, :], Bs[0:64, :], ALU.mult)
                        else:
                            tb = tmpb[qc % 2]
                            P.tt(tb[0:64, :], O[0:64, :], Bs[0:64, :], ALU.mult)
                            P.tt(sl, sl, tb[0:64, :], ALU.add, eng="pool")
                if NSA_STOP <= 3:
                    return
                ofb = CF["fbt"][0]
                for g8 in range(4):
                    for j in range(8):
                        qt = g8 * 8 + j
                        pt = ps("X")
                        P.tr(pt[:, 0:64], impT[0:64, qt * 128:(qt + 1) * 128], ident_f[0:64, 0:64])
                        P.tt(imp2, pt[:, 0:64], cff[:, ofb + 63 - 2 * qt:ofb + 127 - 2 * qt], ALU.add)
                        P.memset(imp2[:, 0:1], 1000.0, eng="dve")
                        a_, b_, c_, d_ = m8a.ap, imp2.ap, tmp2.ap, m8b.ap
                        P.add("dve", lambda e, a_=a_, b_=b_: e.max(a_, b_), reads=[imp2.buf], writes=[m8a.buf])
                        P.add("dve", lambda e, a_=a_, b_=b_, c_=c_: e.match_replace(c_, a_, b_, -1e30), reads=[imp2.buf, m8a.buf], writes=[tmp2.buf])
                        P.add("dve", lambda e, c_=c_, d_=d_: e.max(d_, c_), reads=[tmp2.buf], writes=[m8b.buf])
                        P.ts(mvb[:, j * 128 + 64 - 64 * g:j * 128 + 128 - 64 * g], imp2, m8b[:, 7:8], -BIGM, ALU.is_lt, ALU.mult)
                    pt = ps("X").bc(BF16)
                    for j in range(8):
                        P.tr(pt[:, j * 128:(j + 1) * 128], mvb[:, j * 128:(j + 1) * 128], ident_bf)
                    evac(qT[orows_g, g8 * 1024:(g8 + 1) * 1024], pt[orows_g, :])
                if NSA_STOP <= 4:
                    return
                P.barrier()
                A.top = m2
                vsa = A.alloc([128, 32, 129], BF16, "vsa")
                vwa = A.alloc([128, 32, 129], BF16, "vwa")
                for t_ in (vsa, vwa):
                    P.memset(t_[:, :, 0:1], 1.0)
                    P.memset(t_[:, :, 1:64], 0.0)
                    P.memset(t_[:, :, 128:129], 1.0)
                proj_fm(ksT, lw(l, C_NKS + 64 * g, 64), 64 * g, 64)
                P.copy(ksT[orows_g, :], cbf[orows_g, oes:oes + S], eng="pool")
                proj_fm(kwT, lw(l, C_NKW + 64 * g, 64), 64 * g, 64)
                P.memset(kwT[orows_g, :], 0.0)
                for cbase, dstv in ((C_NVS, vsa), (C_NVW, vwa)):
                    wv = lw(l, cbase + 64 * g, 64)

                    def sinkv(tt, pb, dstv=dstv):
                        evac(dstv[:, tt, 64:128], pb[:, 0:64])
                    proj_tm(wv, 64, sinkv)
                if NSA_STOP <= 5:
                    return
                for p in range(4):
                    if NSA_STOP <= 9 and p >= NSA_STOP - 5:
                        return
                    h = 4 * g + p
                    par = p % 2
                    orow = slice(64 * par, 64 * par + 64)
                    proj_fm(qT, lw(l, C_NQ + h * 64, 64), 64 * g, 64)
                    if par == 0:
                        c0, c1, M, dr = 64, 129, 65, 64
                    else:
                        c0, c1, M, dr = 0, 128, 128, 0
                    for qc in range(8):
                        ac = acc[qc % 2]
                        for j in range(3):
                            if j not in NSA_BR:
                                continue
                            if j == 0:
                                pl = pairs_cmp(qc)
                                kTj = kcTs[g]
                                vfn = lambda kt: vcaug[:, g * 2 + kt, c0:c1]
                            elif j == 1:
                                pl = []
                                for kt in range(4 * qc + 4):
                                    masks = []
                                    if kt >= 4 * qc:
                                        delta = 128 * kt - 512 * qc
                                        masks.append((ident_bf, cbf[:, otc + 384 - delta:otc + 384 - delta + 512]))
                                    pl.append((kt, masks))
                                kTj = ksT
                                vfn = lambda kt: vsa[:, kt, c0:c1]
                            else:
                                pl = []
                                for kt in range(max(0, 4 * qc - 4), 4 * qc + 4):
                                    delta = 128 * kt - 512 * qc
                                    pl.append((kt, [(ident_bf, cbf[:, otw + 384 - delta:otw + 384 - delta + 512])]))
                                kTj = kwT
                                vfn = lambda kt: vwa[:, kt, c0:c1]
                            O, _ = attn_chunk(qc, qT, kTj, pl, vfn, M, 0.125)
                            idx = h * 3 + j
                            Gb = None
                            if not NSA_NOGATE:
                                Gb = ps("X")
                                P.mm(Gb, cbf[:, osel + idx * 128:osel + (idx + 1) * 128], Gt[:, qc * 512:(qc + 1) * 512])
                            Bs = bcast_rden(O, dr, gate_ps=Gb, guard=(j == 0))
                            if j == 0:
                                P.tt(ac[orow, :], O[orow, :], Bs[orow, :], ALU.mult)
                            else:
                                tb = tmpb[j % 2]
                                P.tt(tb[orow, :], O[orow, :], Bs[orow, :], ALU.mult)
                                if j == 1:
                                    P.tt(ac[orow, :], ac[orow, :], tb[orow, :], ALU.add, eng="pool")
                                else:
                                    P.tt(oc[orow, qc * 512:(qc + 1) * 512], ac[orow, :], tb[orow, :], ALU.add, eng="pool")
                    if par == 1:
                        gate_store(l, 4 + h // 2, oc, 0)
                if NSA_STOP <= 10:
                    return

        def phase_ret(l):
            qfT = A.alloc([128, S], BF16, "qfT")
            qcT = A.alloc([128, S], BF16, "qcT")
            kfT = A.alloc([128, S], BF16, "kfT")
            kdtm = A.alloc([128, 32, 128], BF16, "kdtm")
            vtm = A.alloc([128, 32, 128], BF16, "rvtm")
            Rb = [A.alloc([128, 128], BF16, "Rb%d" % i) for i in range(8)]
            Rf = A.alloc([128, 128], F32, "Rf")
            cs = [A.alloc([128, 512], F32, "cs%d" % i) for i in range(2)]
            t12 = [A.alloc([128, 512], F32, "t12%d" % i) for i in range(2)]
            sm = [A.alloc([128, 512], BF16, "sm%d" % i) for i in range(2)]
            on = [A.alloc([128, 128], BF16, "on%d" % i) for i in range(2)]
            bst = A.alloc([128, 4 * 6], F32, "bst")
            bag = A.alloc([128, 4 * 2], F32, "bag")
            rsd = A.alloc([128, 8], F32, "rsd")
            cross3 = lambda pair: V(cff.ap[:, CF["cross"][0] + pair * 128:CF["cross"][0] + (pair + 1) * 128]
                                    .rearrange("p (o i) -> p o i", o=1).to_broadcast([128, 4, 128]), cff.buf)
            for pair in range(2):
                for which, cbase, dst in (("q", C_RQ, qfT), ("k", C_RK, kfT)):
                    wx, slx = load_w(l, cbase + pair * 128, 128)
                    wy, sly = load_w(l, cbase + pair * 128 + 32, 32, dstoff=0, neg=True)
                    load_w(l, cbase + pair * 128, 32, dstoff=32, slot=sly)
                    load_w(l, cbase + pair * 128 + 96, 32, dstoff=64, slot=sly, neg=True)
                    wy, _ = load_w(l, cbase + pair * 128 + 64, 32, dstoff=96, slot=sly)
                    for tc in range(8):
                        px = ps("X")
                        py = ps("X")
                        for c in range(8):
                            P.mm(px, wx[:, c, :], hT[:, c, tc * 512:(tc + 1) * 512], start=(c == 0), stop=(c == 7))
                        for c in range(8):
                            P.mm(py, wy[:, c, :], hT[:, c, tc * 512:(tc + 1) * 512], start=(c == 0), stop=(c == 7))
                        P.dma(cs[0], rc_d[:, tc * 512:(tc + 1) * 512], eng="sp", chan="cs0")
                        P.dma(cs[1], rs_d[:, tc * 512:(tc + 1) * 512], eng="sp", chan="cs1")
                        P.tt(t12[0], px, cs[0], ALU.mult)
                        P.tt(t12[1], py, cs[1], ALU.mult)
                        sl = dst[:, tc * 512:(tc + 1) * 512]
                        P.tt(sl, t12[0], t12[1], ALU.add, eng="pool")
                        if which == "q":
                            P.tt(qcT[:, tc * 512:(tc + 1) * 512].re("p (n i) -> p n i", i=128), sl.re("p (n i) -> p n i", i=128),
                                 cross3(pair), ALU.mult, eng="pool")
                for hh in range(2):
                    h = 2 * pair + hh
                    rows = slice(64 * hh, 64 * hh + 64)
                    dec = RET_G[h] ** 128.0
                    wv = lw(l, C_RV + h * 128, 128)

                    def sinkv(tt, pb):
                        evac(vtm[:, tt, :], pb[:, 0:128])
                    proj_tm(wv, 128, sinkv)
                    okd = CF["kdec"][0]
                    for n8 in range(4):
                        pt = ps("X").bc(BF16)
                        for j in range(8):
                            n = n8 * 8 + j
                            P.tr(pt[:, j * 64:(j + 1) * 64], kfT[rows, n * 128:(n + 1) * 128], cbf[rows, CB["ident"][0] + 64 * hh:CB["ident"][0] + 64 * hh + 64])
                        P.ts(kdtm[:, n8 * 8:(n8 + 1) * 8, 64 * hh:64 * hh + 64], pt[:, 0:512].re("p (n d) -> p n d", d=64),
                             cff[:, okd + h:okd + h + 1], None, ALU.mult)
                    oc = sh["oc"][h % 2]
                    P.memset(Rf[rows, :], 0.0, eng="dve")
                    P.memset(Rb[0][rows, :], 0.0, eng="dve")
                    oin = CF["intra"][0]
                    intra3 = V(cff.ap[:, oin + h * 128:oin + (h + 1) * 128].rearrange("p (o i) -> p o i", o=1).to_broadcast([128, 4, 128]), cff.buf)
                    for n4 in range(8):
                        pS = ps("S")
                        for k in range(4):
                            n = n4 * 4 + k
                            P.mm(pS[:, k * 128:(k + 1) * 128], kfT[rows, n * 128:(n + 1) * 128], qfT[rows, n * 128:(n + 1) * 128])
                        smt = sm[n4 % 2]
                        P.tt(smt.re("p (k i) -> p k i", i=128), pS.re("p (k i) -> p k i", i=128), intra3, ALU.mult)
                        pU = ps("X")
                        for k in range(4):
                            n = n4 * 4 + k
                            P.mm(pU[rows, k * 128:(k + 1) * 128], kdtm[:, n, 64 * hh:64 * hh + 64], vtm[:, n, :])
                        pO = ps("O")
                        for k in range(4):
                            n = n4 * 4 + k
                            Rcur = Rb[n % 8]
                            P.mm(pO[:, k * 128:(k + 1) * 128], smt[:, k * 128:(k + 1) * 128], vtm[:, n, :], start=True, stop=(n == 0))
                            if n > 0:
                                P.mm(pO[:, k * 128:(k + 1) * 128], qcT[rows, n * 128:(n + 1) * 128], Rcur[rows, :], start=False, stop=True)
                            if n < 31:
                                P.stt(Rf[rows, :], Rf[rows, :], dec, pU[rows, k * 128:(k + 1) * 128], ALU.mult, ALU.add)
                                P.copy(Rb[(n + 1) % 8][rows, :], Rf[rows, :], eng="act")
                        for k in range(4):
                            o_, i_ = bst.ap[:, k * 6:(k + 1) * 6], pO.ap[:, k * 128:(k + 1) * 128]
                            P.add("dve", lambda e, o_=o_, i_=i_: e.bn_stats(o_, i_), reads=[pO.buf], writes=[bst.buf])
                            o2_, i2_ = bag.ap[:, k * 2:(k + 1) * 2], bst.ap[:, k * 6:(k + 1) * 6]
                            P.add("dve", lambda e, o2_=o2_, i2_=i2_: e.bn_aggr(o2_, i2_), reads=[bst.buf], writes=[bag.buf])
                        bag3 = bag.re("p (k t) -> p k t", t=2)
                        P.ts(rsd[:, 0:4], bag3[:, :, 1], EPS, None, ALU.add)
                        P.tt(rsd[:, 4:8], rsd[:, 0:4], V(cff.ap[:, CF["neghalf"][0]:CF["neghalf"][0] + 1].to_broadcast([128, 4]), cff.buf), ALU.pow, eng="pool")
                        pt = ps("X").bc(BF16)
                        for k in range(4):
                            n = n4 * 4 + k
                            ont = on[k % 2]
                            P.ts(ont, pO[:, k * 128:(k + 1) * 128], bag[:, 2 * k:2 * k + 1], rsd[:, 4 + k:5 + k], ALU.subtract, ALU.mult)
                            P.tr(pt[:, k * 128:(k + 1) * 128], ont, ident_bf)
                        P.ts(oc[:, n4 * 512:(n4 + 1) * 512], pt[:, 0:512], retg[:, l * 4 + h:l * 4 + h + 1], None, ALU.mult)
                    gate_store(l, 8 + h, oc, h % 2)

        fns = {"mem": phase_mem, "moba": phase_moba, "nsa": phase_nsa, "ret": phase_ret}
        for l in range(nl):
            if l == 0:
                src_ap, src_bufs = x_d.ap, [x_d.buf] * 32
            else:
                src_ap, src_bufs = x1_ap, x1_bufs
            if l == nl - 1:
                dst_ap, dst_bufs = out_ap, out_bufs
            else:
                dst_ap, dst_bufs = x1_ap, x1_bufs
            P.barrier()
            A.top = mark0
            phase_norm_T(src_ap, src_bufs, S, gpre[:, l * 8:(l + 1) * 8], hT)
            for ph in phases:
                new_phase(n_oc=(1 if ph == "nsa" else 2))
                fns[ph](l)
            P.barrier()
            A.top = mark0
            phase_out(l, src_ap, src_bufs, dst_ap, dst_bufs)
        print("arena peak words", A.peak, "of", NW, "ops", {e: len(P.ops[e]) for e in ENGS}, "chans", len(P.chans), flush=True)
        P.emit(st)
    return nc


_CACHE = {}


def kernel(**inputs):
    n = 8
    if "nc" not in _CACHE:
        _CACHE["nc"] = build()
        _CACHE["consts"] = make_consts()
    nc = _CACHE["nc"]
    cbv, cfv, rc, rs = _CACHE["consts"]
    f = lambda a: np.ascontiguousarray(np.asarray(a, dtype=np.float32))
    shared = {k: f(inputs[k]) for k in ("pre_norm_g", "post_norm_g", "mem_norm_g", "w_in", "w_mem_kv", "nsa_pe_k", "nsa_w1_k",
                                        "nsa_w2_k", "nsa_pe_v", "nsa_w1_v", "nsa_w2_v", "ret_gn_g", "w_out")}
    shared.update(cb=cbv, cf=cfv, rc=rc, rs=rs)
    x = f(inputs["x"])
    mem = f(inputs["mem"])
    in_maps = []
    for i in range(n):
        m = dict(shared)
        m["x"] = np.ascontiguousarray(x[i])
        m["mem"] = np.ascontiguousarray(mem[i])
        in_maps.append(m)
    res = run_bass_kernel_spmd(nc, in_maps, core_ids=list(range(n)))
    return np.stack([np.asarray(r["out"], dtype=np.float32) for r in res.results], axis=0)
```

```python
import numpy as np
import ml_dtypes
import concourse.bass as bass
import concourse.mybir as mybir
from concourse.bass_utils import run_bass_kernel_spmd

F32 = mybir.dt.float32
BF16 = mybir.dt.bfloat16
AF = mybir.ActivationFunctionType
ALU = mybir.AluOpType
AX = mybir.AxisListType

S = 4096
D = 1024
NL = 2
INC = 6424
NEG = -30000.0
EPS = 1e-6

C_MQ, C_MK, C_MV = 0, 512, 1024
C_NQ = 1536
C_NKC, C_NVC, C_NKS, C_NVS, C_NKW, C_NVW = 2048, 2176, 2304, 2432, 2560, 2688
C_NG = 2816
C_RQ, C_RK, C_RV = 2840, 3096, 3352
C_CQ = 3864
C_Z = 4376


class Buf:
    __slots__ = ("name", "w", "r")

    def __init__(self, name=""):
        self.name = name
        self.w = None
        self.r = []


class V:
    __slots__ = ("ap", "buf")

    def __init__(self, ap, buf):
        self.ap = ap
        self.buf = buf

    def __getitem__(self, k):
        return V(self.ap[k], self.buf)

    def re(self, s, **kw):
        return V(self.ap.rearrange(s, **kw), self.buf)

    def bc(self, dt):
        return V(self.ap.bitcast(dt), self.buf)


class Op:
    __slots__ = ("eng", "fn", "deps", "signal", "sem", "val", "is_dma", "chan", "gidx")

    def __init__(self, eng, fn, is_dma=False, chan=None):
        self.eng = eng
        self.fn = fn
        self.deps = []
        self.signal = False
        self.sem = None
        self.val = 0
        self.is_dma = is_dma
        self.chan = chan


ENGS = ("pe", "act", "dve", "pool", "sp")


class Prog:
    def __init__(self, nc):
        self.nc = nc
        self.ops = {e: [] for e in ENGS}
        self.all = []
        self.chans = {}
        self.chan_last = {}
        self.bar = None
        self.bar_pending = set()

    def add(self, eng, fn, reads=(), writes=(), is_dma=False, chan=None):
        op = Op(eng, fn, is_dma, chan)
        deps = []
        for b in reads:
            if b.w is not None:
                deps.append((b.w, "raw"))
        for b in writes:
            if b.w is not None:
                deps.append((b.w, "waw"))
            for r in b.r:
                deps.append((r, "war"))
        if eng in self.bar_pending:
            for d in self.bar:
                deps.append((d, "bar"))
            self.bar_pending.discard(eng)
        seen = set()
        for d, kind in deps:
            if d is op or id(d) in seen:
                continue
            if not self._needs(d, op, kind):
                continue
            seen.add(id(d))
            d.signal = True
            op.deps.append(d)
        for b in reads:
            b.r.append(op)
        for b in writes:
            b.w = op
            b.r = []
        op.gidx = len(self.all)
        self.all.append(op)
        self.ops[eng].append(op)
        return op

    @staticmethod
    def _needs(d, op, kind):
        if d.is_dma:
            return True
        if op.is_dma:
            return True
        if d.eng != op.eng:
            return True
        if d.eng == "pe":
            return False
        return True

    def barrier(self):
        last = []
        for e in ENGS:
            seen_compute = False
            for op in reversed(self.ops[e]):
                if op.is_dma:
                    last.append(op)
                elif not seen_compute:
                    last.append(op)
                    seen_compute = True
                if len(last) > 4000:
                    break
        per = {}
        out = []
        for op in last:
            if op.is_dma:
                if op.chan not in per:
                    per[op.chan] = op
                    out.append(op)
            else:
                out.append(op)
        self.bar = out
        self.bar_pending = set(ENGS)

    def mm(self, out, lhsT, rhs, start=True, stop=True, extra_reads=()):
        o, l, r = out.ap, lhsT.ap, rhs.ap
        return self.add("pe", lambda e: e.matmul(o, l, r, start=start, stop=stop),
                        reads=[lhsT.buf, rhs.buf] + list(extra_reads), writes=[out.buf])

    def tr(self, out, in_, ident):
        o, i, d = out.ap, in_.ap, ident.ap
        return self.add("pe", lambda e: e.transpose(o, i, d), reads=[in_.buf, ident.buf], writes=[out.buf])

    def act(self, out, in_, func, scale=1.0, bias=0.0, accum=None, eng="act"):
        o, i = out.ap, in_.ap
        reads = [in_.buf]
        writes = [out.buf]
        b = bias
        if isinstance(bias, V):
            reads.append(bias.buf)
            b = bias.ap
        sc = scale
        if isinstance(scale, V):
            reads.append(scale.buf)
            sc = scale.ap
        acc = None
        if accum is not None:
            writes.append(accum.buf)
            acc = accum.ap
        if acc is None:
            fn = lambda e: e.activation(o, i, func, bias=b, scale=sc)
        else:
            fn = lambda e: e.activation(o, i, func, bias=b, scale=sc, accum_out=acc)
        return self.add("act", fn, reads=reads, writes=writes)

    def ts(self, out, in0, s1, s2, op0, op1=None, eng="dve", accum=None):
        o, i = out.ap, in0.ap
        reads = [in0.buf]
        a1, a2 = s1, s2
        if isinstance(s1, V):
            reads.append(s1.buf)
            a1 = s1.ap
        if isinstance(s2, V):
            reads.append(s2.buf)
            a2 = s2.ap
        writes = [out.buf]
        kw = {}
        if op1 is not None:
            kw["op1"] = op1
        if accum is not None:
            kw["accum_out"] = accum.ap
            writes.append(accum.buf)
        return self.add(eng, lambda e: e.tensor_scalar(o, i, a1, a2, op0, **kw), reads=reads, writes=writes)

    def tt(self, out, in0, in1, op, eng="dve"):
        o, a, b = out.ap, in0.ap, in1.ap
        return self.add(eng, lambda e: e.tensor_tensor(o, a, b, op), reads=[in0.buf, in1.buf], writes=[out.buf])

    def stt(self, out, in0, scalar, in1, op0, op1, eng="dve"):
        o, a, b = out.ap, in0.ap, in1.ap
        reads = [in0.buf, in1.buf]
        s = scalar
        if isinstance(scalar, V):
            reads.append(scalar.buf)
            s = scalar.ap
        return self.add(eng, lambda e: e.scalar_tensor_tensor(o, a, s, b, op0, op1), reads=reads, writes=[out.buf])

    def copy(self, out, in_, eng="dve"):
        o, i = out.ap, in_.ap
        if eng == "act":
            return self.add("act", lambda e: e.copy(o, i), reads=[in_.buf], writes=[out.buf])
        return self.add(eng, lambda e: e.tensor_copy(o, i), reads=[in_.buf], writes=[out.buf])

    def recip(self, out, in_):
        o, i = out.ap, in_.ap
        return self.add("dve", lambda e: e.reciprocal(o, i), reads=[in_.buf], writes=[out.buf])

    def memset(self, out, val, eng="pool"):
        o = out.ap
        return self.add(eng, lambda e: e.memset(o, val), writes=[out.buf])

    def dma(self, out, in_, eng="sp", chan=None, noncontig=False, extra_reads=()):
        o, i = out.ap, in_.ap
        if chan is None:
            chan = "c_" + str(id(out.buf))
        if chan not in self.chans:
            self.chans[chan] = [None, 0]
            self.chan_last[chan] = None
        if noncontig:
            fn = lambda e: e.dma_start(out=o, in_=i, allow_slow_non_contiguous=True)
        else:
            fn = lambda e: e.dma_start(out=o, in_=i)
        op = self.add(eng, fn, reads=[in_.buf] + list(extra_reads), writes=[out.buf], is_dma=True, chan=chan)
        self.chans[chan][1] += 16
        op.val = self.chans[chan][1]
        prev = self.chan_last.get(chan)
        if prev is not None and prev not in op.deps:
            prev.signal = True
            op.deps.append(prev)
        self.chan_last[chan] = op
        return op

    def emit(self, stack):
        nc = self.nc
        esem = {}
        for e in ENGS:
            esem[e] = stack.enter_context(nc.semaphore("s_" + e))
        for c in self.chans:
            self.chans[c][0] = stack.enter_context(nc.semaphore("d%d" % len(esem)))
            esem["chan_" + c] = self.chans[c][0]
        for e in ENGS:
            cnt = 0
            for op in self.ops[e]:
                if op.is_dma:
                    op.sem = self.chans[op.chan][0]
                    op.signal = True
                else:
                    op.sem = esem[e]
                    if op.signal:
                        cnt += 1
                        op.val = cnt
        print("sem max", {e: max([op.val for op in self.ops[e] if not op.is_dma] + [0]) for e in ENGS},
              "chan max", max(v[1] for v in self.chans.values()), flush=True)
        final_waits = []
        for c, (sem, val) in self.chans.items():
            final_waits.append((sem, val))
        self.final_waits = final_waits
        prog = self

        def run(engine, ename):
            seen = {}
            nwait = 0
            for op in prog.ops[ename]:
                for d in op.deps:
                    k = id(d.sem)
                    if seen.get(k, 0) >= d.val:
                        continue
                    seen[k] = d.val
                    engine.wait_ge(d.sem, d.val)
                    nwait += 1
                ins = op.fn(engine)
                if op.signal:
                    ins.then_inc(op.sem, 16 if op.is_dma else 1)
            if ename == "sp":
                for sem, val in prog.final_waits:
                    if seen.get(id(sem), 0) < val:
                        engine.wait_ge(sem, val)

        with nc.Block() as block:
            @block.tensor
            def _(e):
                run(e, "pe")

            @block.scalar
            def _(e):
                run(e, "act")

            @block.vector
            def _(e):
                run(e, "dve")

            @block.gpsimd
            def _(e):
                run(e, "pool")

            @block.sync
            def _(e):
                run(e, "sp")


BIGM = 30000.0
NSA_STOP = 99
NSA_G = (0, 1)
NSA_BR = (0, 1, 2)
NSA_NOGATE = False
RET_G = [1.0 - 2.0 ** (-5.0 - h) for h in range(4)]


def _layout(items):
    off = 0
    d = {}
    for name, w in items:
        d[name] = (off, w)
        off += w
    return d, off


CB_ITEMS = [("ident", 128), ("ones", 128), ("tcaus", 896), ("twin", 1408), ("tcmp", 2560),
            ("eb16", 2048), ("eslc", 4096), ("sel", 3072), ("caug", 130)]
CB, NCB = _layout(CB_ITEMS)
CF_ITEMS = [("ident", 128), ("ones", 128), ("pastm", 512), ("ownm2", 512), ("fbt", 128),
            ("intra", 512), ("cross", 256), ("kdec", 4), ("neghalf", 1), ("pad", 3)]
CF, NCF = _layout(CF_ITEMS)


def make_consts():
    cb = np.zeros((128, NCB), np.float32)
    cf = np.zeros((128, NCF), np.float32)

    def put(arr, lay, name, val):
        o, w = lay[name]
        assert val.shape[1] == w, (name, val.shape, w)
        arr[: val.shape[0], o:o + w] = val

    p = np.arange(128)[:, None]
    put(cb, CB, "ident", np.eye(128, dtype=np.float32))
    put(cb, CB, "ones", np.ones((128, 128), np.float32))
    j = np.arange(896)[None, :]
    put(cb, CB, "tcaus", np.where((j - 384) >= p, 0.0, NEG).astype(np.float32))
    j = np.arange(1408)[None, :]
    dd = (j - 384) - p
    put(cb, CB, "twin", np.where((dd >= 0) & (dd < 512), 0.0, NEG).astype(np.float32))
    j = np.arange(2560)[None, :]
    put(cb, CB, "tcmp", np.where(16 * p + 31 <= j, 0.0, NEG).astype(np.float32))
    e = np.zeros((16, 2048), np.float32)
    for b in range(16):
        e[b, b * 128:(b + 1) * 128] = 1.0
    put(cb, CB, "eb16", e)
    e = np.zeros((128, 4096), np.float32)
    for b in range(64):
        e[b, b * 64:(b + 1) * 64] = 1.0
        e[64 + b, b * 64:(b + 1) * 64] = 1.0
    put(cb, CB, "eslc", e)
    e = np.zeros((56, 3072), np.float32)
    for r in range(24):
        e[r, r * 128:(r + 1) * 128] = 1.0
        e[32 + r, r * 128:(r + 1) * 128] = 1.0
    put(cb, CB, "sel", e)
    n_cmp, n_slc = 255, 64
    mat = np.zeros((256, 64), np.float32)
    for jj in range(n_slc):
        for a in range(4):
            for b in range(2):
                i = 4 * jj + a - b
                if 0 <= i < n_cmp:
                    mat[i, jj] += 1.0
    ca = np.zeros((128, 130), np.float32)
    for t in range(2):
        ca[:, t * 65:t * 65 + 64] = mat[t * 128:(t + 1) * 128]
        ca[:, t * 65 + 64] = 1.0
    put(cb, CB, "caug", ca)

    put(cf, CF, "ident", np.eye(128, dtype=np.float32))
    put(cf, CF, "ones", np.ones((128, 128), np.float32))
    pm = np.zeros((32, 16), np.float32)
    om = np.zeros((32, 16), np.float32)
    for qt in range(32):
        own = qt // 2
        for b in range(16):
            pm[qt, b] = 0.0 if b < own else -1e30
            om[qt, b] = (-BIGM if b < own else (0.0 if b == own else -2 * BIGM))
    put(cf, CF, "pastm", np.broadcast_to(pm.reshape(1, 512), (128, 512)))
    put(cf, CF, "ownm2", np.broadcast_to(om.reshape(1, 512), (128, 512)))
    fb = np.zeros((128, 128), np.float32)
    for ql in range(128):
        orel = ql // 64
        for jx in range(127):
            m = jx - 63
            if m > orel:
                fb[ql, jx] = -1000.0
            elif m == orel or m == orel - 1:
                fb[ql, jx] = 1000.0
    put(cf, CF, "fbt", fb)
    it = np.zeros((128, 512), np.float32)
    jj = np.arange(128)[:, None]
    ii = np.arange(128)[None, :]
    for h in range(4):
        g = RET_G[h]
        it[:, h * 128:(h + 1) * 128] = np.where(ii >= jj, g ** np.maximum(ii - jj, 0), 0.0) * 0.125
    put(cf, CF, "intra", it)
    cr = np.zeros((128, 256), np.float32)
    for pair in range(2):
        for half in range(2):
            g = RET_G[2 * pair + half]
            cr[half * 64:(half + 1) * 64, pair * 128:(pair + 1) * 128] = (g ** (np.arange(128) + 1.0))[None, :]
    put(cf, CF, "cross", cr)
    kd = np.zeros((128, 4), np.float32)
    for h in range(4):
        kd[:, h] = RET_G[h] ** (127.0 - np.arange(128)) * 0.125
    put(cf, CF, "kdec", kd)
    put(cf, CF, "neghalf", np.full((128, 1), -0.5, np.float32))
    inv_freq = (1.0 / (10000.0 ** np.linspace(0.0, 1.0, 32))).astype(np.float32)
    ang = np.arange(S, dtype=np.float32)[:, None] * inv_freq[None, :]
    cos = np.cos(ang).astype(np.float32).T
    sin = np.sin(ang).astype(np.float32).T
    c2 = np.concatenate([cos, cos, cos, cos], 0)
    s2 = np.concatenate([sin, sin, sin, sin], 0)
    return cb, cf, np.ascontiguousarray(c2), np.ascontiguousarray(s2)


class Arena:
    def __init__(self, ap32, nwords):
        self.ap = ap32
        self.n = nwords
        self.top = 0
        self.peak = 0

    def alloc(self, shape, dt, name=""):
        free = 1
        for s in shape[1:]:
            free *= s
        words = free if dt == F32 else (free + 1) // 2
        a = self.top
        self.top += words
        self.peak = max(self.peak, self.top)
        assert self.top <= self.n, ("arena overflow", name, self.top, self.n)
        ap = self.ap[:, a:a + words]
        if dt != F32:
            ap = ap.bitcast(dt)[:, 0:free]
        if len(shape) == 3:
            ap = ap.rearrange("p (a b) -> p a b", a=shape[1])
        ap = ap[0:shape[0]]
        return V(ap, Buf(name))


def build(nl=NL, dbg=False, phases=("mem", "moba", "nsa", "ret")):
    from contextlib import ExitStack
    nc = bass.Bass("TRN2", target_bir_lowering=False)

    def din(name, shape, dt=F32):
        return V(nc.dram_tensor(name, shape, dt, kind="ExternalInput").ap(), Buf(name))

    x_d = din("x", [S, D])
    mem_d = din("mem", [256, D])
    pre_g_d = din("pre_norm_g", [NL, D])
    post_g_d = din("post_norm_g", [NL, D])
    mem_g_d = din("mem_norm_g", [NL, D])
    w_in_d = din("w_in", [NL, D, INC])
    w_mem_d = din("w_mem_kv", [NL, D, 1024])
    pe_k_d = din("nsa_pe_k", [NL, 32, 64])
    w1_k_d = din("nsa_w1_k", [NL, 2048, 128])
    w2_k_d = din("nsa_w2_k", [NL, 128, 64])
    pe_v_d = din("nsa_pe_v", [NL, 32, 64])
    w1_v_d = din("nsa_w1_v", [NL, 2048, 128])
    w2_v_d = din("nsa_w2_v", [NL, 128, 64])
    retg_d = din("ret_gn_g", [NL, 512])
    w_out_d = din("w_out", [NL, 2048, D])
    cb_d = din("cb", [128, NCB])
    cf_d = din("cf", [128, NCF])
    rc_d = din("rc", [128, S])
    rs_d = din("rs", [128, S])
    out_ap = nc.dram_tensor("out", [S, D], F32, kind="ExternalOutput").ap()
    x1_ap = nc.dram_tensor("x1s", [S, D], F32, kind="Internal").ap()
    oT_kind = "ExternalOutput" if dbg else "Internal"
    oT_ap = nc.dram_tensor("oTs", [2048, S], BF16, kind=oT_kind).ap()
    oT_bufs = [Buf("oT%d" % i) for i in range(16)]
    x1_bufs = [Buf("x1_%d" % i) for i in range(32)]
    out_bufs = [Buf("out_%d" % i) for i in range(32)]

    P = Prog(nc)
    st = ExitStack()
    with st:
        NW = 52000
        arena_t = st.enter_context(nc.sbuf_tensor("arena", [128, NW], F32))
        A = Arena(arena_t[:, :], NW)
        banks = []
        for i in range(8):
            pt = st.enter_context(nc.psum_tensor("psb%d" % i, [128, 512], F32))
            banks.append(V(pt[:, :], Buf("ps%d" % i)))
        grp = {"S": [0, 1, 2], "O": [3, 4], "X": [5, 6, 7]}
        gctr = {"S": 0, "O": 0, "X": 0}

        def ps(g):
            b = banks[grp[g][gctr[g] % len(grp[g])]]
            gctr[g] += 1
            return b

        cbf = A.alloc([128, NCB], BF16, "cbf")
        cff = A.alloc([128, NCF], F32, "cff")
        hT = A.alloc([128, 8, S], BF16, "hT")
        gpre = A.alloc([128, 8 * NL], F32, "gpre")
        gmem = A.alloc([128, 8 * NL], F32, "gmem")
        retg = A.alloc([128, 4 * NL], F32, "retg")

        def cb(name, rows=128):
            o, w = CB[name]
            return cbf[0:rows, o:o + w]

        def cf(name, rows=128):
            o, w = CF[name]
            return cff[0:rows, o:o + w]

        for k in range(0, NCB, 2048):
            e = min(NCB, k + 2048)
            P.dma(cbf[:, k:e], cb_d[:, k:e], eng="pool", chan="const")
        P.dma(cff, cf_d, eng="sp", chan="const2")
        for l in range(NL):
            P.dma(gpre[:, l * 8:(l + 1) * 8], V(pre_g_d.ap[l].rearrange("(c p) -> p c", p=128), pre_g_d.buf), eng="sp", chan="const2", noncontig=True)
            P.dma(gmem[:, l * 8:(l + 1) * 8], V(mem_g_d.ap[l].rearrange("(c p) -> p c", p=128), mem_g_d.buf), eng="sp", chan="const2", noncontig=True)
            P.dma(retg[:, l * 4:(l + 1) * 4], V(retg_d.ap[l].rearrange("(h v) -> v h", v=128), retg_d.buf), eng="sp", chan="const2", noncontig=True)
        ident_bf = cb("ident")
        ones_bf = cb("ones")
        ident_f = cf("ident")
        ones_f = cf("ones")
        mark0 = A.top

        ev_ctr = [0]

        def evac(dst, src, scale=None):
            ev_ctr[0] += 1
            if ev_ctr[0] % 2 == 0:
                P.act(dst, src, AF.Copy, scale=(1.0 if scale is None else scale))
            else:
                if scale is None:
                    P.copy(dst, src, eng="dve")
                else:
                    P.ts(dst, src, scale, None, ALU.mult)

        sh = {}

        def new_phase(n_oc=2, n_w=4):
            P.barrier()
            A.top = mark0
            sh["w"] = [A.alloc([128, 8, 128], BF16, "w%d" % i) for i in range(n_w)]
            sh["wc"] = 0
            sh["p"] = [A.alloc([128, 512], BF16, "pb%d" % i) for i in range(4)]
            sh["pc"] = 0
            sh["rr"] = [A.alloc([128, 512], F32, "rr%d" % i) for i in range(2)]
            sh["bs"] = [A.alloc([128, 512], F32, "bs%d" % i) for i in range(2)]
            sh["fc"] = 0
            sh["sz"] = [A.alloc([128, 512], BF16, "sz%d" % i) for i in range(2)]
            sh["szc"] = 0
            sh["oc"] = [A.alloc([128, S], BF16, "oc%d" % i) for i in range(n_oc)]

        def load_w(l, col0, n, src=None, dstoff=0, slot=None, neg=False):
            if slot is None:
                k = sh["wc"] % len(sh["w"])
                slot = (sh["w"][k], k)
                sh["wc"] += 1
            sl, k = slot
            srcv = (w_in_d if src is None else src)
            s3 = V(srcv.ap[l].rearrange("(c p) n -> p c n", p=128)[:, :, col0:col0 + n], srcv.buf)
            P.dma(sl[:, :, dstoff:dstoff + n], s3, eng="pool", chan="w%d" % k)
            if neg:
                P.ts(sl[:, :, dstoff:dstoff + n], sl[:, :, dstoff:dstoff + n], -1.0, None, ALU.mult, eng="pool")
            return sl[:, :, 0:dstoff + n], slot

        def lw(l, col0, n, src=None):
            return load_w(l, col0, n, src=src)[0]

        def proj_fm(dst, w, prow, M, sink=None):
            for tc in range(8):
                pb = ps("X")
                for c in range(8):
                    P.mm(pb[prow:prow + M, :], w[:, c, :], hT[:, c, tc * 512:(tc + 1) * 512], start=(c == 0), stop=(c == 7))
                if sink is None:
                    evac(dst[prow:prow + M, tc * 512:(tc + 1) * 512], pb[prow:prow + M, :])
                else:
                    sink(tc, pb)

        def proj_tm(w, n, sink, src=None, ntile=32):
            srcT = hT if src is None else src
            for tt in range(ntile):
                pb = ps("X")
                for c in range(8):
                    P.mm(pb[:, 0:n], srcT[:, c, tt * 128:(tt + 1) * 128], w[:, c, :], start=(c == 0), stop=(c == 7))
                sink(tt, pb)

        def attn_chunk(qc, qT, kT, pl, vfn, M, scale, extra=None):
            n = len(pl)
            O = ps("O")
            O2 = ps("O") if extra is not None else None
            pbs = {}

            def do_s(i):
                kt, masks = pl[i]
                Sp = ps("S")
                P.mm(Sp, kT[:, kt * 128:(kt + 1) * 128], qT[:, qc * 512:(qc + 1) * 512], start=True, stop=(len(masks) == 0))
                for mi, (ml, mr) in enumerate(masks):
                    P.mm(Sp, ml, mr, start=False, stop=(mi == len(masks) - 1))
                Pb = sh["p"][sh["pc"] % 4]
                sh["pc"] += 1
                P.act(Pb, Sp, AF.Exp, scale=scale)
                return Pb

            LOOK = 2
            for i in range(min(LOOK, n)):
                pbs[i] = do_s(i)
            for i in range(n):
                if i + LOOK < n:
                    pbs[i + LOOK] = do_s(i + LOOK)
                pb = pbs.pop(i)
                P.mm(O[0:M, :], vfn(pl[i][0]), pb, start=(i == 0), stop=(i == n - 1))
                if extra is not None:
                    em, efn = extra
                    P.mm(O2[0:em, :], efn(pl[i][0]), pb, start=(i == 0), stop=(i == n - 1))
            return O, O2

        def bcast_rden(O, dr, gate_ps=None, guard=False):
            k = sh["fc"]
            sh["fc"] += 1
            r1 = sh["rr"][k % 2]
            P.copy(r1[dr:dr + 1, :], O[dr:dr + 1, :], eng="act")
            B = ps("X")
            P.mm(B, ones_f[dr:dr + 1, :], r1[dr:dr + 1, :])
            Bs = sh["bs"][k % 2]
            if guard:
                P.ts(Bs, B, 1e-30, None, ALU.max)
                P.recip(Bs, Bs)
            else:
                P.recip(Bs, B)
            if gate_ps is not None:
                P.tt(Bs, Bs, gate_ps, ALU.mult)
            return Bs

        def gate_store(l, ci, ocT, slot_id):
            w = lw(l, C_Z + ci * 128, 128)

            def sink(tc, pb):
                sz = sh["sz"][sh["szc"] % 2]
                sh["szc"] += 1
                P.act(sz, pb, AF.Silu)
                sl = ocT[:, tc * 512:(tc + 1) * 512]
                P.tt(sl, sl, sz, ALU.mult, eng="pool")
            proj_fm(None, w, 0, 128, sink=sink)
            P.dma(V(oT_ap[ci * 128:(ci + 1) * 128, :], oT_bufs[ci]), ocT, eng="sp", chan="oc%d" % slot_id)

        def phase_norm_T(src_ap, src_bufs, ntok, g_sb, dstT):
            xsl = [A.alloc([128, D], F32, "xs%d" % i) for i in range(4)]
            xnl = [A.alloc([128, D], BF16, "xn%d" % i) for i in range(3)]
            junk = A.alloc([128, D], BF16, "junk")
            stt_ = [A.alloc([128, 4], F32, "st%d" % i) for i in range(6)]
            g3 = V(g_sb.ap.rearrange("p (c o) -> p c o", o=1).to_broadcast([128, 8, 128]), g_sb.buf)
            n = ntok // 128

            def stage_a(tt):
                xs = xsl[tt % 4]
                P.dma(xs, V(src_ap[tt * 128:(tt + 1) * 128, :], src_bufs[tt]), eng="sp", chan="xs%d" % (tt % 4))
                stt = stt_[tt % 6]
                P.act(junk, xs, AF.Square, accum=stt[:, 0:1])
                P.ts(stt[:, 1:2], stt[:, 0:1], 1.0 / D, EPS, ALU.mult, ALU.add)
                P.tt(stt[:, 2:3], stt[:, 1:2], cf("neghalf"), ALU.pow, eng="pool")

            def stage_b(tt):
                P.ts(xnl[tt % 3], xsl[tt % 4], stt_[tt % 6][:, 2:3], None, ALU.mult)

            def stage_c(tt):
                xn = xnl[tt % 3]
                pt = ps("X").bc(BF16)
                for c in range(8):
                    P.tr(pt[:, c * 128:(c + 1) * 128], xn[:, c * 128:(c + 1) * 128], ident_bf)
                P.tt(dstT[:, :, tt * 128:(tt + 1) * 128], pt.re("p (c t) -> p c t", c=8), g3, ALU.mult)

            for step in range(n + 2):
                if step < n:
                    stage_a(step)
                if 1 <= step <= n:
                    stage_b(step - 1)
                if step >= 2:
                    stage_c(step - 2)

        def phase_out(l, src_ap, src_bufs, dst_ap, dst_bufs):
            wout = A.alloc([128, 16, D], BF16, "wout")
            w3 = V(w_out_d.ap[l].rearrange("(c p) n -> p c n", p=128), w_out_d.buf)
            for c0 in range(0, 16, 2):
                P.dma(wout[:, c0:c0 + 2, :], w3[:, c0:c0 + 2, :], eng="pool", chan="wout")
            gp = A.alloc([128, D], F32, "gpost")
            P.dma(gp, V(post_g_d.ap[l:l + 1, :].to_broadcast([128, D]), post_g_d.buf), eng="sp", chan="gpost")
            xsl = [A.alloc([128, D], F32, "xo%d" % i) for i in range(2)]
            otl = [A.alloc([128, 16, 128], BF16, "ot%d" % i) for i in range(2)]
            ynl = [A.alloc([128, D], F32, "yn%d" % i) for i in range(2)]
            junk = A.alloc([128, 512], BF16, "junk2")
            stt_ = [A.alloc([128, 4], F32, "sto%d" % i) for i in range(4)]
            oT3 = oT_ap.rearrange("(c p) t -> p c t", p=128)
            for tt in range(32):
                ot = otl[tt % 2]
                P.dma(ot, V(oT3[:, :, tt * 128:(tt + 1) * 128], oT_bufs[0]), eng="sp", chan="ot%d" % (tt % 2), extra_reads=oT_bufs[1:])
                xs = xsl[tt % 2]
                P.dma(xs, V(src_ap[tt * 128:(tt + 1) * 128, :], src_bufs[tt]), eng="sp", chan="xo%d" % (tt % 2))
                stt = stt_[tt % 4]
                pbs_ = []
                for half in range(2):
                    pb = banks[(tt * 2 + half) % 8]
                    pbs_.append(pb)
                    for c in range(16):
                        P.mm(pb, ot[:, c, :], wout[:, c, half * 512:(half + 1) * 512], start=(c == 0), stop=(c == 15))
                    P.act(junk, pb, AF.Square, accum=stt[:, half:half + 1])
                P.tt(stt[:, 2:3], stt[:, 0:1], stt[:, 1:2], ALU.add)
                P.ts(stt[:, 2:3], stt[:, 2:3], 1.0 / D, EPS, ALU.mult, ALU.add)
                P.tt(stt[:, 3:4], stt[:, 2:3], cf("neghalf"), ALU.pow, eng="pool")
                yn = ynl[tt % 2]
                for half in range(2):
                    hs = slice(half * 512, (half + 1) * 512)
                    P.stt(yn[:, hs], pbs_[half], stt[:, 3:4], gp[:, hs], ALU.mult, ALU.mult)
                P.tt(yn, yn, xs, ALU.add, eng="pool")
                P.dma(V(dst_ap[tt * 128:(tt + 1) * 128, :], dst_bufs[tt]), yn, eng="sp", chan="sto%d" % (tt % 2))

        def phase_mem(l):
            memT = A.alloc([128, 8, 256], BF16, "memT")
            m0 = A.top
            phase_norm_T(mem_d.ap, [mem_d.buf] * 2, 256, gmem[:, l * 8:(l + 1) * 8], memT)
            A.top = m0
            P.barrier()
            vtm = A.alloc([128, 2, 512], BF16, "memv")
            kTm = A.alloc([128, 256], BF16, "memk")
            qTm = A.alloc([128, S], BF16, "memq")
            rdn = [A.alloc([128, 512], F32, "rdn%d" % i) for i in range(2)]
            for j in range(4):
                w = lw(l, 512 + j * 128, 128, src=w_mem_d)

                def sinkv(tt, pb, j=j):
                    evac(vtm[:, tt, j * 128:(j + 1) * 128], pb[:, 0:128])
                proj_tm(w, 128, sinkv, src=memT, ntile=2)
            sc = 128.0 ** -0.5
            for h in range(4):
                wk = lw(l, h * 128, 128, src=w_mem_d)
                pb = ps("X")
                for c in range(8):
                    P.mm(pb[:, 0:256], wk[:, c, :], memT[:, c, :], start=(c == 0), stop=(c == 7))
                evac(kTm, pb[:, 0:256])
                wq = lw(l, C_CQ + h * 128, 128)
                proj_fm(qTm, wq, 0, 128)
                oc = sh["oc"][h % 2]
                for qc in range(8):
                    O, O2 = attn_chunk(qc, qTm, kTm, [(0, []), (1, [])], lambda kt, h=h: vtm[:, kt, h * 128:(h + 1) * 128], 128, sc,
                                       extra=(128, lambda kt: ones_bf))
                    r = rdn[qc % 2]
                    P.recip(r, O2)
                    P.tt(oc[:, qc * 512:(qc + 1) * 512], O, r, ALU.mult)
                gate_store(l, 12 + h, oc, h % 2)

        def phase_moba(l):
            qh = A.alloc([128, S], BF16, "mqh")
            kh = A.alloc([128, S], BF16, "mkh")
            vaug = A.alloc([128, 32, 193], BF16, "mvaug")
            kmf = A.alloc([128, 16], F32, "kmf")
            kmb = A.alloc([128, 16], BF16, "kmb")
            gsb = A.alloc([128, 512], F32, "gsb")
            m8 = A.alloc([128, 256], F32, "m8")
            a1 = A.alloc([128, 512], F32, "a1")
            mv = A.alloc([128, 512], BF16, "mv")
            mv64 = A.alloc([128, 2048], BF16, "mv64")
            mvp = A.alloc([128, 8 * 128], BF16, "mvp")
            P.memset(vaug[:, :, 64:66], 1.0)
            P.memset(vaug[:, :, 66:129], 0.0)
            P.memset(mvp, 0.0)
            otc = CB["tcaus"][0]
            oes = CB["eslc"][0]
            for pair in range(4):
                wv = lw(l, C_MV + pair * 128, 128)

                def sinkv(tt, pb):
                    evac(vaug[:, tt, 0:64], pb[:, 0:64])
                    evac(vaug[:, tt, 129:193], pb[:, 64:128])
                proj_tm(wv, 128, sinkv)
                oc = sh["oc"][pair % 2]
                for hh in range(2):
                    h = 2 * pair + hh
                    r0 = 64 * hh
                    rows = slice(r0, r0 + 64)
                    orows = slice(64 - r0, 128 - r0)
                    proj_fm(qh, lw(l, C_MQ + h * 64, 64), r0, 64)
                    proj_fm(kh, lw(l, C_MK + h * 64, 64), r0, 64)
                    P.copy(kh[orows, :], cbf[orows, oes:oes + S], eng="pool")
                    o_, i_ = kmf.ap[rows], kh.ap[rows].rearrange("p (b k) -> p b k", k=256)
                    P.add("dve", lambda e, o_=o_, i_=i_: e.tensor_reduce(o_, i_, AX.X, ALU.add), reads=[kh.buf], writes=[kmf.buf])
                    P.ts(kmb[rows], kmf[rows], 1.0 / 256, None, ALU.mult)
                    G = ps("X")
                    for qt in range(32):
                        P.mm(G[:, qt * 16:(qt + 1) * 16], qh[rows, qt * 128:(qt + 1) * 128], kmb[rows, :], start=True, stop=True)
                    P.tt(gsb, G, cf("pastm"), ALU.add)
                    for qt in range(32):
                        o_, i_ = m8.ap[:, qt * 8:(qt + 1) * 8], gsb.ap[:, qt * 16:(qt + 1) * 16]
                        P.add("dve", lambda e, o_=o_, i_=i_: e.max(o_, i_), reads=[gsb.buf], writes=[m8.buf])
                    thr = V(m8.ap.rearrange("p (q e) -> p q e", e=8)[:, :, 2:3].to_broadcast([128, 32, 16]), m8.buf)
                    P.tt(a1.re("p (q b) -> p q b", b=16), gsb.re("p (q b) -> p q b", b=16), thr, ALU.is_ge)
                    P.stt(a1, a1, BIGM, cf("ownm2"), ALU.mult, ALU.add)
                    P.ts(mv, a1, 0.0, None, ALU.min)
                    P.copy(mv64.re("p (c r) -> p c r", r=4),
                           V(mv.ap.rearrange("p (c o) -> p c o", o=1).to_broadcast([128, 512, 4]), mv.buf), eng="dve")
                    c0 = 64 - r0
                    for g4 in range(4):
                        P.copy(mvp.re("p (j c) -> p j c", c=128)[:, :, c0:c0 + 64],
                               mv64[:, g4 * 512:(g4 + 1) * 512].re("p (j c) -> p j c", c=64), eng="pool")
                        pt = ps("X").bc(BF16)
                        for j in range(8):
                            P.tr(pt[:, j * 128:(j + 1) * 128], mvp[:, j * 128:(j + 1) * 128], ident_bf)
                        evac(qh[orows, g4 * 1024:(g4 + 1) * 1024], pt[orows, :])
                    if hh == 0:
                        vfn = lambda kt: vaug[:, kt, 0:65]
                        M, dr = 65, 64
                    else:
                        vfn = lambda kt: vaug[:, kt, 65:193]
                        M, dr = 128, 0
                    for qc in range(8):
                        pl = []
                        for kt in range(4 * qc + 4):
                            masks = []
                            if kt >= 4 * qc:
                                delta = 128 * kt - 512 * qc
                                masks.append((ident_bf, cbf[:, otc + 384 - delta:otc + 384 - delta + 512]))
                            pl.append((kt, masks))
                        O, _ = attn_chunk(qc, qh, kh, pl, vfn, M, 0.125)
                        Bs = bcast_rden(O, dr)
                        P.tt(oc[rows, qc * 512:(qc + 1) * 512], O[rows, :], Bs[rows, :], ALU.mult)
                gate_store(l, pair, oc, pair % 2)

        def phase_nsa(l):
            oc = sh["oc"][0]
            kcTs = [A.alloc([128, 256], BF16, "kcT%d" % i) for i in range(2)]
            for t_ in kcTs:
                P.memset(t_, 0.0)
            vcaug = A.alloc([128, 4, 129], BF16, "vcaug")
            Gt = A.alloc([128, S], BF16, "Gt")
            qT = A.alloc([128, S], BF16, "nqT")
            P.memset(qT, 0.0)
            acc = [A.alloc([128, 512], F32, "acc%d" % i) for i in range(2)]
            tmpb = [A.alloc([128, 512], F32, "tmpb%d" % i) for i in range(2)]
            m1 = A.top
            kin = A.alloc([128, S], BF16, "kin")
            vin = A.alloc([128, S], BF16, "vin")
            proj_fm(kin, lw(l, C_NKC, 128), 0, 128)
            proj_fm(vin, lw(l, C_NVC, 128), 0, 128)
            P.memset(vcaug[:, :, 0:1], 1.0)
            P.memset(vcaug[:, :, 1:64], 0.0)
            P.memset(vcaug[:, :, 128:129], 1.0)
            w1 = A.alloc([128, 32, 128], BF16, "w1")
            peT = A.alloc([128, 32], BF16, "peT")
            w2 = A.alloc([128, 64], BF16, "w2")
            bias = A.alloc([128, 2], F32, "cbias")
            hid = A.alloc([128, 256], BF16, "hid")
            for which, pe_d, w1_d, w2_d, xin in (("k", pe_k_d, w1_k_d, w2_k_d, kin), ("v", pe_v_d, w1_v_d, w2_v_d, vin)):
                s3 = V(w1_d.ap[l].rearrange("(l d) j -> d l j", d=64), w1_d.buf)
                for hlf in range(2):
                    for l0 in range(0, 32, 8):
                        P.dma(w1[hlf * 64:(hlf + 1) * 64, l0:l0 + 8, :], s3[:, l0:l0 + 8, :], eng="pool", chan="w1")
                    P.dma(peT[hlf * 64:(hlf + 1) * 64, :], V(pe_d.ap[l].rearrange("l d -> d l"), pe_d.buf), eng="pool", chan="w1", noncontig=True)
                P.dma(w2, V(w2_d.ap[l], w2_d.buf), eng="pool", chan="w1")
                for g in range(2):
                    rows = slice(64 * g, 64 * g + 64)
                    pbias = ps("X")
                    for l_ in range(32):
                        P.mm(pbias[:, 0:1], w1[rows, l_, :], peT[rows, l_:l_ + 1], start=(l_ == 0), stop=(l_ == 31))
                    P.copy(bias[:, g:g + 1], pbias[:, 0:1], eng="dve")
                    ph = ps("X")
                    x3 = V(xin.ap[rows].rearrange("p (i s) -> p i s", s=16), xin.buf)
                    for l_ in range(32):
                        P.mm(ph[:, 0:255], w1[rows, l_, :], x3[:, l_ // 16:l_ // 16 + 255, l_ % 16], start=(l_ == 0), stop=(l_ == 31))
                    P.memset(hid[:, 255:256], 0.0, eng="dve")
                    P.act(hid[:, 0:255], ph[:, 0:255], AF.Silu, bias=bias[:, g:g + 1])
                    if which == "k":
                        pk = ps("X")
                        P.mm(pk[rows, 0:256], w2, hid)
                        evac(kcTs[g][rows, :], pk[rows, 0:256])
                    else:
                        for t in range(2):
                            pv = ps("X")
                            P.mm(pv[:, 0:64], hid[:, t * 128:(t + 1) * 128], w2)
                            evac(vcaug[:, g * 2 + t, 64:128], pv[:, 0:64])
            if NSA_STOP <= 1:
                return
            wg = lw(l, C_NG, 24)
            P.memset(Gt, 0.0)
            sg = A.alloc([56, 512], F32, "sg")
            hi2 = A.alloc([56, 512], BF16, "hi2")
            for tc in range(8):
                pb = ps("X")
                for r0 in (0, 32):
                    for c in range(8):
                        P.mm(pb[r0:r0 + 24, :], wg[:, c, :], hT[:, c, tc * 512:(tc + 1) * 512], start=(c == 0), stop=(c == 7))
                    P.act(sg[r0:r0 + 24, :], pb[r0:r0 + 24, :], AF.Sigmoid)
                P.copy(Gt[0:24, tc * 512:(tc + 1) * 512], sg[0:24, :], eng="dve")
                P.copy(hi2[32:56, :], sg[32:56, :], eng="dve")
                P.tt(Gt[32:56, tc * 512:(tc + 1) * 512], sg[32:56, :], hi2[32:56, :], ALU.subtract)
            if NSA_STOP <= 2:
                return
            A.top = m1
            P.barrier()
            ksT = A.alloc([128, S], BF16, "ksT")
            kwT = A.alloc([128, S], BF16, "kwT")
            m2 = A.top
            ocaug = CB["caug"][0]
            otcmp = CB["tcmp"][0]
            otc = CB["tcaus"][0]
            otw = CB["twin"][0]
            oes = CB["eslc"][0]
            osel = CB["sel"][0]

            def pairs_cmp(qc):
                pl = []
                if qc >= 5:
                    pl.append((0, []))
                else:
                    d0 = 512 * qc
                    pl.append((0, [(ident_bf, cbf[:, otcmp + d0:otcmp + d0 + 512])]))
                if qc >= 4:
                    d1 = 512 * qc - 2048
                    pl.append((1, [(ident_bf, cbf[:, otcmp + d1:otcmp + d1 + 512])]))
                return pl

            for g in NSA_G:
                rows = slice(64 * g, 64 * g + 64)
                orows_g = slice(64 - 64 * g, 128 - 64 * g)
                A.top = m2
                P.barrier()
                impT = A.alloc([64, S], F32, "impT")
                imp2 = A.alloc([128, 64], F32, "imp2")
                tmp2 = A.alloc([128, 64], F32, "tmp2")
                m8a = A.alloc([128, 8], F32, "m8a")
                m8b = A.alloc([128, 8], F32, "m8b")
                mvb = A.alloc([128, 8 * 128], BF16, "mvb")
                P.memset(mvb, 0.0)
                for p in range(4):
                    h = 4 * g + p
                    proj_fm(qT, lw(l, C_NQ + h * 64, 64), 64 * g, 64)
                    for qc in range(8):
                        O, _ = attn_chunk(qc, qT, kcTs[g], pairs_cmp(qc),
                                          lambda kt: cbf[:, ocaug + kt * 65:ocaug + kt * 65 + 65], 65, 0.125)
                        Bs = bcast_rden(O, 64, guard=True)
                        sl = impT[0:64, qc * 512:(qc + 1) * 512]
                        if p == 0:
                            P.tt(sl, O[0:64, :], Bs[0:64, :], ALU.mult)
                        else:
                            tb = tmpb[qc % 2]
                            P.tt(tb[0:64, :], O[0:64, :], Bs[0:64, :], ALU.mult)
                            P.tt(sl, sl, tb[0:64, :], ALU.add, eng="pool")
                if NSA_STOP <= 3:
                    return
                ofb = CF["fbt"][0]
                for g8 in range(4):
                    for j in range(8):
                        qt = g8 * 8 + j
                        pt = ps("X")
                        P.tr(pt[:, 0:64], impT[0:64, qt * 128:(qt + 1) * 128], ident_f[0:64, 0:64])
                        P.tt(imp2, pt[:, 0:64], cff[:, ofb + 63 - 2 * qt:ofb + 127 - 2 * qt], ALU.add)
                        P.memset(imp2[:, 0:1], 1000.0, eng="dve")
                        a_, b_, c_, d_ = m8a.ap, imp2.ap, tmp2.ap, m8b.ap
                        P.add("dve", lambda e, a_=a_, b_=b_: e.max(a_, b_), reads=[imp2.buf], writes=[m8a.buf])
                        P.add("dve", lambda e, a_=a_, b_=b_, c_=c_: e.match_replace(c_, a_, b_, -1e30), reads=[imp2.buf, m8a.buf], writes=[tmp2.buf])
                        P.add("dve", lambda e, c_=c_, d_=d_: e.max(d_, c_), reads=[tmp2.buf], writes=[m8b.buf])
                        P.ts(mvb[:, j * 128 + 64 - 64 * g:j * 128 + 128 - 64 * g], imp2, m8b[:, 7:8], -BIGM, ALU.is_lt, ALU.mult)
                    pt = ps("X").bc(BF16)
                    for j in range(8):
                        P.tr(pt[:, j * 128:(j + 1) * 128], mvb[:, j * 128:(j + 1) * 128], ident_bf)
                    evac(qT[orows_g, g8 * 1024:(g8 + 1) * 1024], pt[orows_g, :])
                if NSA_STOP <= 4:
                    return
                P.barrier()
                A.top = m2
                vsa = A.alloc([128, 32, 129], BF16, "vsa")
                vwa = A.alloc([128, 32, 129], BF16, "vwa")
                for t_ in (vsa, vwa):
                    P.memset(t_[:, :, 0:1], 1.0)
                    P.memset(t_[:, :, 1:64], 0.0)
                    P.memset(t_[:, :, 128:129], 1.0)
                proj_fm(ksT, lw(l, C_NKS + 64 * g, 64), 64 * g, 64)
                P.copy(ksT[orows_g, :], cbf[orows_g, oes:oes + S], eng="pool")
                proj_fm(kwT, lw(l, C_NKW + 64 * g, 64), 64 * g, 64)
                P.memset(kwT[orows_g, :], 0.0)
                for cbase, dstv in ((C_NVS, vsa), (C_NVW, vwa)):
                    wv = lw(l, cbase + 64 * g, 64)

                    def sinkv(tt, pb, dstv=dstv):
                        evac(dstv[:, tt, 64:128], pb[:, 0:64])
                    proj_tm(wv, 64, sinkv)
                if NSA_STOP <= 5:
                    return
                for p in range(4):
                    if NSA_STOP <= 9 and p >= NSA_STOP - 5:
                        return
                    h = 4 * g + p
                    par = p % 2
                    orow = slice(64 * par, 64 * par + 64)
                    proj_fm(qT, lw(l, C_NQ + h * 64, 64), 64 * g, 64)
                    if par == 0:
                        c0, c1, M, dr = 64, 129, 65, 64
                    else:
                        c0, c1, M, dr = 0, 128, 128, 0
                    for qc in range(8):
                        ac = acc[qc % 2]
                        for j in range(3):
                            if j not in NSA_BR:
                                continue
                            if j == 0:
                                pl = pairs_cmp(qc)
                                kTj = kcTs[g]
                                vfn = lambda kt: vcaug[:, g * 2 + kt, c0:c1]
                            elif j == 1:
                                pl = []
                                for kt in range(4 * qc + 4):
                                    masks = []
                                    if kt >= 4 * qc:
                                        delta = 128 * kt - 512 * qc
                                        masks.append((ident_bf, cbf[:, otc + 384 - delta:otc + 384 - delta + 512]))
                                    pl.append((kt, masks))
                                kTj = ksT
                                vfn = lambda kt: vsa[:, kt, c0:c1]
                            else:
                                pl = []
                                for kt in range(max(0, 4 * qc - 4), 4 * qc + 4):
                                    delta = 128 * kt - 512 * qc
                                    pl.append((kt, [(ident_bf, cbf[:, otw + 384 - delta:otw + 384 - delta + 512])]))
                                kTj = kwT
                                vfn = lambda kt: vwa[:, kt, c0:c1]
                            O, _ = attn_chunk(qc, qT, kTj, pl, vfn, M, 0.125)
                            idx = h * 3 + j
                            Gb = None
                            if not NSA_NOGATE:
                                Gb = ps("X")
                                P.mm(Gb, cbf[:, osel + idx * 128:osel + (idx + 1) * 128], Gt[:, qc * 512:(qc + 1) * 512])
                            Bs = bcast_rden(O, dr, gate_ps=Gb, guard=(j == 0))
                            if j == 0:
                                P.tt(ac[orow, :], O[orow, :], Bs[orow, :], ALU.mult)
                            else:
                                tb = tmpb[j % 2]
                                P.tt(tb[orow, :], O[orow, :], Bs[orow, :], ALU.mult)
                                if j == 1:
                                    P.tt(ac[orow, :], ac[orow, :], tb[orow, :], ALU.add, eng="pool")
                                else:
                                    P.tt(oc[orow, qc * 512:(qc + 1) * 512], ac[orow, :], tb[orow, :], ALU.add, eng="pool")
                    if par == 1:
                        gate_store(l, 4 + h // 2, oc, 0)
                if NSA_STOP <= 10:
                    return

        def phase_ret(l):
            qfT = A.alloc([128, S], BF16, "qfT")
            qcT = A.alloc([128, S], BF16, "qcT")
            kfT = A.alloc([128, S], BF16, "kfT")
            kdtm = A.alloc([128, 32, 128], BF16, "kdtm")
            vtm = A.alloc([128, 32, 128], BF16, "rvtm")
            Rb = [A.alloc([128, 128], BF16, "Rb%d" % i) for i in range(8)]
            Rf = A.alloc([128, 128], F32, "Rf")
            cs = [A.alloc([128, 512], F32, "cs%d" % i) for i in range(2)]
            t12 = [A.alloc([128, 512], F32, "t12%d" % i) for i in range(2)]
            sm = [A.alloc([128, 512], BF16, "sm%d" % i) for i in range(2)]
            on = [A.alloc([128, 128], BF16, "on%d" % i) for i in range(2)]
            bst = A.alloc([128, 4 * 6], F32, "bst")
            bag = A.alloc([128, 4 * 2], F32, "bag")
            rsd = A.alloc([128, 8], F32, "rsd")
            cross3 = lambda pair: V(cff.ap[:, CF["cross"][0] + pair * 128:CF["cross"][0] + (pair + 1) * 128]
                                    .rearrange("p (o i) -> p o i", o=1).to_broadcast([128, 4, 128]), cff.buf)
            for pair in range(2):
                for which, cbase, dst in (("q", C_RQ, qfT), ("k", C_RK, kfT)):
                    wx, slx = load_w(l, cbase + pair * 128, 128)
                    wy, sly = load_w(l, cbase + pair * 128 + 32, 32, dstoff=0, neg=True)
                    load_w(l, cbase + pair * 128, 32, dstoff=32, slot=sly)
                    load_w(l, cbase + pair * 128 + 96, 32, dstoff=64, slot=sly, neg=True)
                    wy, _ = load_w(l, cbase + pair * 128 + 64, 32, dstoff=96, slot=sly)
                    for tc in range(8):
                        px = ps("X")
                        py = ps("X")
                        for c in range(8):
                            P.mm(px, wx[:, c, :], hT[:, c, tc * 512:(tc + 1) * 512], start=(c == 0), stop=(c == 7))
                        for c in range(8):
                            P.mm(py, wy[:, c, :], hT[:, c, tc * 512:(tc + 1) * 512], start=(c == 0), stop=(c == 7))
                        P.dma(cs[0], rc_d[:, tc * 512:(tc + 1) * 512], eng="sp", chan="cs0")
                        P.dma(cs[1], rs_d[:, tc * 512:(tc + 1) * 512], eng="sp", chan="cs1")
                        P.tt(t12[0], px, cs[0], ALU.mult)
                        P.tt(t12[1], py, cs[1], ALU.mult)
                        sl = dst[:, tc * 512:(tc + 1) * 512]
                        P.tt(sl, t12[0], t12[1], ALU.add, eng="pool")
                        if which == "q":
                            P.tt(qcT[:, tc * 512:(tc + 1) * 512].re("p (n i) -> p n i", i=128), sl.re("p (n i) -> p n i", i=128),
                                 cross3(pair), ALU.mult, eng="pool")
                for hh in range(2):
                    h = 2 * pair + hh
                    rows = slice(64 * hh, 64 * hh + 64)
                    dec = RET_G[h] ** 128.0
                    wv = lw(l, C_RV + h * 128, 128)

                    def sinkv(tt, pb):
                        evac(vtm[:, tt, :], pb[:, 0:128])
                    proj_tm(wv, 128, sinkv)
                    okd = CF["kdec"][0]
                    for n8 in range(4):
                        pt = ps("X").bc(BF16)
                        for j in range(8):
                            n = n8 * 8 + j
                            P.tr(pt[:, j * 64:(j + 1) * 64], kfT[rows, n * 128:(n + 1) * 128], cbf[rows, CB["ident"][0] + 64 * hh:CB["ident"][0] + 64 * hh + 64])
                        P.ts(kdtm[:, n8 * 8:(n8 + 1) * 8, 64 * hh:64 * hh + 64], pt[:, 0:512].re("p (n d) -> p n d", d=64),
                             cff[:, okd + h:okd + h + 1], None, ALU.mult)
                    oc = sh["oc"][h % 2]
                    P.memset(Rf[rows, :], 0.0, eng="dve")
                    P.memset(Rb[0][rows, :], 0.0, eng="dve")
                    oin = CF["intra"][0]
                    intra3 = V(cff.ap[:, oin + h * 128:oin + (h + 1) * 128].rearrange("p (o i) -> p o i", o=1).to_broadcast([128, 4, 128]), cff.buf)
                    for n4 in range(8):
                        pS = ps("S")
                        for k in range(4):
                            n = n4 * 4 + k
                            P.mm(pS[:, k * 128:(k + 1) * 128], kfT[rows, n * 128:(n + 1) * 128], qfT[rows, n * 128:(n + 1) * 128])
                        smt = sm[n4 % 2]
                        P.tt(smt.re("p (k i) -> p k i", i=128), pS.re("p (k i) -> p k i", i=128), intra3, ALU.mult)
                        pU = ps("X")
                        for k in range(4):
                            n = n4 * 4 + k
                            P.mm(pU[rows, k * 128:(k + 1) * 128], kdtm[:, n, 64 * hh:64 * hh + 64], vtm[:, n, :])
                        pO = ps("O")
                        for k in range(4):
                            n = n4 * 4 + k
                            Rcur = Rb[n % 8]
                            P.mm(pO[:, k * 128:(k + 1) * 128], smt[:, k * 128:(k + 1) * 128], vtm[:, n, :], start=True, stop=(n == 0))
                            if n > 0:
                                P.mm(pO[:, k * 128:(k + 1) * 128], qcT[rows, n * 128:(n + 1) * 128], Rcur[rows, :], start=False, stop=True)
                            if n < 31:
                                P.stt(Rf[rows, :], Rf[rows, :], dec, pU[rows, k * 128:(k + 1) * 128], ALU.mult, ALU.add)
                                P.copy(Rb[(n + 1) % 8][rows, :], Rf[rows, :], eng="act")
                        for k in range(4):
                            o_, i_ = bst.ap[:, k * 6:(k + 1) * 6], pO.ap[:, k * 128:(k + 1) * 128]
                            P.add("dve", lambda e, o_=o_, i_=i_: e.bn_stats(o_, i_), reads=[pO.buf], writes=[bst.buf])
                            o2_, i2_ = bag.ap[:, k * 2:(k + 1) * 2], bst.ap[:, k * 6:(k + 1) * 6]
                            P.add("dve", lambda e, o2_=o2_, i2_=i2_: e.bn_aggr(o2_, i2_), reads=[bst.buf], writes=[bag.buf])
                        bag3 = bag.re("p (k t) -> p k t", t=2)
                        P.ts(rsd[:, 0:4], bag3[:, :, 1], EPS, None, ALU.add)
                        P.tt(rsd[:, 4:8], rsd[:, 0:4], V(cff.ap[:, CF["neghalf"][0]:CF["neghalf"][0] + 1].to_broadcast([128, 4]), cff.buf), ALU.pow, eng="pool")
                        pt = ps("X").bc(BF16)
                        for k in range(4):
                            n = n4 * 4 + k
                            ont = on[k % 2]
                            P.ts(ont, pO[:, k * 128:(k + 1) * 128], bag[:, 2 * k:2 * k + 1], rsd[:, 4 + k:5 + k], ALU.subtract, ALU.mult)
                            P.tr(pt[:, k * 128:(k + 1) * 128], ont, ident_bf)
                        P.ts(oc[:, n4 * 512:(n4 + 1) * 512], pt[:, 0:512], retg[:, l * 4 + h:l * 4 + h + 1], None, ALU.mult)
                    gate_store(l, 8 + h, oc, h % 2)

        fns = {"mem": phase_mem, "moba": phase_moba, "nsa": phase_nsa, "ret": phase_ret}
        for l in range(nl):
            if l == 0:
                src_ap, src_bufs = x_d.ap, [x_d.buf] * 32
            else:
                src_ap, src_bufs = x1_ap, x1_bufs
            if l == nl - 1:
                dst_ap, dst_bufs = out_ap, out_bufs
            else:
                dst_ap, dst_bufs = x1_ap, x1_bufs
            P.barrier()
            A.top = mark0
            phase_norm_T(src_ap, src_bufs, S, gpre[:, l * 8:(l + 1) * 8], hT)
            for ph in phases:
                new_phase(n_oc=(1 if ph == "nsa" else 2))
                fns[ph](l)
            P.barrier()
            A.top = mark0
            phase_out(l, src_ap, src_bufs, dst_ap, dst_bufs)
        print("arena peak words", A.peak, "of", NW, "ops", {e: len(P.ops[e]) for e in ENGS}, "chans", len(P.chans), flush=True)
        P.emit(st)
    return nc


_CACHE = {}


def kernel(**inputs):
    n = 8
    if "nc" not in _CACHE:
        _CACHE["nc"] = build()
        _CACHE["consts"] = make_consts()
    nc = _CACHE["nc"]
    cbv, cfv, rc, rs = _CACHE["consts"]
    f = lambda a: np.ascontiguousarray(np.asarray(a, dtype=np.float32))
    shared = {k: f(inputs[k]) for k in ("pre_norm_g", "post_norm_g", "mem_norm_g", "w_in", "w_mem_kv", "nsa_pe_k", "nsa_w1_k",
                                        "nsa_w2_k", "nsa_pe_v", "nsa_w1_v", "nsa_w2_v", "ret_gn_g", "w_out")}
    shared.update(cb=cbv, cf=cfv, rc=rc, rs=rs)
    x = f(inputs["x"])
    mem = f(inputs["mem"])
    in_maps = []
    for i in range(n):
        m = dict(shared)
        m["x"] = np.ascontiguousarray(x[i])
        m["mem"] = np.ascontiguousarray(mem[i])
        in_maps.append(m)
    res = run_bass_kernel_spmd(nc, in_maps, core_ids=list(range(n)))
    return np.stack([np.asarray(r["out"], dtype=np.float32) for r in res.results], axis=0)
```

```python
import numpy as np
import ml_dtypes
import concourse.bass as bass
import concourse.mybir as mybir
from concourse.bass_utils import run_bass_kernel_spmd

F32 = mybir.dt.float32
BF16 = mybir.dt.bfloat16
AF = mybir.ActivationFunctionType
ALU = mybir.AluOpType
AX = mybir.AxisListType

S = 4096
D = 1024
NL = 2
INC = 6424
NEG = -30000.0
EPS = 1e-6

C_MQ, C_MK, C_MV = 0, 512, 1024
C_NQ = 1536
C_NKC, C_NVC, C_NKS, C_NVS, C_NKW, C_NVW = 2048, 2176, 2304, 2432, 2560, 2688
C_NG = 2816
C_RQ, C_RK, C_RV = 2840, 3096, 3352
C_CQ = 3864
C_Z = 4376


class Buf:
    __slots__ = ("name", "w", "r")

    def __init__(self, name=""):
        self.name = name
        self.w = None
        self.r = []


class V:
    __slots__ = ("ap", "buf")

    def __init__(self, ap, buf):
        self.ap = ap
        self.buf = buf

    def __getitem__(self, k):
        return V(self.ap[k], self.buf)

    def re(self, s, **kw):
        return V(self.ap.rearrange(s, **kw), self.buf)

    def bc(self, dt):
        return V(self.ap.bitcast(dt), self.buf)


class Op:
    __slots__ = ("eng", "fn", "deps", "signal", "sem", "val", "is_dma", "chan", "gidx")

    def __init__(self, eng, fn, is_dma=False, chan=None):
        self.eng = eng
        self.fn = fn
        self.deps = []
        self.signal = False
        self.sem = None
        self.val = 0
        self.is_dma = is_dma
        self.chan = chan


ENGS = ("pe", "act", "dve", "pool", "sp")


class Prog:
    def __init__(self, nc):
        self.nc = nc
        self.ops = {e: [] for e in ENGS}
        self.all = []
        self.chans = {}
        self.chan_last = {}
        self.bar = None
        self.bar_pending = set()

    def add(self, eng, fn, reads=(), writes=(), is_dma=False, chan=None):
        op = Op(eng, fn, is_dma, chan)
        deps = []
        for b in reads:
            if b.w is not None:
                deps.append((b.w, "raw"))
        for b in writes:
            if b.w is not None:
                deps.append((b.w, "waw"))
            for r in b.r:
                deps.append((r, "war"))
        if eng in self.bar_pending:
            for d in self.bar:
                deps.append((d, "bar"))
            self.bar_pending.discard(eng)
        seen = set()
        for d, kind in deps:
            if d is op or id(d) in seen:
                continue
            if not self._needs(d, op, kind):
                continue
            seen.add(id(d))
            d.signal = True
            op.deps.append(d)
        for b in reads:
            b.r.append(op)
        for b in writes:
            b.w = op
            b.r = []
        op.gidx = len(self.all)
        self.all.append(op)
        self.ops[eng].append(op)
        return op

    @staticmethod
    def _needs(d, op, kind):
        if d.is_dma:
            return True
        if op.is_dma:
            return True
        if d.eng != op.eng:
            return True
        if d.eng == "pe":
            return False
        return True

    def barrier(self):
        last = []
        for e in ENGS:
            seen_compute = False
            for op in reversed(self.ops[e]):
                if op.is_dma:
                    last.append(op)
                elif not seen_compute:
                    last.append(op)
                    seen_compute = True
                if len(last) > 4000:
                    break
        per = {}
        out = []
        for op in last:
            if op.is_dma:
                if op.chan not in per:
                    per[op.chan] = op
                    out.append(op)
            else:
                out.append(op)
        self.bar = out
        self.bar_pending = set(ENGS)

    def mm(self, out, lhsT, rhs, start=True, stop=True, extra_reads=()):
        o, l, r = out.ap, lhsT.ap, rhs.ap
        return self.add("pe", lambda e: e.matmul(o, l, r, start=start, stop=stop),
                        reads=[lhsT.buf, rhs.buf] + list(extra_reads), writes=[out.buf])

    def tr(self, out, in_, ident):
        o, i, d = out.ap, in_.ap, ident.ap
        return self.add("pe", lambda e: e.transpose(o, i, d), reads=[in_.buf, ident.buf], writes=[out.buf])

    def act(self, out, in_, func, scale=1.0, bias=0.0, accum=None, eng="act"):
        o, i = out.ap, in_.ap
        reads = [in_.buf]
        writes = [out.buf]
        b = bias
        if isinstance(bias, V):
            reads.append(bias.buf)
            b = bias.ap
        sc = scale
        if isinstance(scale, V):
            reads.append(scale.buf)
            sc = scale.ap
        acc = None
        if accum is not None:
            writes.append(accum.buf)
            acc = accum.ap
        if acc is None:
            fn = lambda e: e.activation(o, i, func, bias=b, scale=sc)
        else:
            fn = lambda e: e.activation(o, i, func, bias=b, scale=sc, accum_out=acc)
        return self.add("act", fn, reads=reads, writes=writes)

    def ts(self, out, in0, s1, s2, op0, op1=None, eng="dve", accum=None):
        o, i = out.ap, in0.ap
        reads = [in0.buf]
        a1, a2 = s1, s2
        if isinstance(s1, V):
            reads.append(s1.buf)
            a1 = s1.ap
        if isinstance(s2, V):
            reads.append(s2.buf)
            a2 = s2.ap
        writes = [out.buf]
        kw = {}
        if op1 is not None:
            kw["op1"] = op1
        if accum is not None:
            kw["accum_out"] = accum.ap
            writes.append(accum.buf)
        return self.add(eng, lambda e: e.tensor_scalar(o, i, a1, a2, op0, **kw), reads=reads, writes=writes)

    def tt(self, out, in0, in1, op, eng="dve"):
        o, a, b = out.ap, in0.ap, in1.ap
        return self.add(eng, lambda e: e.tensor_tensor(o, a, b, op), reads=[in0.buf, in1.buf], writes=[out.buf])

    def stt(self, out, in0, scalar, in1, op0, op1, eng="dve"):
        o, a, b = out.ap, in0.ap, in1.ap
        reads = [in0.buf, in1.buf]
        s = scalar
        if isinstance(scalar, V):
            reads.append(scalar.buf)
            s = scalar.ap
        return self.add(eng, lambda e: e.scalar_tensor_tensor(o, a, s, b, op0, op1), reads=reads, writes=[out.buf])

    def copy(self, out, in_, eng="dve"):
        o, i = out.ap, in_.ap
        if eng == "act":
            return self.add("act", lambda e: e.copy(o, i), reads=[in_.buf], writes=[out.buf])
        return self.add(eng, lambda e: e.tensor_copy(o, i), reads=[in_.buf], writes=[out.buf])

    def recip(self, out, in_):
        o, i = out.ap, in_.ap
        return self.add("dve", lambda e: e.reciprocal(o, i), reads=[in_.buf], writes=[out.buf])

    def memset(self, out, val, eng="pool"):
        o = out.ap
        return self.add(eng, lambda e: e.memset(o, val), writes=[out.buf])

    def dma(self, out, in_, eng="sp", chan=None, noncontig=False, extra_reads=()):
        o, i = out.ap, in_.ap
        if chan is None:
            chan = "c_" + str(id(out.buf))
        if chan not in self.chans:
            self.chans[chan] = [None, 0]
            self.chan_last[chan] = None
        if noncontig:
            fn = lambda e: e.dma_start(out=o, in_=i, allow_slow_non_contiguous=True)
        else:
            fn = lambda e: e.dma_start(out=o, in_=i)
        op = self.add(eng, fn, reads=[in_.buf] + list(extra_reads), writes=[out.buf], is_dma=True, chan=chan)
        self.chans[chan][1] += 16
        op.val = self.chans[chan][1]
        prev = self.chan_last.get(chan)
        if prev is not None and prev not in op.deps:
            prev.signal = True
            op.deps.append(prev)
        self.chan_last[chan] = op
        return op

    def emit(self, stack):
        nc = self.nc
        esem = {}
        for e in ENGS:
            esem[e] = stack.enter_context(nc.semaphore("s_" + e))
        for c in self.chans:
            self.chans[c][0] = stack.enter_context(nc.semaphore("d%d" % len(esem)))
            esem["chan_" + c] = self.chans[c][0]
        for e in ENGS:
            cnt = 0
            for op in self.ops[e]:
                if op.is_dma:
                    op.sem = self.chans[op.chan][0]
                    op.signal = True
                else:
                    op.sem = esem[e]
                    if op.signal:
                        cnt += 1
                        op.val = cnt
        print("sem max", {e: max([op.val for op in self.ops[e] if not op.is_dma] + [0]) for e in ENGS},
              "chan max", max(v[1] for v in self.chans.values()), flush=True)
        final_waits = []
        for c, (sem, val) in self.chans.items():
            final_waits.append((sem, val))
        self.final_waits = final_waits
        prog = self

        def run(engine, ename):
            seen = {}
            nwait = 0
            for op in prog.ops[ename]:
                for d in op.deps:
                    k = id(d.sem)
                    if seen.get(k, 0) >= d.val:
                        continue
                    seen[k] = d.val
                    engine.wait_ge(d.sem, d.val)
                    nwait += 1
                ins = op.fn(engine)
                if op.signal:
                    ins.then_inc(op.sem, 16 if op.is_dma else 1)
            if ename == "sp":
                for sem, val in prog.final_waits:
                    if seen.get(id(sem), 0) < val:
                        engine.wait_ge(sem, val)

        with nc.Block() as block:
            @block.tensor
            def _(e):
                run(e, "pe")

            @block.scalar
            def _(e):
                run(e, "act")

            @block.vector
            def _(e):
                run(e, "dve")

            @block.gpsimd
            def _(e):
                run(e, "pool")

            @block.sync
            def _(e):
                run(e, "sp")


BIGM = 30000.0
NSA_STOP = 99
NSA_G = (0, 1)
NSA_BR = (0, 1, 2)
NSA_NOGATE = False
RET_G = [1.0 - 2.0 ** (-5.0 - h) for h in range(4)]


def _layout(items):
    off = 0
    d = {}
    for name, w in items:
        d[name] = (off, w)
        off += w
    return d, off


CB_ITEMS = [("ident", 128), ("ones", 128), ("tcaus", 896), ("twin", 1408), ("tcmp", 2560),
            ("eb16", 2048), ("eslc", 4096), ("sel", 3072), ("caug", 130)]
CB, NCB = _layout(CB_ITEMS)
CF_ITEMS = [("ident", 128), ("ones", 128), ("pastm", 512), ("ownm2", 512), ("fbt", 128),
            ("intra", 512), ("cross", 256), ("kdec", 4), ("neghalf", 1), ("pad", 3)]
CF, NCF = _layout(CF_ITEMS)


def make_consts():
    cb = np.zeros((128, NCB), np.float32)
    cf = np.zeros((128, NCF), np.float32)

    def put(arr, lay, name, val):
        o, w = lay[name]
        assert val.shape[1] == w, (name, val.shape, w)
        arr[: val.shape[0], o:o + w] = val

    p = np.arange(128)[:, None]
    put(cb, CB, "ident", np.eye(128, dtype=np.float32))
    put(cb, CB, "ones", np.ones((128, 128), np.float32))
    j = np.arange(896)[None, :]
    put(cb, CB, "tcaus", np.where((j - 384) >= p, 0.0, NEG).astype(np.float32))
    j = np.arange(1408)[None, :]
    dd = (j - 384) - p
    put(cb, CB, "twin", np.where((dd >= 0) & (dd < 512), 0.0, NEG).astype(np.float32))
    j = np.arange(2560)[None, :]
    put(cb, CB, "tcmp", np.where(16 * p + 31 <= j, 0.0, NEG).astype(np.float32))
    e = np.zeros((16, 2048), np.float32)
    for b in range(16):
        e[b, b * 128:(b + 1) * 128] = 1.0
    put(cb, CB, "eb16", e)
    e = np.zeros((128, 4096), np.float32)
    for b in range(64):
        e[b, b * 64:(b + 1) * 64] = 1.0
        e[64 + b, b * 64:(b + 1) * 64] = 1.0
    put(cb, CB, "eslc", e)
    e = np.zeros((56, 3072), np.float32)
    for r in range(24):
        e[r, r * 128:(r + 1) * 128] = 1.0
        e[32 + r, r * 128:(r + 1) * 128] = 1.0
    put(cb, CB, "sel", e)
    n_cmp, n_slc = 255, 64
    mat = np.zeros((256, 64), np.float32)
    for jj in range(n_slc):
        for a in range(4):
            for b in range(2):
                i = 4 * jj + a - b
                if 0 <= i < n_cmp:
                    mat[i, jj] += 1.0
    ca = np.zeros((128, 130), np.float32)
    for t in range(2):
        ca[:, t * 65:t * 65 + 64] = mat[t * 128:(t + 1) * 128]
        ca[:, t * 65 + 64] = 1.0
    put(cb, CB, "caug", ca)

    put(cf, CF, "ident", np.eye(128, dtype=np.float32))
    put(cf, CF, "ones", np.ones((128, 128), np.float32))
    pm = np.zeros((32, 16), np.float32)
    om = np.zeros((32, 16), np.float32)
    for qt in range(32):
        own = qt // 2
        for b in range(16):
            pm[qt, b] = 0.0 if b < own else -1e30
            om[qt, b] = (-BIGM if b < own else (0.0 if b == own else -2 * BIGM))
    put(cf, CF, "pastm", np.broadcast_to(pm.reshape(1, 512), (128, 512)))
    put(cf, CF, "ownm2", np.broadcast_to(om.reshape(1, 512), (128, 512)))
    fb = np.zeros((128, 128), np.float32)
    for ql in range(128):
        orel = ql // 64
        for jx in range(127):
            m = jx - 63
            if m > orel:
                fb[ql, jx] = -1000.0
            elif m == orel or m == orel - 1:
                fb[ql, jx] = 1000.0
    put(cf, CF, "fbt", fb)
    it = np.zeros((128, 512), np.float32)
    jj = np.arange(128)[:, None]
    ii = np.arange(128)[None, :]
    for h in range(4):
        g = RET_G[h]
        it[:, h * 128:(h + 1) * 128] = np.where(ii >= jj, g ** np.maximum(ii - jj, 0), 0.0) * 0.125
    put(cf, CF, "intra", it)
    cr = np.zeros((128, 256), np.float32)
    for pair in range(2):
        for half in range(2):
            g = RET_G[2 * pair + half]
            cr[half * 64:(half + 1) * 64, pair * 128:(pair + 1) * 128] = (g ** (np.arange(128) + 1.0))[None, :]
    put(cf, CF, "cross", cr)
    kd = np.zeros((128, 4), np.float32)
    for h in range(4):
        kd[:, h] = RET_G[h] ** (127.0 - np.arange(128)) * 0.125
    put(cf, CF, "kdec", kd)
    put(cf, CF, "neghalf", np.full((128, 1), -0.5, np.float32))
    inv_freq = (1.0 / (10000.0 ** np.linspace(0.0, 1.0, 32))).astype(np.float32)
    ang = np.arange(S, dtype=np.float32)[:, None] * inv_freq[None, :]
    cos = np.cos(ang).astype(np.float32).T
    sin = np.sin(ang).astype(np.float32).T
    c2 = np.concatenate([cos, cos, cos, cos], 0)
    s2 = np.concatenate([sin, sin, sin, sin], 0)
    return cb, cf, np.ascontiguousarray(c2), np.ascontiguousarray(s2)


class Arena:
    def __init__(self, ap32, nwords):
        self.ap = ap32
        self.n = nwords
        self.top = 0
        self.peak = 0

    def alloc(self, shape, dt, name=""):
        free = 1
        for s in shape[1:]:
            free *= s
        words = free if dt == F32 else (free + 1) // 2
        a = self.top
        self.top += words
        self.peak = max(self.peak, self.top)
        assert self.top <= self.n, ("arena overflow", name, self.top, self.n)
        ap = self.ap[:, a:a + words]
        if dt != F32:
            ap = ap.bitcast(dt)[:, 0:free]
        if len(shape) == 3:
            ap = ap.rearrange("p (a b) -> p a b", a=shape[1])
        ap = ap[0:shape[0]]
        return V(ap, Buf(name))


def build(nl=NL, dbg=False, phases=("mem", "moba", "nsa", "ret")):
    from contextlib import ExitStack
    nc = bass.Bass("TRN2", target_bir_lowering=False)

    def din(name, shape, dt=F32):
        return V(nc.dram_tensor(name, shape, dt, kind="ExternalInput").ap(), Buf(name))

    x_d = din("x", [S, D])
    mem_d = din("mem", [256, D])
    pre_g_d = din("pre_norm_g", [NL, D])
    post_g_d = din("post_norm_g", [NL, D])
    mem_g_d = din("mem_norm_g", [NL, D])
    w_in_d = din("w_in", [NL, D, INC])
    w_mem_d = din("w_mem_kv", [NL, D, 1024])
    pe_k_d = din("nsa_pe_k", [NL, 32, 64])
    w1_k_d = din("nsa_w1_k", [NL, 2048, 128])
    w2_k_d = din("nsa_w2_k", [NL, 128, 64])
    pe_v_d = din("nsa_pe_v", [NL, 32, 64])
    w1_v_d = din("nsa_w1_v", [NL, 2048, 128])
    w2_v_d = din("nsa_w2_v", [NL, 128, 64])
    retg_d = din("ret_gn_g", [NL, 512])
    w_out_d = din("w_out", [NL, 2048, D])
    cb_d = din("cb", [128, NCB])
    cf_d = din("cf", [128, NCF])
    rc_d = din("rc", [128, S])
    rs_d = din("rs", [128, S])
    out_ap = nc.dram_tensor("out", [S, D], F32, kind="ExternalOutput").ap()
    x1_ap = nc.dram_tensor("x1s", [S, D], F32, kind="Internal").ap()
    oT_kind = "ExternalOutput" if dbg else "Internal"
    oT_ap = nc.dram_tensor("oTs", [2048, S], BF16, kind=oT_kind).ap()
    oT_bufs = [Buf("oT%d" % i) for i in range(16)]
    x1_bufs = [Buf("x1_%d" % i) for i in range(32)]
    out_bufs = [Buf("out_%d" % i) for i in range(32)]

    P = Prog(nc)
    st = ExitStack()
    with st:
        NW = 52000
        arena_t = st.enter_context(nc.sbuf_tensor("arena", [128, NW], F32))
        A = Arena(arena_t[:, :], NW)
        banks = []
        for i in range(8):
            pt = st.enter_context(nc.psum_tensor("psb%d" % i, [128, 512], F32))
            banks.append(V(pt[:, :], Buf("ps%d" % i)))
        grp = {"S": [0, 1, 2, 3], "O": [4, 5], "X": [6, 7]}
        gctr = {"S": 0, "O": 0, "X": 0}

        def ps(g):
            b = banks[grp[g][gctr[g] % len(grp[g])]]
            gctr[g] += 1
            return b

        cbf = A.alloc([128, NCB], BF16, "cbf")
        cff = A.alloc([128, NCF], F32, "cff")
        hT = A.alloc([128, 8, S], BF16, "hT")
        gpre = A.alloc([128, 8 * NL], F32, "gpre")
        gmem = A.alloc([128, 8 * NL], F32, "gmem")
        retg = A.alloc([128, 4 * NL], F32, "retg")

        def cb(name, rows=128):
            o, w = CB[name]
            return cbf[0:rows, o:o + w]

        def cf(name, rows=128):
            o, w = CF[name]
            return cff[0:rows, o:o + w]

        for k in range(0, NCB, 2048):
            e = min(NCB, k + 2048)
            P.dma(cbf[:, k:e], cb_d[:, k:e], eng="pool", chan="const")
        P.dma(cff, cf_d, eng="sp", chan="const2")
        for l in range(NL):
            P.dma(gpre[:, l * 8:(l + 1) * 8], V(pre_g_d.ap[l].rearrange("(c p) -> p c", p=128), pre_g_d.buf), eng="sp", chan="const2", noncontig=True)
            P.dma(gmem[:, l * 8:(l + 1) * 8], V(mem_g_d.ap[l].rearrange("(c p) -> p c", p=128), mem_g_d.buf), eng="sp", chan="const2", noncontig=True)
            P.dma(retg[:, l * 4:(l + 1) * 4], V(retg_d.ap[l].rearrange("(h v) -> v h", v=128), retg_d.buf), eng="sp", chan="const2", noncontig=True)
        ident_bf = cb("ident")
        ones_bf = cb("ones")
        ident_f = cf("ident")
        ones_f = cf("ones")
        mark0 = A.top

        ev_ctr = [0]

        def evac(dst, src, scale=None):
            ev_ctr[0] += 1
            if ev_ctr[0] % 2 == 0:
                P.act(dst, src, AF.Copy, scale=(1.0 if scale is None else scale))
            else:
                if scale is None:
                    P.copy(dst, src, eng="dve")
                else:
                    P.ts(dst, src, scale, None, ALU.mult)

        sh = {}

        def new_phase(n_oc=2, n_w=4):
            P.barrier()
            A.top = mark0
            sh["w"] = [A.alloc([128, 8, 128], BF16, "w%d" % i) for i in range(n_w)]
            sh["wc"] = 0
            sh["p"] = [A.alloc([128, 512], BF16, "pb%d" % i) for i in range(6)]
            sh["pc"] = 0
            sh["rr"] = [A.alloc([128, 512], F32, "rr%d" % i) for i in range(2)]
            sh["bs"] = [A.alloc([128, 512], F32, "bs%d" % i) for i in range(2)]
            sh["fc"] = 0
            sh["sz"] = [A.alloc([128, 512], BF16, "sz%d" % i) for i in range(2)]
            sh["szc"] = 0
            sh["oc"] = [A.alloc([128, S], BF16, "oc%d" % i) for i in range(n_oc)]

        def load_w(l, col0, n, src=None, dstoff=0, slot=None, neg=False):
            if slot is None:
                k = sh["wc"] % len(sh["w"])
                slot = (sh["w"][k], k)
                sh["wc"] += 1
            sl, k = slot
            srcv = (w_in_d if src is None else src)
            s3 = V(srcv.ap[l].rearrange("(c p) n -> p c n", p=128)[:, :, col0:col0 + n], srcv.buf)
            P.dma(sl[:, :, dstoff:dstoff + n], s3, eng="pool", chan="w%d" % k)
            if neg:
                P.ts(sl[:, :, dstoff:dstoff + n], sl[:, :, dstoff:dstoff + n], -1.0, None, ALU.mult, eng="pool")
            return sl[:, :, 0:dstoff + n], slot

        def lw(l, col0, n, src=None):
            return load_w(l, col0, n, src=src)[0]

        def proj_fm(dst, w, prow, M, sink=None):
            for tc in range(8):
                pb = ps("X")
                for c in range(8):
                    P.mm(pb[prow:prow + M, :], w[:, c, :], hT[:, c, tc * 512:(tc + 1) * 512], start=(c == 0), stop=(c == 7))
                if sink is None:
                    evac(dst[prow:prow + M, tc * 512:(tc + 1) * 512], pb[prow:prow + M, :])
                else:
                    sink(tc, pb)

        def proj_tm(w, n, sink, src=None, ntile=32):
            srcT = hT if src is None else src
            for tt in range(ntile):
                pb = ps("X")
                for c in range(8):
                    P.mm(pb[:, 0:n], srcT[:, c, tt * 128:(tt + 1) * 128], w[:, c, :], start=(c == 0), stop=(c == 7))
                sink(tt, pb)

        def attn_chunk(qc, qT, kT, pl, vfn, M, scale, extra=None):
            n = len(pl)
            O = ps("O")
            O2 = ps("O") if extra is not None else None
            pbs = {}

            def do_s(i):
                kt, masks = pl[i]
                Sp = ps("S")
                P.mm(Sp, kT[:, kt * 128:(kt + 1) * 128], qT[:, qc * 512:(qc + 1) * 512], start=True, stop=(len(masks) == 0))
                for mi, (ml, mr) in enumerate(masks):
                    P.mm(Sp, ml, mr, start=False, stop=(mi == len(masks) - 1))
                Pb = sh["p"][sh["pc"] % 6]
                sh["pc"] += 1
                P.act(Pb, Sp, AF.Exp, scale=scale)
                return Pb

            LOOK = 3
            for i in range(min(LOOK, n)):
                pbs[i] = do_s(i)
            for i in range(n):
                if i + LOOK < n:
                    pbs[i + LOOK] = do_s(i + LOOK)
                pb = pbs.pop(i)
                P.mm(O[0:M, :], vfn(pl[i][0]), pb, start=(i == 0), stop=(i == n - 1))
                if extra is not None:
                    em, efn = extra
                    P.mm(O2[0:em, :], efn(pl[i][0]), pb, start=(i == 0), stop=(i == n - 1))
            return O, O2

        def bcast_rden(O, dr, gate_ps=None, guard=False):
            k = sh["fc"]
            sh["fc"] += 1
            r1 = sh["rr"][k % 2]
            P.copy(r1[dr:dr + 1, :], O[dr:dr + 1, :], eng="act")
            B = ps("X")
            P.mm(B, ones_f[dr:dr + 1, :], r1[dr:dr + 1, :])
            Bs = sh["bs"][k % 2]
            if guard:
                P.ts(Bs, B, 1e-30, None, ALU.max)
                P.recip(Bs, Bs)
            else:
                P.recip(Bs, B)
            if gate_ps is not None:
                P.tt(Bs, Bs, gate_ps, ALU.mult)
            return Bs

        def gate_store(l, ci, ocT, slot_id):
            w = lw(l, C_Z + ci * 128, 128)

            def sink(tc, pb):
                sz = sh["sz"][sh["szc"] % 2]
                sh["szc"] += 1
                P.act(sz, pb, AF.Silu)
                sl = ocT[:, tc * 512:(tc + 1) * 512]
                P.tt(sl, sl, sz, ALU.mult, eng="pool")
            proj_fm(None, w, 0, 128, sink=sink)
            P.dma(V(oT_ap[ci * 128:(ci + 1) * 128, :], oT_bufs[ci]), ocT, eng="sp", chan="oc%d" % slot_id)

        def phase_norm_T(src_ap, src_bufs, ntok, g_sb, dstT):
            xsl = [A.alloc([128, D], F32, "xs%d" % i) for i in range(4)]
            xnl = [A.alloc([128, D], BF16, "xn%d" % i) for i in range(3)]
            junk = A.alloc([128, D], BF16, "junk")
            stt_ = [A.alloc([128, 4], F32, "st%d" % i) for i in range(6)]
            g3 = V(g_sb.ap.rearrange("p (c o) -> p c o", o=1).to_broadcast([128, 8, 128]), g_sb.buf)
            n = ntok // 128

            def stage_a(tt):
                xs = xsl[tt % 4]
                P.dma(xs, V(src_ap[tt * 128:(tt + 1) * 128, :], src_bufs[tt]), eng="sp", chan="xs%d" % (tt % 4))
                stt = stt_[tt % 6]
                P.act(junk, xs, AF.Square, accum=stt[:, 0:1])
                P.ts(stt[:, 1:2], stt[:, 0:1], 1.0 / D, EPS, ALU.mult, ALU.add)
                P.tt(stt[:, 2:3], stt[:, 1:2], cf("neghalf"), ALU.pow, eng="pool")

            def stage_b(tt):
                P.ts(xnl[tt % 3], xsl[tt % 4], stt_[tt % 6][:, 2:3], None, ALU.mult)

            def stage_c(tt):
                xn = xnl[tt % 3]
                pt = ps("X").bc(BF16)
                for c in range(8):
                    P.tr(pt[:, c * 128:(c + 1) * 128], xn[:, c * 128:(c + 1) * 128], ident_bf)
                P.tt(dstT[:, :, tt * 128:(tt + 1) * 128], pt.re("p (c t) -> p c t", c=8), g3, ALU.mult)

            for step in range(n + 2):
                if step < n:
                    stage_a(step)
                if 1 <= step <= n:
                    stage_b(step - 1)
                if step >= 2:
                    stage_c(step - 2)

        def phase_out(l, src_ap, src_bufs, dst_ap, dst_bufs):
            wout = A.alloc([128, 16, D], BF16, "wout")
            w3 = V(w_out_d.ap[l].rearrange("(c p) n -> p c n", p=128), w_out_d.buf)
            for c0 in range(0, 16, 2):
                P.dma(wout[:, c0:c0 + 2, :], w3[:, c0:c0 + 2, :], eng="pool", chan="wout")
            gp = A.alloc([128, D], F32, "gpost")
            P.dma(gp, V(post_g_d.ap[l:l + 1, :].to_broadcast([128, D]), post_g_d.buf), eng="sp", chan="gpost")
            xsl = [A.alloc([128, D], F32, "xo%d" % i) for i in range(2)]
            otl = [A.alloc([128, 16, 128], BF16, "ot%d" % i) for i in range(2)]
            ynl = [A.alloc([128, D], F32, "yn%d" % i) for i in range(2)]
            junk = A.alloc([128, 512], BF16, "junk2")
            stt_ = [A.alloc([128, 4], F32, "sto%d" % i) for i in range(4)]
            oT3 = oT_ap.rearrange("(c p) t -> p c t", p=128)
            for tt in range(32):
                ot = otl[tt % 2]
                P.dma(ot, V(oT3[:, :, tt * 128:(tt + 1) * 128], oT_bufs[0]), eng="sp", chan="ot%d" % (tt % 2), extra_reads=oT_bufs[1:])
                xs = xsl[tt % 2]
                P.dma(xs, V(src_ap[tt * 128:(tt + 1) * 128, :], src_bufs[tt]), eng="sp", chan="xo%d" % (tt % 2))
                stt = stt_[tt % 4]
                pbs_ = []
                for half in range(2):
                    pb = banks[(tt * 2 + half) % 8]
                    pbs_.append(pb)
                    for c in range(16):
                        P.mm(pb, ot[:, c, :], wout[:, c, half * 512:(half + 1) * 512], start=(c == 0), stop=(c == 15))
                    P.act(junk, pb, AF.Square, accum=stt[:, half:half + 1])
                P.tt(stt[:, 2:3], stt[:, 0:1], stt[:, 1:2], ALU.add)
                P.ts(stt[:, 2:3], stt[:, 2:3], 1.0 / D, EPS, ALU.mult, ALU.add)
                P.tt(stt[:, 3:4], stt[:, 2:3], cf("neghalf"), ALU.pow, eng="pool")
                yn = ynl[tt % 2]
                for half in range(2):
                    hs = slice(half * 512, (half + 1) * 512)
                    P.stt(yn[:, hs], pbs_[half], stt[:, 3:4], gp[:, hs], ALU.mult, ALU.mult)
                P.tt(yn, yn, xs, ALU.add, eng="pool")
                P.dma(V(dst_ap[tt * 128:(tt + 1) * 128, :], dst_bufs[tt]), yn, eng="sp", chan="sto%d" % (tt % 2))

        def phase_mem(l):
            memT = A.alloc([128, 8, 256], BF16, "memT")
            m0 = A.top
            phase_norm_T(mem_d.ap, [mem_d.buf] * 2, 256, gmem[:, l * 8:(l + 1) * 8], memT)
            A.top = m0
            P.barrier()
            vtm = A.alloc([128, 2, 512], BF16, "memv")
            kTm = A.alloc([128, 256], BF16, "memk")
            qTm = A.alloc([128, S], BF16, "memq")
            rdn = [A.alloc([128, 512], F32, "rdn%d" % i) for i in range(2)]
            for j in range(4):
                w = lw(l, 512 + j * 128, 128, src=w_mem_d)

                def sinkv(tt, pb, j=j):
                    evac(vtm[:, tt, j * 128:(j + 1) * 128], pb[:, 0:128])
                proj_tm(w, 128, sinkv, src=memT, ntile=2)
            sc = 128.0 ** -0.5
            for h in range(4):
                wk = lw(l, h * 128, 128, src=w_mem_d)
                pb = ps("X")
                for c in range(8):
                    P.mm(pb[:, 0:256], wk[:, c, :], memT[:, c, :], start=(c == 0), stop=(c == 7))
                evac(kTm, pb[:, 0:256])
                wq = lw(l, C_CQ + h * 128, 128)
                proj_fm(qTm, wq, 0, 128)
                oc = sh["oc"][h % 2]
                for qc in range(8):
                    O, O2 = attn_chunk(qc, qTm, kTm, [(0, []), (1, [])], lambda kt, h=h: vtm[:, kt, h * 128:(h + 1) * 128], 128, sc,
                                       extra=(128, lambda kt: ones_bf))
                    r = rdn[qc % 2]
                    P.recip(r, O2)
                    P.tt(oc[:, qc * 512:(qc + 1) * 512], O, r, ALU.mult)
                gate_store(l, 12 + h, oc, h % 2)

        def phase_moba(l):
            qh = A.alloc([128, S], BF16, "mqh")
            kh = A.alloc([128, S], BF16, "mkh")
            vaug = A.alloc([128, 32, 193], BF16, "mvaug")
            kmf = A.alloc([128, 16], F32, "kmf")
            kmb = A.alloc([128, 16], BF16, "kmb")
            gsb = A.alloc([128, 512], F32, "gsb")
            m8 = A.alloc([128, 256], F32, "m8")
            a1 = A.alloc([128, 512], F32, "a1")
            mv = A.alloc([128, 512], BF16, "mv")
            mv64 = A.alloc([128, 2048], BF16, "mv64")
            mvp = A.alloc([128, 8 * 128], BF16, "mvp")
            P.memset(vaug[:, :, 64:66], 1.0)
            P.memset(vaug[:, :, 66:129], 0.0)
            P.memset(mvp, 0.0)
            otc = CB["tcaus"][0]
            oes = CB["eslc"][0]
            for pair in range(4):
                wv = lw(l, C_MV + pair * 128, 128)

                def sinkv(tt, pb):
                    evac(vaug[:, tt, 0:64], pb[:, 0:64])
                    evac(vaug[:, tt, 129:193], pb[:, 64:128])
                proj_tm(wv, 128, sinkv)
                oc = sh["oc"][pair % 2]
                for hh in range(2):
                    h = 2 * pair + hh
                    r0 = 64 * hh
                    rows = slice(r0, r0 + 64)
                    orows = slice(64 - r0, 128 - r0)
                    proj_fm(qh, lw(l, C_MQ + h * 64, 64), r0, 64)
                    proj_fm(kh, lw(l, C_MK + h * 64, 64), r0, 64)
                    P.copy(kh[orows, :], cbf[orows, oes:oes + S], eng="pool")
                    o_, i_ = kmf.ap[rows], kh.ap[rows].rearrange("p (b k) -> p b k", k=256)
                    P.add("dve", lambda e, o_=o_, i_=i_: e.tensor_reduce(o_, i_, AX.X, ALU.add), reads=[kh.buf], writes=[kmf.buf])
                    P.ts(kmb[rows], kmf[rows], 1.0 / 256, None, ALU.mult)
                    G = ps("X")
                    for qt in range(32):
                        P.mm(G[:, qt * 16:(qt + 1) * 16], qh[rows, qt * 128:(qt + 1) * 128], kmb[rows, :], start=True, stop=True)
                    P.tt(gsb, G, cf("pastm"), ALU.add)
                    for qt in range(32):
                        o_, i_ = m8.ap[:, qt * 8:(qt + 1) * 8], gsb.ap[:, qt * 16:(qt + 1) * 16]
                        P.add("dve", lambda e, o_=o_, i_=i_: e.max(o_, i_), reads=[gsb.buf], writes=[m8.buf])
                    thr = V(m8.ap.rearrange("p (q e) -> p q e", e=8)[:, :, 2:3].to_broadcast([128, 32, 16]), m8.buf)
                    P.tt(a1.re("p (q b) -> p q b", b=16), gsb.re("p (q b) -> p q b", b=16), thr, ALU.is_ge)
                    P.stt(a1, a1, BIGM, cf("ownm2"), ALU.mult, ALU.add)
                    P.ts(mv, a1, 0.0, None, ALU.min)
                    P.copy(mv64.re("p (c r) -> p c r", r=4),
                           V(mv.ap.rearrange("p (c o) -> p c o", o=1).to_broadcast([128, 512, 4]), mv.buf), eng="dve")
                    c0 = 64 - r0
                    for g4 in range(4):
                        P.copy(mvp.re("p (j c) -> p j c", c=128)[:, :, c0:c0 + 64],
                               mv64[:, g4 * 512:(g4 + 1) * 512].re("p (j c) -> p j c", c=64), eng="pool")
                        pt = ps("X").bc(BF16)
                        for j in range(8):
                            P.tr(pt[:, j * 128:(j + 1) * 128], mvp[:, j * 128:(j + 1) * 128], ident_bf)
                        evac(qh[orows, g4 * 1024:(g4 + 1) * 1024], pt[orows, :])
                    if hh == 0:
                        vfn = lambda kt: vaug[:, kt, 0:65]
                        M, dr = 65, 64
                    else:
                        vfn = lambda kt: vaug[:, kt, 65:193]
                        M, dr = 128, 0
                    for qc in range(8):
                        pl = []
                        for kt in range(4 * qc + 4):
                            masks = []
                            if kt >= 4 * qc:
                                delta = 128 * kt - 512 * qc
                                masks.append((ident_bf, cbf[:, otc + 384 - delta:otc + 384 - delta + 512]))
                            pl.append((kt, masks))
                        O, _ = attn_chunk(qc, qh, kh, pl, vfn, M, 0.125)
                        Bs = bcast_rden(O, dr)
                        P.tt(oc[rows, qc * 512:(qc + 1) * 512], O[rows, :], Bs[rows, :], ALU.mult)
                gate_store(l, pair, oc, pair % 2)

        def phase_nsa(l):
            oc = sh["oc"][0]
            kcTs = [A.alloc([128, 256], BF16, "kcT%d" % i) for i in range(2)]
            for t_ in kcTs:
                P.memset(t_, 0.0)
            vcaug = A.alloc([128, 4, 129], BF16, "vcaug")
            Gt = A.alloc([128, S], BF16, "Gt")
            qT = A.alloc([128, S], BF16, "nqT")
            P.memset(qT, 0.0)
            acc = [A.alloc([128, 512], F32, "acc%d" % i) for i in range(2)]
            tmpb = [A.alloc([128, 512], F32, "tmpb%d" % i) for i in range(2)]
            m1 = A.top
            kin = A.alloc([128, S], BF16, "kin")
            vin = A.alloc([128, S], BF16, "vin")
            proj_fm(kin, lw(l, C_NKC, 128), 0, 128)
            proj_fm(vin, lw(l, C_NVC, 128), 0, 128)
            P.memset(vcaug[:, :, 0:1], 1.0)
            P.memset(vcaug[:, :, 1:64], 0.0)
            P.memset(vcaug[:, :, 128:129], 1.0)
            w1 = A.alloc([128, 32, 128], BF16, "w1")
            peT = A.alloc([128, 32], BF16, "peT")
            w2 = A.alloc([128, 64], BF16, "w2")
            bias = A.alloc([128, 2], F32, "cbias")
            hid = A.alloc([128, 256], BF16, "hid")
            for which, pe_d, w1_d, w2_d, xin in (("k", pe_k_d, w1_k_d, w2_k_d, kin), ("v", pe_v_d, w1_v_d, w2_v_d, vin)):
                s3 = V(w1_d.ap[l].rearrange("(l d) j -> d l j", d=64), w1_d.buf)
                for hlf in range(2):
                    for l0 in range(0, 32, 8):
                        P.dma(w1[hlf * 64:(hlf + 1) * 64, l0:l0 + 8, :], s3[:, l0:l0 + 8, :], eng="pool", chan="w1")
                    P.dma(peT[hlf * 64:(hlf + 1) * 64, :], V(pe_d.ap[l].rearrange("l d -> d l"), pe_d.buf), eng="pool", chan="w1", noncontig=True)
                P.dma(w2, V(w2_d.ap[l], w2_d.buf), eng="pool", chan="w1")
                for g in range(2):
                    rows = slice(64 * g, 64 * g + 64)
                    pbias = ps("X")
                    for l_ in range(32):
                        P.mm(pbias[:, 0:1], w1[rows, l_, :], peT[rows, l_:l_ + 1], start=(l_ == 0), stop=(l_ == 31))
                    P.copy(bias[:, g:g + 1], pbias[:, 0:1], eng="dve")
                    ph = ps("X")
                    x3 = V(xin.ap[rows].rearrange("p (i s) -> p i s", s=16), xin.buf)
                    for l_ in range(32):
                        P.mm(ph[:, 0:255], w1[rows, l_, :], x3[:, l_ // 16:l_ // 16 + 255, l_ % 16], start=(l_ == 0), stop=(l_ == 31))
                    P.memset(hid[:, 255:256], 0.0, eng="dve")
                    P.act(hid[:, 0:255], ph[:, 0:255], AF.Silu, bias=bias[:, g:g + 1])
                    if which == "k":
                        pk = ps("X")
                        P.mm(pk[rows, 0:256], w2, hid)
                        evac(kcTs[g][rows, :], pk[rows, 0:256])
                    else:
                        for t in range(2):
                            pv = ps("X")
                            P.mm(pv[:, 0:64], hid[:, t * 128:(t + 1) * 128], w2)
                            evac(vcaug[:, g * 2 + t, 64:128], pv[:, 0:64])
            if NSA_STOP <= 1:
                return
            wg = lw(l, C_NG, 24)
            P.memset(Gt, 0.0)
            sg = A.alloc([56, 512], F32, "sg")
            hi2 = A.alloc([56, 512], BF16, "hi2")
            for tc in range(8):
                pb = ps("X")
                for r0 in (0, 32):
                    for c in range(8):
                        P.mm(pb[r0:r0 + 24, :], wg[:, c, :], hT[:, c, tc * 512:(tc + 1) * 512], start=(c == 0), stop=(c == 7))
                    P.act(sg[r0:r0 + 24, :], pb[r0:r0 + 24, :], AF.Sigmoid)
                P.copy(Gt[0:24, tc * 512:(tc + 1) * 512], sg[0:24, :], eng="dve")
                P.copy(hi2[32:56, :], sg[32:56, :], eng="dve")
                P.tt(Gt[32:56, tc * 512:(tc + 1) * 512], sg[32:56, :], hi2[32:56, :], ALU.subtract)
            if NSA_STOP <= 2:
                return
            A.top = m1
            P.barrier()
            ksT = A.alloc([128, S], BF16, "ksT")
            kwT = A.alloc([128, S], BF16, "kwT")
            m2 = A.top
            ocaug = CB["caug"][0]
            otcmp = CB["tcmp"][0]
            otc = CB["tcaus"][0]
            otw = CB["twin"][0]
            oes = CB["eslc"][0]
            osel = CB["sel"][0]

            def pairs_cmp(qc):
                pl = []
                if qc >= 5:
                    pl.append((0, []))
                else:
                    d0 = 512 * qc
                    pl.append((0, [(ident_bf, cbf[:, otcmp + d0:otcmp + d0 + 512])]))
                if qc >= 4:
                    d1 = 512 * qc - 2048
                    pl.append((1, [(ident_bf, cbf[:, otcmp + d1:otcmp + d1 + 512])]))
                return pl

            for g in NSA_G:
                rows = slice(64 * g, 64 * g + 64)
                orows_g = slice(64 - 64 * g, 128 - 64 * g)
                A.top = m2
                P.barrier()
                impT = A.alloc([64, S], F32, "impT")
                imp2 = A.alloc([128, 64], F32, "imp2")
                tmp2 = A.alloc([128, 64], F32, "tmp2")
                m8a = A.alloc([128, 8], F32, "m8a")
                m8b = A.alloc([128, 8], F32, "m8b")
                mvb = A.alloc([128, 8 * 128], BF16, "mvb")
                P.memset(mvb, 0.0)
                for p in range(4):
                    h = 4 * g + p
                    proj_fm(qT, lw(l, C_NQ + h * 64, 64), 64 * g, 64)
                    for qc in range(8):
                        O, _ = attn_chunk(qc, qT, kcTs[g], pairs_cmp(qc),
                                          lambda kt: cbf[:, ocaug + kt * 65:ocaug + kt * 65 + 65], 65, 0.125)
                        Bs = bcast_rden(O, 64, guard=True)
                        sl = impT[0:64, qc * 512:(qc + 1) * 512]
                        if p == 0:
                            P.tt(sl, O[0:64, :], Bs[0:64, :], ALU.mult)
                        else:
                            tb = tmpb[qc % 2]
                            P.tt(tb[0:64, :], O[0:64, :], Bs[0:64, :], ALU.mult)
                            P.tt(sl, sl, tb[0:64, :], ALU.add, eng="pool")
                if NSA_STOP <= 3:
                    return
                ofb = CF["fbt"][0]
                for g8 in range(4):
                    for j in range(8):
                        qt = g8 * 8 + j
                        pt = ps("X")
                        P.tr(pt[:, 0:64], impT[0:64, qt * 128:(qt + 1) * 128], ident_f[0:64, 0:64])
                        P.tt(imp2, pt[:, 0:64], cff[:, ofb + 63 - 2 * qt:ofb + 127 - 2 * qt], ALU.add)
                        P.memset(imp2[:, 0:1], 1000.0, eng="dve")
                        a_, b_, c_, d_ = m8a.ap, imp2.ap, tmp2.ap, m8b.ap
                        P.add("dve", lambda e, a_=a_, b_=b_: e.max(a_, b_), reads=[imp2.buf], writes=[m8a.buf])
                        P.add("dve", lambda e, a_=a_, b_=b_, c_=c_: e.match_replace(c_, a_, b_, -1e30), reads=[imp2.buf, m8a.buf], writes=[tmp2.buf])
                        P.add("dve", lambda e, c_=c_, d_=d_: e.max(d_, c_), reads=[tmp2.buf], writes=[m8b.buf])
                        P.ts(mvb[:, j * 128 + 64 - 64 * g:j * 128 + 128 - 64 * g], imp2, m8b[:, 7:8], -BIGM, ALU.is_lt, ALU.mult)
                    pt = ps("X").bc(BF16)
                    for j in range(8):
                        P.tr(pt[:, j * 128:(j + 1) * 128], mvb[:, j * 128:(j + 1) * 128], ident_bf)
                    evac(qT[orows_g, g8 * 1024:(g8 + 1) * 1024], pt[orows_g, :])
                if NSA_STOP <= 4:
                    return
                P.barrier()
                A.top = m2
                vsa = A.alloc([128, 32, 129], BF16, "vsa")
                vwa = A.alloc([128, 32, 129], BF16, "vwa")
                for t_ in (vsa, vwa):
                    P.memset(t_[:, :, 0:1], 1.0)
                    P.memset(t_[:, :, 1:64], 0.0)
                    P.memset(t_[:, :, 128:129], 1.0)
                proj_fm(ksT, lw(l, C_NKS + 64 * g, 64), 64 * g, 64)
                P.copy(ksT[orows_g, :], cbf[orows_g, oes:oes + S], eng="pool")
                proj_fm(kwT, lw(l, C_NKW + 64 * g, 64), 64 * g, 64)
                P.memset(kwT[orows_g, :], 0.0)
                for cbase, dstv in ((C_NVS, vsa), (C_NVW, vwa)):
                    wv = lw(l, cbase + 64 * g, 64)

                    def sinkv(tt, pb, dstv=dstv):
                        evac(dstv[:, tt, 64:128], pb[:, 0:64])
                    proj_tm(wv, 64, sinkv)
                if NSA_STOP <= 5:
                    return
                for p in range(4):
                    if NSA_STOP <= 9 and p >= NSA_STOP - 5:
                        return
                    h = 4 * g + p
                    par = p % 2
                    orow = slice(64 * par, 64 * par + 64)
                    proj_fm(qT, lw(l, C_NQ + h * 64, 64), 64 * g, 64)
                    if par == 0:
                        c0, c1, M, dr = 64, 129, 65, 64
                    else:
                        c0, c1, M, dr = 0, 128, 128, 0
                    for qc in range(8):
                        ac = acc[qc % 2]
                        for j in range(3):
                            if j not in NSA_BR:
                                continue
                            if j == 0:
                                pl = pairs_cmp(qc)
                                kTj = kcTs[g]
                                vfn = lambda kt: vcaug[:, g * 2 + kt, c0:c1]
                            elif j == 1:
                                pl = []
                                for kt in range(4 * qc + 4):
                                    masks = []
                                    if kt >= 4 * qc:
                                        delta = 128 * kt - 512 * qc
                                        masks.append((ident_bf, cbf[:, otc + 384 - delta:otc + 384 - delta + 512]))
                                    pl.append((kt, masks))
                                kTj = ksT
                                vfn = lambda kt: vsa[:, kt, c0:c1]
                            else:
                                pl = []
                                for kt in range(max(0, 4 * qc - 4), 4 * qc + 4):
                                    delta = 128 * kt - 512 * qc
                                    pl.append((kt, [(ident_bf, cbf[:, otw + 384 - delta:otw + 384 - delta + 512])]))
                                kTj = kwT
                                vfn = lambda kt: vwa[:, kt, c0:c1]
                            O, _ = attn_chunk(qc, qT, kTj, pl, vfn, M, 0.125)
                            idx = h * 3 + j
                            Gb = None
                            if not NSA_NOGATE:
                                Gb = ps("X")
                                P.mm(Gb, cbf[:, osel + idx * 128:osel + (idx + 1) * 128], Gt[:, qc * 512:(qc + 1) * 512])
                            Bs = bcast_rden(O, dr, gate_ps=Gb, guard=(j == 0))
                            if j == 0:
                                P.tt(ac[orow, :], O[orow, :], Bs[orow, :], ALU.mult)
                            else:
                                tb = tmpb[j % 2]
                                P.tt(tb[orow, :], O[orow, :], Bs[orow, :], ALU.mult)
                                if j == 1:
                                    P.tt(ac[orow, :], ac[orow, :], tb[orow, :], ALU.add, eng="pool")
                                else:
                                    P.tt(oc[orow, qc * 512:(qc + 1) * 512], ac[orow, :], tb[orow, :], ALU.add, eng="pool")
                    if par == 1:
                        gate_store(l, 4 + h // 2, oc, 0)
                if NSA_STOP <= 10:
                    return

        def phase_ret(l):
            qfT = A.alloc([128, S], BF16, "qfT")
            qcT = A.alloc([128, S], BF16, "qcT")
            kfT = A.alloc([128, S], BF16, "kfT")
            kdtm = A.alloc([128, 32, 128], BF16, "kdtm")
            vtm = A.alloc([128, 32, 128], BF16, "rvtm")
            Rb = [A.alloc([128, 128], BF16, "Rb%d" % i) for i in range(8)]
            Rf = A.alloc([128, 128], F32, "Rf")
            cs = [A.alloc([128, 512], F32, "cs%d" % i) for i in range(2)]
            t12 = [A.alloc([128, 512], F32, "t12%d" % i) for i in range(2)]
            sm = [A.alloc([128, 512], BF16, "sm%d" % i) for i in range(2)]
            on = [A.alloc([128, 128], BF16, "on%d" % i) for i in range(2)]
            bst = A.alloc([128, 4 * 6], F32, "bst")
            bag = A.alloc([128, 4 * 2], F32, "bag")
            rsd = A.alloc([128, 8], F32, "rsd")
            cross3 = lambda pair: V(cff.ap[:, CF["cross"][0] + pair * 128:CF["cross"][0] + (pair + 1) * 128]
                                    .rearrange("p (o i) -> p o i", o=1).to_broadcast([128, 4, 128]), cff.buf)
            for pair in range(2):
                for which, cbase, dst in (("q", C_RQ, qfT), ("k", C_RK, kfT)):
                    wx, slx = load_w(l, cbase + pair * 128, 128)
                    wy, sly = load_w(l, cbase + pair * 128 + 32, 32, dstoff=0, neg=True)
                    load_w(l, cbase + pair * 128, 32, dstoff=32, slot=sly)
                    load_w(l, cbase + pair * 128 + 96, 32, dstoff=64, slot=sly, neg=True)
                    wy, _ = load_w(l, cbase + pair * 128 + 64, 32, dstoff=96, slot=sly)
                    for tc in range(8):
                        px = ps("X")
                        py = ps("X")
                        for c in range(8):
                            P.mm(px, wx[:, c, :], hT[:, c, tc * 512:(tc + 1) * 512], start=(c == 0), stop=(c == 7))
                        for c in range(8):
                            P.mm(py, wy[:, c, :], hT[:, c, tc * 512:(tc + 1) * 512], start=(c == 0), stop=(c == 7))
                        P.dma(cs[0], rc_d[:, tc * 512:(tc + 1) * 512], eng="sp", chan="cs0")
                        P.dma(cs[1], rs_d[:, tc * 512:(tc + 1) * 512], eng="sp", chan="cs1")
                        P.tt(t12[0], px, cs[0], ALU.mult)
                        P.tt(t12[1], py, cs[1], ALU.mult)
                        sl = dst[:, tc * 512:(tc + 1) * 512]
                        P.tt(sl, t12[0], t12[1], ALU.add, eng="pool")
                        if which == "q":
                            P.tt(qcT[:, tc * 512:(tc + 1) * 512].re("p (n i) -> p n i", i=128), sl.re("p (n i) -> p n i", i=128),
                                 cross3(pair), ALU.mult, eng="pool")
                for hh in range(2):
                    h = 2 * pair + hh
                    rows = slice(64 * hh, 64 * hh + 64)
                    dec = RET_G[h] ** 128.0
                    wv = lw(l, C_RV + h * 128, 128)

                    def sinkv(tt, pb):
                        evac(vtm[:, tt, :], pb[:, 0:128])
                    proj_tm(wv, 128, sinkv)
                    okd = CF["kdec"][0]
                    for n8 in range(4):
                        pt = ps("X").bc(BF16)
                        for j in range(8):
                            n = n8 * 8 + j
                            P.tr(pt[:, j * 64:(j + 1) * 64], kfT[rows, n * 128:(n + 1) * 128], cbf[rows, CB["ident"][0] + 64 * hh:CB["ident"][0] + 64 * hh + 64])
                        P.ts(kdtm[:, n8 * 8:(n8 + 1) * 8, 64 * hh:64 * hh + 64], pt[:, 0:512].re("p (n d) -> p n d", d=64),
                             cff[:, okd + h:okd + h + 1], None, ALU.mult)
                    oc = sh["oc"][h % 2]
                    P.memset(Rf[rows, :], 0.0, eng="dve")
                    P.memset(Rb[0][rows, :], 0.0, eng="dve")
                    oin = CF["intra"][0]
                    intra3 = V(cff.ap[:, oin + h * 128:oin + (h + 1) * 128].rearrange("p (o i) -> p o i", o=1).to_broadcast([128, 4, 128]), cff.buf)
                    for n4 in range(8):
                        pS = ps("S")
                        for k in range(4):
                            n = n4 * 4 + k
                            P.mm(pS[:, k * 128:(k + 1) * 128], kfT[rows, n * 128:(n + 1) * 128], qfT[rows, n * 128:(n + 1) * 128])
                        smt = sm[n4 % 2]
                        P.tt(smt.re("p (k i) -> p k i", i=128), pS.re("p (k i) -> p k i", i=128), intra3, ALU.mult)
                        pU = ps("X")
                        for k in range(4):
                            n = n4 * 4 + k
                            P.mm(pU[rows, k * 128:(k + 1) * 128], kdtm[:, n, 64 * hh:64 * hh + 64], vtm[:, n, :])
                        pO = ps("O")
                        for k in range(4):
                            n = n4 * 4 + k
                            Rcur = Rb[n % 8]
                            P.mm(pO[:, k * 128:(k + 1) * 128], smt[:, k * 128:(k + 1) * 128], vtm[:, n, :], start=True, stop=(n == 0))
                            if n > 0:
                                P.mm(pO[:, k * 128:(k + 1) * 128], qcT[rows, n * 128:(n + 1) * 128], Rcur[rows, :], start=False, stop=True)
                            if n < 31:
                                P.stt(Rf[rows, :], Rf[rows, :], dec, pU[rows, k * 128:(k + 1) * 128], ALU.mult, ALU.add)
                                P.copy(Rb[(n + 1) % 8][rows, :], Rf[rows, :], eng="act")
                        for k in range(4):
                            o_, i_ = bst.ap[:, k * 6:(k + 1) * 6], pO.ap[:, k * 128:(k + 1) * 128]
                            P.add("dve", lambda e, o_=o_, i_=i_: e.bn_stats(o_, i_), reads=[pO.buf], writes=[bst.buf])
                            o2_, i2_ = bag.ap[:, k * 2:(k + 1) * 2], bst.ap[:, k * 6:(k + 1) * 6]
                            P.add("dve", lambda e, o2_=o2_, i2_=i2_: e.bn_aggr(o2_, i2_), reads=[bst.buf], writes=[bag.buf])
                        bag3 = bag.re("p (k t) -> p k t", t=2)
                        P.ts(rsd[:, 0:4], bag3[:, :, 1], EPS, None, ALU.add)
                        P.tt(rsd[:, 4:8], rsd[:, 0:4], V(cff.ap[:, CF["neghalf"][0]:CF["neghalf"][0] + 1].to_broadcast([128, 4]), cff.buf), ALU.pow, eng="pool")
                        pt = ps("X").bc(BF16)
                        for k in range(4):
                            n = n4 * 4 + k
                            ont = on[k % 2]
                            P.ts(ont, pO[:, k * 128:(k + 1) * 128], bag[:, 2 * k:2 * k + 1], rsd[:, 4 + k:5 + k], ALU.subtract, ALU.mult)
                            P.tr(pt[:, k * 128:(k + 1) * 128], ont, ident_bf)
                        P.ts(oc[:, n4 * 512:(n4 + 1) * 512], pt[:, 0:512], retg[:, l * 4 + h:l * 4 + h + 1], None, ALU.mult)
                    gate_store(l, 8 + h, oc, h % 2)

        fns = {"mem": phase_mem, "moba": phase_moba, "nsa": phase_nsa, "ret": phase_ret}
        for l in range(nl):
            if l == 0:
                src_ap, src_bufs = x_d.ap, [x_d.buf] * 32
            else:
                src_ap, src_bufs = x1_ap, x1_bufs
            if l == nl - 1:
                dst_ap, dst_bufs = out_ap, out_bufs
            else:
                dst_ap, dst_bufs = x1_ap, x1_bufs
            P.barrier()
            A.top = mark0
            phase_norm_T(src_ap, src_bufs, S, gpre[:, l * 8:(l + 1) * 8], hT)
            for ph in phases:
                new_phase(n_oc=(1 if ph == "nsa" else 2))
                fns[ph](l)
            P.barrier()
            A.top = mark0
            phase_out(l, src_ap, src_bufs, dst_ap, dst_bufs)
        print("arena peak words", A.peak, "of", NW, "ops", {e: len(P.ops[e]) for e in ENGS}, "chans", len(P.chans), flush=True)
        P.emit(st)
    return nc


_CACHE = {}


def kernel(**inputs):
    n = 8
    if "nc" not in _CACHE:
        _CACHE["nc"] = build()
        _CACHE["consts"] = make_consts()
    nc = _CACHE["nc"]
    cbv, cfv, rc, rs = _CACHE["consts"]
    f = lambda a: np.ascontiguousarray(np.asarray(a, dtype=np.float32))
    shared = {k: f(inputs[k]) for k in ("pre_norm_g", "post_norm_g", "mem_norm_g", "w_in", "w_mem_kv", "nsa_pe_k", "nsa_w1_k",
                                        "nsa_w2_k", "nsa_pe_v", "nsa_w1_v", "nsa_w2_v", "ret_gn_g", "w_out")}
    shared.update(cb=cbv, cf=cfv, rc=rc, rs=rs)
    x = f(inputs["x"])
    mem = f(inputs["mem"])
    in_maps = []
    for i in range(n):
        m = dict(shared)
        m["x"] = np.ascontiguousarray(x[i])
        m["mem"] = np.ascontiguousarray(mem[i])
        in_maps.append(m)
    res = run_bass_kernel_spmd(nc, in_maps, core_ids=list(range(n)))
    return np.stack([np.asarray(r["out"], dtype=np.float32) for r in res.results], axis=0)
```
